# Optimizing a Trainium2 kernel written in Bass

```python
import math
import jax, jax.numpy as jnp
from jax import lax
import numpy as np

D_MODEL = 1024
BATCH = 1
SEQ = 16384
DEPTH = 1

D_MIX = D_MODEL
LRU_WIDTH = D_MIX // 2
LRU_BLOCKS = 8
LRU_BLOCK = LRU_WIDTH // LRU_BLOCKS
CONV_LRU = 4
LRU_C = 8.0
N_DIFF_HEADS = 4
DIFF_HEAD_DIM = 64
DIFF_V_DIM = 2 * DIFF_HEAD_DIM
QK_WIDTH = N_DIFF_HEADS * 2 * DIFF_HEAD_DIM
ATTN_WIDTH = N_DIFF_HEADS * DIFF_V_DIM
D_IN = 2 * QK_WIDTH + ATTN_WIDTH + 2 * LRU_WIDTH
D_FF = 3 * D_MODEL
CONV_FFN = 3
NUM_BUCKETS = 32
MAX_EXACT = NUM_BUCKETS // 2
MAX_DISTANCE = 128
Q_BLOCK = 128
EPS = 1e-6
NEG_INF = -1e30

kernel_name = 'hybrid_rglru_diffattn_convffn_block'


def rms_norm(x, g):
    xf = x.astype(jnp.float32)
    y = xf * lax.rsqrt(jnp.mean(xf * xf, axis=-1, keepdims=True) + EPS)
    return (y * g.astype(jnp.float32)).astype(x.dtype)


def causal_dwconv(x, w, b):
    k = w.shape[0]
    s = x.shape[1]
    xp = jnp.pad(x, ((0, 0), (k - 1, 0), (0, 0)))
    y = b
    for j in range(k):
        y = y + xp[:, j:j + s] * w[j]
    return y


def block_diag_linear(x, w, b):
    xb = x.reshape(x.shape[:-1] + (LRU_BLOCKS, LRU_BLOCK))
    y = jnp.einsum('bsnc,ncd->bsnd', xb, w)
    return y.reshape(x.shape) + b


def rg_lru(x, w_a, b_a, w_x, b_x, lam):
    xf = x.astype(jnp.float32)
    r = jax.nn.sigmoid(block_diag_linear(xf, w_a.astype(jnp.float32), b_a.astype(jnp.float32)))
    i = jax.nn.sigmoid(block_diag_linear(xf, w_x.astype(jnp.float32), b_x.astype(jnp.float32)))
    log_a = -LRU_C * r * jax.nn.softplus(-lam.astype(jnp.float32))
    a = jnp.exp(log_a)
    b_in = jnp.sqrt(-jnp.expm1(2.0 * log_a)) * (i * xf)

    def combine(left, right):
        a1, b1 = left
        a2, b2 = right
        return a1 * a2, a2 * b1 + b2

    _, h = lax.associative_scan(combine, (a, b_in), axis=1)
    return h.astype(x.dtype)


def t5_bucket(rel):
    n = jnp.maximum(rel, 0)
    nf = jnp.maximum(n, 1).astype(jnp.float32)
    large = MAX_EXACT + (jnp.log(nf / MAX_EXACT) / math.log(MAX_DISTANCE / MAX_EXACT)
                         * (NUM_BUCKETS - MAX_EXACT)).astype(jnp.int32)
    large = jnp.minimum(large, NUM_BUCKETS - 1)
    return jnp.where(n < MAX_EXACT, n, large)


def diff_attention(q, k, v, rel_table, lam):
    b, s = q.shape[0], q.shape[1]
    nb = s // Q_BLOCK
    qb = q.reshape(b, nb, Q_BLOCK, N_DIFF_HEADS, 2, DIFF_HEAD_DIM).transpose(1, 0, 2, 3, 4, 5)
    kpos = jnp.arange(s, dtype=jnp.int32)
    scale = DIFF_HEAD_DIM ** -0.5

    def block(args):
        idx, qblk = args
        qpos = idx * Q_BLOCK + jnp.arange(Q_BLOCK, dtype=jnp.int32)
        rel = qpos[:, None] - kpos[None, :]
        bias = jnp.take(rel_table, t5_bucket(rel), axis=0)
        bias = jnp.transpose(bias, (2, 0, 1)).astype(jnp.float32)
        sc = jnp.einsum('bqhmd,bkhmd->bhmqk', qblk, k,
                        preferred_element_type=jnp.float32) * scale + bias[None, :, None]
        sc = jnp.where(rel >= 0, sc, NEG_INF)
        p = jax.nn.softmax(sc, axis=-1)
        wts = p[:, :, 0] - lam * p[:, :, 1]
        return jnp.einsum('bhqk,bkhd->bqhd', wts.astype(v.dtype), v)

    out = lax.map(block, (jnp.arange(nb, dtype=jnp.int32), qb))
    return out.transpose(1, 0, 2, 3, 4).reshape(b, s, N_DIFF_HEADS, DIFF_V_DIM)


def setup_inputs(seed: int = 0) -> dict:
    key = jax.random.key(seed)
    ks = jax.random.split(key, 32)
    f32 = jnp.float32
    nrm = lambda k, shape, s: jax.random.normal(k, shape, f32) * s
    gain = lambda k, shape: 1.0 + 0.02 * jax.random.normal(k, shape, f32)
    u = jax.random.uniform(ks[12], (DEPTH, LRU_WIDTH), f32, 0.9, 0.999)
    a_base = u ** (1.0 / LRU_C)
    lru_lambda = jnp.log(a_base) - jnp.log1p(-a_base)
    return {
        'x': nrm(ks[0], (BATCH, SEQ, D_MODEL), 1.0),
        'c': nrm(ks[1], (BATCH, D_MODEL), 1.0),
        'w_ada': nrm(ks[2], (DEPTH, D_MODEL, 6 * D_MODEL), 0.5 * D_MODEL ** -0.5),
        'b_ada': nrm(ks[3], (DEPTH, 6 * D_MODEL), 0.02),
        'g_norm1': gain(ks[4], (DEPTH, D_MODEL)),
        'w_in': nrm(ks[5], (DEPTH, D_MODEL, D_IN), D_MODEL ** -0.5),
        'conv_lru_w': nrm(ks[6], (DEPTH, CONV_LRU, LRU_WIDTH), 0.5),
        'conv_lru_b': nrm(ks[7], (DEPTH, LRU_WIDTH), 0.02),
        'lru_wa': nrm(ks[8], (DEPTH, LRU_BLOCKS, LRU_BLOCK, LRU_BLOCK), LRU_BLOCK ** -0.5),
        'lru_ba': nrm(ks[9], (DEPTH, LRU_WIDTH), 0.02),
        'lru_wx': nrm(ks[10], (DEPTH, LRU_BLOCKS, LRU_BLOCK, LRU_BLOCK), LRU_BLOCK ** -0.5),
        'lru_bx': nrm(ks[11], (DEPTH, LRU_WIDTH), 0.02),
        'lru_lambda': lru_lambda,
        'lam_q1': nrm(ks[13], (DEPTH, DIFF_HEAD_DIM), 0.1),
        'lam_k1': nrm(ks[14], (DEPTH, DIFF_HEAD_DIM), 0.1),
        'lam_q2': nrm(ks[15], (DEPTH, DIFF_HEAD_DIM), 0.1),
        'lam_k2': nrm(ks[16], (DEPTH, DIFF_HEAD_DIM), 0.1),
        'g_subln': gain(ks[17], (DEPTH, DIFF_V_DIM)),
        'w_out': nrm(ks[18], (DEPTH, D_MIX, D_MODEL), D_MIX ** -0.5),
        'g_norm2': gain(ks[19], (DEPTH, D_MODEL)),
        'w_up': nrm(ks[20], (DEPTH, D_MODEL, 2 * D_FF), D_MODEL ** -0.5),
        'conv_ffn_w': nrm(ks[21], (DEPTH, CONV_FFN, D_FF), 0.5),
        'conv_ffn_b': nrm(ks[22], (DEPTH, D_FF), 0.02),
        'w_down': nrm(ks[23], (DEPTH, D_FF, D_MODEL), D_FF ** -0.5),
        'rel_bias': nrm(ks[24], (NUM_BUCKETS, N_DIFF_HEADS), 0.5),
        'g_final': gain(ks[25], (D_MODEL,)),
    }


def reference(x, c, w_ada, b_ada, g_norm1, w_in, conv_lru_w, conv_lru_b, lru_wa, lru_ba,
              lru_wx, lru_bx, lru_lambda, lam_q1, lam_k1, lam_q2, lam_k2, g_subln, w_out,
              g_norm2, w_up, conv_ffn_w, conv_ffn_b, w_down, rel_bias, g_final):
    b, s, _ = x.shape
    splits = [QK_WIDTH, 2 * QK_WIDTH, 2 * QK_WIDTH + ATTN_WIDTH, 2 * QK_WIDTH + ATTN_WIDTH + LRU_WIDTH]
    cond = jax.nn.silu(c)
    for l in range(DEPTH):
        mod = cond @ w_ada[l] + b_ada[l]
        shift1, scale1, gate1, shift2, scale2, gate2 = jnp.split(mod[:, None, :], 6, axis=-1)

        h = rms_norm(x, g_norm1[l]) * (1.0 + scale1) + shift1
        proj = h @ w_in[l]
        q, k, v, xr, yg = jnp.split(proj, splits, axis=-1)
        q = q.reshape(b, s, N_DIFF_HEADS, 2, DIFF_HEAD_DIM)
        k = k.reshape(b, s, N_DIFF_HEADS, 2, DIFF_HEAD_DIM)
        v = v.reshape(b, s, N_DIFF_HEADS, DIFF_V_DIM)

        lambda_init = 0.8 - 0.6 * math.exp(-0.3 * l)
        lam = (jnp.exp(jnp.sum(lam_q1[l].astype(jnp.float32) * lam_k1[l].astype(jnp.float32)))
               - jnp.exp(jnp.sum(lam_q2[l].astype(jnp.float32) * lam_k2[l].astype(jnp.float32)))
               + lambda_init)
        attn = diff_attention(q, k, v, rel_bias, lam)
        attn = (rms_norm(attn, g_subln[l]) * (1.0 - lambda_init)).reshape(b, s, ATTN_WIDTH)

        xr = causal_dwconv(xr, conv_lru_w[l], conv_lru_b[l])
        lru = rg_lru(xr, lru_wa[l], lru_ba[l], lru_wx[l], lru_bx[l], lru_lambda[l])
        lru = lru * jax.nn.gelu(yg, approximate=True)

        mix = jnp.concatenate([lru, attn], axis=-1) @ w_out[l]
        x = x + gate1 * mix

        h = rms_norm(x, g_norm2[l]) * (1.0 + scale2) + shift2
        a, g = jnp.split(h @ w_up[l], 2, axis=-1)
        a = causal_dwconv(a, conv_ffn_w[l], conv_ffn_b[l])
        ff = (jax.nn.gelu(a, approximate=True) * g) @ w_down[l]
        x = x + gate2 * ff
    return rms_norm(x, g_final)
```

```python
import numpy as np
from contextlib import ExitStack
import concourse.bass as bass
import concourse.mybir as mybir
from concourse.bass import ds
from concourse.bass_utils import run_bass_kernel_spmd

F32 = mybir.dt.float32
BF16 = mybir.dt.bfloat16
ALU = mybir.AluOpType
AF = mybir.ActivationFunctionType

NCORES = 8
S_TOT = 16384
D = 1024
TPC = S_TOT // NCORES
NT = TPC // 128
D_IN = 2560
D_FF = 3072
EPS = 1e-6
NQB = 64
LAMBDA_INIT = 0.8 - 0.6
QOFF, KOFF, VOFF = 0, 4 * 128 * 2048, 2 * 4 * 128 * 2048
NB1 = VOFF + 4 * 128 * 16 * 130
GELU_K = 1.5957691216057308
GELU_C = 0.044715


class Tok:
    __slots__ = ("name", "ws", "wg", "gdeps", "r")

    def __init__(self, name=""):
        self.name = name
        self.ws = []
        self.wg = None
        self.gdeps = []
        self.r = {}


class Op:
    __slots__ = ("eng", "fn", "deps", "dma", "needed", "ev", "idx", "tail", "cc")


class Sched:
    ENGS = ("pe", "act", "dve", "pool", "sp")

    def __init__(self, nc, es):
        self.nc = nc
        self.es = es
        self.sem = {k: es.enter_context(nc.semaphore("s_" + k)) for k in self.ENGS}
        self.dsem = {}
        self.dcnt = {}
        self.dgroup = set()
        self.cnt = {k: 0 for k in self.ENGS}
        self.seen = {k: {} for k in self.ENGS}
        self.pending = []
        self.nidx = 0

    def op(self, eng, fn, reads=(), writes=(), dma=None, deps=(), cc=False, wg=None, group=False):
        o = Op()
        o.eng, o.fn, o.dma, o.needed, o.ev, o.cc = eng, fn, dma, False, None, cc
        o.idx = self.nidx
        o.tail = None
        self.nidx += 1
        if dma is not None and dma not in self.dsem:
            self.dsem[dma] = self.es.enter_context(self.nc.semaphore("d_" + str(dma)))
            self.dcnt[dma] = 0
        if group:
            self.dgroup.add(dma)
        d = {}

        def add(x):
            if x is None or x is o:
                return
            k = ("d", x.dma) if x.dma is not None else ("e", x.eng)
            if k not in d or d[k].idx < x.idx:
                d[k] = x

        for x in deps:
            add(x)
        for t in reads:
            for x in t.ws:
                add(x)
        for t in writes:
            if wg is not None and t.wg == wg:
                for x in t.gdeps:
                    add(x)
                for x in t.r.values():
                    add(x)
            else:
                gd = list(t.ws) + list(t.r.values())
                for x in gd:
                    add(x)
                t.gdeps = gd
        o.deps = list(d.values())
        for x in o.deps:
            x.needed = True
        mk = ("d", dma) if dma is not None else ("e", eng)
        for t in reads:
            t.r[mk] = o
        for t in writes:
            if wg is not None and t.wg == wg:
                t.ws.append(o)
            else:
                t.ws = [o]
                t.wg = wg
                t.r = {}
        self.pending.append(o)
        return o

    def flush(self):
        nc = self.nc
        last = {}
        for o in self.pending:
            if o.dma is None:
                last[o.eng] = o
        for o in last.values():
            o.needed = True
        for o in self.pending:
            o.tail = last
            if o.dma is not None:
                self.dcnt[o.dma] += (1 if o.cc else 16)
                o.ev = ("d", o.dma, self.dcnt[o.dma])
            elif o.needed:
                self.cnt[o.eng] += 1
                o.ev = ("e", o.eng, self.cnt[o.eng])
        for o in self.pending:
            if o.dma is not None and o.dma in self.dgroup:
                o.ev = ("d", o.dma, self.dcnt[o.dma])
        byeng = {k: [o for o in self.pending if o.eng == k] for k in self.ENGS}
        drain = sorted({o.dma for o in self.pending if o.dma is not None and not o.cc}, key=str)

        def emit2(E, ename):
            seen = self.seen[ename]
            for o in byeng[ename]:
                waits = []
                for dpo in o.deps:
                    if dpo.ev is None:
                        dpo = dpo.tail[dpo.eng]
                    kind, key, val = dpo.ev
                    if kind == "e" and key == "pe" and ename == "pe" and o.dma is None:
                        continue
                    sk = (kind, key)
                    if seen.get(sk, 0) >= val:
                        continue
                    seen[sk] = val
                    waits.append((self.sem[key] if kind == "e" else self.dsem[key], val))
                for sm, v in waits[1:]:
                    E.wait_ge(sm, v)
                ins = o.fn(E)
                if ins is None:
                    for sm, v in waits[:1]:
                        E.wait_ge(sm, v)
                    if o.ev is not None and o.dma is None:
                        E.nop().then_inc(self.sem[ename], 1)
                    continue
                if not isinstance(ins, (list, tuple)):
                    ins = [ins]
                if waits:
                    ins[0]._wait_ge(*waits[0])
                if o.dma is not None:
                    ins[-1].then_inc(self.dsem[o.dma], 1 if o.cc else 16)
                elif o.ev is not None:
                    ins[-1].then_inc(self.sem[ename], 1)

        with nc.Block() as block:
            @block.tensor
            def _(E):
                emit2(E, "pe")

            @block.scalar
            def _(E):
                emit2(E, "act")

            @block.vector
            def _(E):
                emit2(E, "dve")

            @block.gpsimd
            def _(E):
                emit2(E, "pool")

            @block.sync
            def _(E):
                emit2(E, "sp")
                for k in drain:
                    E.wait_ge(self.dsem[k], self.dcnt[k])
        self.pending = []

    def dma(self, eng, out, in_, reads=(), writes=(), key="misc", group=False, wg=None):
        return self.op(eng, lambda E: E.dma_start(out=out, in_=in_), reads, writes, dma=key, group=group, wg=wg)

    def dmaf(self, eng, fn, reads=(), writes=(), key="misc", group=False, wg=None):
        return self.op(eng, fn, reads, writes, dma=key, group=group, wg=wg)

    def act(self, out, in_, func, reads=(), writes=(), bias=None, scale=1.0, accum=None, wg=None):
        kw = {}
        if bias is not None:
            kw["bias"] = bias
        if accum is not None:
            kw["accum_out"] = accum
        return self.op("act", lambda E: E.activation(out=out, in_=in_, func=func, scale=scale, **kw),
                       reads, writes, wg=wg)

    def ts(self, eng, out, in0, s1, s2, op0, op1=None, reads=(), writes=(), accum=None, wg=None):
        kw = {}
        if op1 is not None:
            kw["op1"] = op1
        if accum is not None:
            kw["accum_out"] = accum
        return self.op(eng, lambda E: E.tensor_scalar(out=out, in0=in0, scalar1=s1, scalar2=s2, op0=op0, **kw),
                       reads, writes, wg=wg)

    def stt(self, eng, out, in0, scalar, in1, op0, op1, reads=(), writes=(), accum=None, wg=None):
        kw = {}
        if accum is not None:
            kw["accum_out"] = accum
        return self.op(eng, lambda E: E.scalar_tensor_tensor(out=out, in0=in0, scalar=scalar, in1=in1,
                                                             op0=op0, op1=op1, **kw), reads, writes, wg=wg)

    def tt(self, eng, out, in0, in1, op, reads=(), writes=(), wg=None):
        return self.op(eng, lambda E: E.tensor_tensor(out=out, in0=in0, in1=in1, op=op), reads, writes, wg=wg)

    def copy(self, eng, out, in_, reads=(), writes=(), wg=None):
        if eng == "act":
            return self.op(eng, lambda E: E.copy(out=out, in_=in_), reads, writes, wg=wg)
        return self.op(eng, lambda E: E.tensor_copy(out=out, in_=in_), reads, writes, wg=wg)

    def memset(self, eng, ap, val, writes=(), wg=None):
        return self.op(eng, lambda E: E.memset(ap, val), (), writes, wg=wg)

    def scan(self, eng, out, d0, d1, init, reads=(), writes=()):
        return self.op(eng, lambda E: E.tensor_tensor_scan(out=out, data0=d0, data1=d1, initial=init,
                                                           op0=ALU.mult, op1=ALU.add), reads, writes)

    def mm(self, out, items, reads=(), writes=(), start=True, stop=True, skip=False, wg=None):
        def fn(E):
            res = []
            n = len(items)
            for i, (l, r) in enumerate(items):
                kw = {}
                if skip:
                    kw["skip_group_check"] = True
                res.append(E.matmul(out, lhsT=l, rhs=r, start=(start and i == 0), stop=(stop and i == n - 1), **kw))
            return res
        return self.op("pe", fn, reads, writes, wg=wg)

    def tr(self, out, in_, ident, reads=(), writes=(), wg=None):
        return self.op("pe", lambda E: E.transpose(out=out, in_=in_, identity=ident), reads, writes, wg=wg)


def _colT(v, n):
    return np.ascontiguousarray(np.asarray(v, np.float32).reshape(n, 128).T)


def _t5_bucket(rel):
    n = np.maximum(rel, 0)
    nf = np.maximum(n, 1).astype(np.float32)
    large = 16 + (np.log(nf / 16) / np.log(128 / 16) * 16).astype(np.int32)
    large = np.minimum(large, 31)
    return np.where(n < 16, n, large)


INPUT_SPECS = [
    ("x_own", [TPC, D]), ("x_prev", [128, D]), ("flags", [128, 4]), ("sel_prev", [128, 8]),
    ("c_in", [1, D]), ("w_adaT", [128, 48, D]), ("b_adaT", [128, 48]), ("g1T", [128, 8]), ("g2T", [128, 8]),
    ("g_final", [1, D]), ("w_in", [D, D_IN]), ("clw", [128, 4, 4]), ("clb", [128, 4]),
    ("waBD", [128, 4, 128]), ("wxBD", [128, 4, 128]), ("lbaT", [128, 4]), ("lbxT", [128, 4]), ("lamT", [128, 4]),
    ("lamq", [1, 256]), ("g_subln", [1, 128]), ("w_out", [D, D]), ("w_up", [D, 2 * D_FF]),
    ("cfw", [128, 24, 3]), ("cfb", [128, 24]), ("w_down", [D_FF, D]),
    ("bias_g", [128, 3, 128]), ("bias_m", [128, 3, 128]), ("bfar", [128, 1]), ("ident", [128, 128]),
]


def make_inmaps(inp):
    f = lambda a: np.ascontiguousarray(np.asarray(a, np.float32))
    x = f(inp["x"])[0]
    w_ada = f(inp["w_ada"])[0]
    w_adaT = np.ascontiguousarray(w_ada.T.reshape(48, 128, D).transpose(1, 0, 2))
    wa, wx = f(inp["lru_wa"])[0], f(inp["lru_wx"])[0]

    def bd(w):
        o = np.zeros((128, 4, 128), np.float32)
        for g in range(4):
            o[0:64, g, 0:64] = w[2 * g]
            o[64:128, g, 64:128] = w[2 * g + 1]
        return o
    clw = np.ascontiguousarray(f(inp["conv_lru_w"])[0].reshape(4, 4, 128).transpose(2, 1, 0))
    cfw = np.ascontiguousarray(f(inp["conv_ffn_w"])[0].reshape(3, 24, 128).transpose(2, 1, 0))
    rel_bias = f(inp["rel_bias"])
    common = {
        "c_in": f(inp["c"]), "w_adaT": w_adaT, "b_adaT": _colT(f(inp["b_ada"])[0], 48),
        "g1T": _colT(f(inp["g_norm1"])[0], 8), "g2T": _colT(f(inp["g_norm2"])[0], 8),
        "g_final": f(inp["g_final"]).reshape(1, D), "w_in": f(inp["w_in"])[0],
        "clw": clw, "clb": _colT(f(inp["conv_lru_b"])[0], 4), "waBD": bd(wa), "wxBD": bd(wx),
        "lbaT": _colT(f(inp["lru_ba"])[0], 4), "lbxT": _colT(f(inp["lru_bx"])[0], 4),
        "lamT": _colT(f(inp["lru_lambda"])[0], 4),
        "lamq": np.concatenate([f(inp["lam_q1"]), f(inp["lam_k1"]), f(inp["lam_q2"]), f(inp["lam_k2"])], 0).reshape(1, 256),
        "g_subln": f(inp["g_subln"]).reshape(1, 128), "w_out": f(inp["w_out"])[0], "w_up": f(inp["w_up"])[0],
        "cfw": cfw, "cfb": _colT(f(inp["conv_ffn_b"])[0], 24), "w_down": f(inp["w_down"])[0],
        "ident": np.eye(128, dtype=np.float32),
    }
    kk = np.arange(128)[:, None]
    qq = np.arange(128)[None, :]
    maps = []
    for r in range(NCORES):
        h, z = r // 2, r % 2
        m = dict(common)
        m["x_own"] = np.ascontiguousarray(x[r * TPC:(r + 1) * TPC])
        m["x_prev"] = np.ascontiguousarray(x[r * TPC - 128:r * TPC]) if r > 0 else np.zeros((128, D), np.float32)
        fl = np.zeros((128, 4), np.float32)
        fl[:, 0] = 1.0 if r > 0 else 0.0
        m["flags"] = fl
        sp = np.zeros((128, 8), np.float32)
        if r > 0:
            sp[:, r - 1] = 1.0
        m["sel_prev"] = sp
        bg = np.zeros((128, 3, 128), np.float32)
        bm = np.zeros((128, 3, 128), np.float32)
        for t in range(3):
            dd = z + 1 - t
            rel = dd * 128 + qq - kk
            bg[:, t, :] = rel_bias[_t5_bucket(rel), h]
            bm[:, t, :] = np.where(rel >= 0, 0.0, -1e30)
        m["bias_g"], m["bias_m"] = bg, bm
        m["bfar"] = np.ascontiguousarray(np.broadcast_to(rel_bias[31, h], (128, 1))).astype(np.float32)
        maps.append(m)
    return maps


def build(debug=False, stop_after=None):
    nc = bass.Bass("TRN2", target_bir_lowering=False)
    es = ExitStack()
    I = {n: nc.dram_tensor(n, list(s), F32, kind="ExternalInput").ap() for n, s in INPUT_SPECS}
    out_d = nc.dram_tensor("out", [TPC, D], F32, kind="ExternalOutput").ap()
    dbg = {}

    def dbg_out(name, shape, dt=F32):
        dbg[name] = nc.dram_tensor("dbg_" + name, list(shape), dt, kind="ExternalOutput").ap()
        return dbg[name]

    ag1_in = nc.dram_tensor("ag1_in", [128, NB1 // 128], BF16)
    ag1_out = nc.dram_tensor("ag1_out", [1024, NB1 // 128], BF16)
    ag2_in = nc.dram_tensor("ag2_in", [128, 8], F32)
    ag2_out = nc.dram_tensor("ag2_out", [1024, 8], F32)
    ag3_in = [nc.dram_tensor("ag3_in%d" % i, [16 * 128, 128], BF16) for i in range(4)]
    ag3_out = nc.dram_tensor("ag3_out", [8 * NQB * 128, 128], BF16)
    ag4_in = nc.dram_tensor("ag4_in", [128, 48], F32)
    ag4_out = nc.dram_tensor("ag4_out", [1024, 48], F32)
    mod_scr = nc.dram_tensor("mod_scr", [1, 48 * 128], F32)
    xmid_scr = nc.dram_tensor("xmid_scr", [TPC, D], F32)
    h2_scr = nc.dram_tensor("h2_scr", [8, 128, TPC], BF16)
    a1f = ag1_in.ap().rearrange("a b -> (a b)")
    g1f = ag1_out.ap().rearrange("a b -> (a b)")
    RG = [list(range(NCORES))]

    S = Sched(nc, es)
    T = Tok
    pidc = {}

    def pid_of(E, name):
        k = (name, S.nidx // 10 ** 9, id(E))
        if k not in pidc:
            pidc[k] = E.partition_id()
        return pidc[k]

    def sbuf(st, name, shape, dt=F32):
        return st.enter_context(nc.sbuf_tensor("sb_" + name, list(shape), dt))

    def psum(st, name, shape, dt=F32):
        return st.enter_context(nc.psum_tensor("ps_" + name, list(shape), dt))

    t_ag1in, t_ag1out, t_ag2in, t_ag2out = T(), T(), T(), T()
    t_ag3in, t_ag3out, t_ag4in, t_ag4out = [T() for _ in range(4)], T(), T(), T()
    t_modscr, t_xmid, t_h2scr = T(), [T() for _ in range(NT)], [T() for _ in range(NT)]

    ident = sbuf(es, "ident", [128, 128])
    identb = sbuf(es, "identb", [128, 128], BF16)
    modT = sbuf(es, "modT", [128, 48])
    gs1 = sbuf(es, "gs1", [128, 8])
    gs2 = sbuf(es, "gs2", [128, 8])
    cpar = sbuf(es, "cpar", [128, 64])
    flags = sbuf(es, "flags", [128, 4])
    selp = sbuf(es, "selp", [128, 8])
    lam_t = sbuf(es, "lam_t", [128, 4])
    lru_out = sbuf(es, "lru_out", [128, 4, TPC], BF16)
    t_const = T()
    t_mod = T()
    t_lru = [T() for _ in range(4)]
    CLW, CLB, LBA, LBX, LAM, NSP = 0, 16, 20, 24, 28, 32
    cfw = sbuf(es, "cfw", [128, 24, 3])
    cfb = sbuf(es, "cfb", [128, 24])

    with ExitStack() as p0:
        cb = sbuf(p0, "cb", [128, D])
        wst = [sbuf(p0, "wst%d" % i, [128, 4, D]) for i in range(2)]
        junk = {"dve": sbuf(p0, "junk_d", [128, D]), "pool": sbuf(p0, "junk_p", [128, D])}
        lq = sbuf(p0, "lq", [128, 4, 64])
        ltmp = sbuf(p0, "ltmp", [128, 8])
        modsb = sbuf(p0, "modsb", [48, 128])
        pt0 = psum(p0, "pt0", [128, 512])
        t_cb, t_wst, t_junk = T(), [T(), T()], {"dve": T(), "pool": T()}
        t_lq, t_ltmp, t_modsb, t_pt0 = T(), T(), T(), T()
        tmodc = [T() for _ in range(48)]

        S.dma("sp", ident[:], I["ident"][:, :], (), [t_const], key="c0", group=True, wg="c0")
        S.dma("sp", flags[:], I["flags"][:, :], (), [t_const], key="c0", group=True, wg="c0")
        S.dma("sp", selp[:], I["sel_prev"][:, :], (), [t_const], key="c0", group=True, wg="c0")
        S.dma("sp", cpar[:, CLW:CLW + 16], I["clw"].rearrange("p g j -> p (g j)"), (), [t_const], key="c0", group=True, wg="c0")
        S.dma("sp", cpar[:, CLB:CLB + 4], I["clb"][:, :], (), [t_const], key="c0", group=True, wg="c0")
        S.dma("sp", cpar[:, LBA:LBA + 4], I["lbaT"][:, :], (), [t_const], key="c0", group=True, wg="c0")
        S.dma("sp", cpar[:, LBX:LBX + 4], I["lbxT"][:, :], (), [t_const], key="c0", group=True, wg="c0")
        S.dma("sp", cpar[:, LAM:LAM + 4], I["lamT"][:, :], (), [t_const], key="c0", group=True, wg="c0")
        S.dma("sp", cfw[:], I["cfw"][:, :, :], (), [t_const], key="c0", group=True, wg="c0")
        S.dma("sp", cfb[:], I["cfb"][:, :], (), [t_const], key="c0", group=True, wg="c0")
        S.dma("sp", gs1[:], I["g1T"][:, :], (), [t_mod], key="c0", group=True, wg="c0")
        S.dma("sp", gs2[:], I["g2T"][:, :], (), [t_mod], key="c0", group=True, wg="c0")
        badd = sbuf(p0, "badd", [128, 48])
        t_badd = T()
        S.dma("sp", badd[:], I["b_adaT"][:, :], (), [t_badd], key="c0", group=True)
        S.copy("dve", identb[:], ident[:], [t_const], [t_const])
        S.dma("sp", cb[:], I["c_in"][0:1, :].partition_broadcast(128), (), [t_cb], key="c0", group=True)
        S.act(cb[:], cb[:], AF.Silu, [t_cb], [t_cb])
        S.dma("sp", lq[:].rearrange("p a b -> p (a b)"),
              I["lamq"][0:1, :].partition_broadcast(128), (), [t_lq], key="c0", group=True)
        for j in range(2):
            S.stt("dve", junk["dve"][:, 0:64], lq[:, 2 * j, :], 1.0, lq[:, 2 * j + 1, :], ALU.mult, ALU.mult,
                  [t_lq], [t_junk["dve"], t_ltmp], accum=ltmp[:, j:j + 1])
        S.act(ltmp[:, 2:4], ltmp[:, 0:2], AF.Exp, [t_ltmp], [t_ltmp])
        S.stt("dve", lam_t[:, 0:1], ltmp[:, 2:3], LAMBDA_INIT, ltmp[:, 3:4], ALU.add, ALU.subtract,
              [t_ltmp], [t_const])
        S.act(cpar[:, NSP:NSP + 4], cpar[:, LAM:LAM + 4], AF.Exp, [t_const], [t_const], scale=-1.0)
        S.act(cpar[:, NSP:NSP + 4], cpar[:, NSP:NSP + 4], AF.Ln, [t_const], [t_const], bias=1.0)
        S.ts("dve", cpar[:, NSP:NSP + 4], cpar[:, NSP:NSP + 4], -8.0, None, ALU.mult, reads=[t_const],
             writes=[t_const])
        for j in range(12):
            b = j % 2
            S.dma("sp", wst[b][:], I["w_adaT"][:, 4 * j:4 * j + 4, :], (), [t_wst[b]], key="wst%d" % b)
            for jj in range(4):
                e = "dve"
                ch = 4 * j + jj
                S.stt(e, junk[e][:], wst[b][:, jj, :], 1.0, cb[:], ALU.mult, ALU.mult,
                      [t_wst[b], t_cb], [t_junk[e], tmodc[ch]], accum=modT[:, ch:ch + 1])
        S.tt("dve", modT[:], modT[:], badd[:], ALU.add, [t_badd] + tmodc, [t_mod])
        S.stt("dve", gs1[:], modT[:, 8:16], 1.0, gs1[:], ALU.add, ALU.mult, [t_mod], [t_mod])
        S.stt("dve", gs2[:], modT[:, 32:40], 1.0, gs2[:], ALU.add, ALU.mult, [t_mod], [t_mod])
        S.tr(pt0[0:48, 0:128], modT[:, :], ident[:], [t_mod, t_const], [t_pt0])
        S.copy("dve", modsb[:], pt0[0:48, 0:128], [t_pt0], [t_modsb])
        S.dma("sp", mod_scr.ap().rearrange("o (a b) -> (o a) b", a=48), modsb[:], [t_modsb], [t_modscr], key="modst")
        if debug:
            S.dma("sp", dbg_out("modT", [128, 48]), modT[:], [t_mod], [], key="dbg", group=True)
        S.flush()
    sh1 = modT[:, 0:8]
    sh2 = modT[:, 24:32]
    if stop_after == "0":
        return finish(nc, es, S, out_d, dbg)

    with ExitStack() as p1:
        hT = sbuf(p1, "hT", [128, 8, 17 * 128], BF16)
        t_hT = [T() for _ in range(17)]
        wlru = sbuf(p1, "wlru", [128, 8, 1024], BF16)
        t_wlru = T()

        with ExitStack() as pa:
            wqkv = sbuf(pa, "wqkv", [128, 8, 1536], BF16)
            t_wqkv = T()
            wis = [sbuf(pa, "wis%d" % i, [128, D_IN]) for i in range(4)]
            t_wis = [T() for _ in range(4)]

            def load_w(kc):
                S.dma("act", wis[kc % 4][:], I["w_in"][kc * 128:(kc + 1) * 128, :], (), [t_wis[kc % 4]],
                      key="wis%d" % (kc % 4))
            for kc in range(4):
                load_w(kc)
            xt = [sbuf(pa, "xt%d" % i, [128, D]) for i in range(2)]
            xn = [sbuf(pa, "xn%d" % i, [128, D]) for i in range(2)]
            sq = sbuf(pa, "sq", [128, D])
            ss = sbuf(pa, "ss", [128, 40])
            stq = [sbuf(pa, "stq%d" % i, [128, 512], BF16) for i in range(3)]
            vst = [sbuf(pa, "vst%d" % i, [128, 4, 4, 130], BF16) for i in range(2)]
            t_xt, t_xn, t_sq, t_ss = [T(), T()], [T(), T()], T(), T()
            t_stq, t_vst = [T() for _ in range(3)], [T(), T()]
            pb = [psum(pa, "pa%d" % i, [128, 512]) for i in range(8)]
            t_pb = [T() for _ in range(8)]
            for i in range(2):
                S.memset("pool", vst[i][:, :, :, 128:129], 1.0, [t_vst[i]])
                S.memset("pool", vst[i][:, :, :, 129:130], 0.0, [t_vst[i]])
            Qv = a1f[QOFF:QOFF + 4 * 128 * 2048].rearrange("(h z p t) -> h z p t", h=4, z=2, p=128)
            Kv = a1f[KOFF:KOFF + 4 * 128 * 2048].rearrange("(h p t) -> h p t", h=4, p=128)
            Vv = a1f[VOFF:VOFF + 4 * 128 * 16 * 130].rearrange("(h p b c) -> p h b c", h=4, p=128, b=16)
            nq = 0

            def cast_w(kc):
                S.copy("dve", wqkv[:, kc, :], wis[kc % 4][:, 0:1536], [t_wis[kc % 4]], [t_wqkv], wg="w")
                S.copy("act", wlru[:, kc, :], wis[kc % 4][:, 1536:2560], [t_wis[kc % 4]], [t_wlru], wg="w")
                if kc + 4 < 8:
                    load_w(kc + 4)
            for ti in range(17):
                if 1 <= ti <= 4:
                    cast_w(2 * (ti - 1))
                    cast_w(2 * (ti - 1) + 1)
                b = ti % 2
                src = I["x_prev"][:, :] if ti == 0 else I["x_own"][(ti - 1) * 128:ti * 128, :]
                S.dma("sp", xt[b][:], src, (), [t_xt[b]], key="xt%d" % b)
                S.act(sq[:], xt[b][:], AF.Square, [t_xt[b]], [t_sq, t_ss], accum=ss[:, ti:ti + 1])
                S.ts("dve", ss[:, ti:ti + 1], ss[:, ti:ti + 1], 1.0 / D, EPS, ALU.mult, ALU.add, [t_ss], [t_ss])
                S.act(ss[:, ti:ti + 1], ss[:, ti:ti + 1], AF.Ln, [t_ss], [t_ss])
                S.act(ss[:, ti:ti + 1], ss[:, ti:ti + 1], AF.Exp, [t_ss], [t_ss], scale=-0.5)
                S.ts("dve", xn[b][:], xt[b][:], ss[:, ti:ti + 1], None, ALU.mult, reads=[t_xt[b], t_ss],
                     writes=[t_xn[b]])
                for half in range(2):
                    bank = (ti % 2) * 2 + half
                    for j in range(4):
                        kc = half * 4 + j
                        S.tr(pb[bank][:, j * 128:(j + 1) * 128], xn[b][:, kc * 128:(kc + 1) * 128], ident[:],
                             [t_xn[b], t_const], [t_pb[bank]])
                    for j in range(4):
                        kc = half * 4 + j
                        o_ap = hT[:, kc, ti * 128:(ti + 1) * 128]
                        i_ap = pb[bank][:, j * 128:(j + 1) * 128]
                        if j % 2 == 0:
                            S.ts("dve", o_ap, i_ap, gs1[:, kc:kc + 1], sh1[:, kc:kc + 1], ALU.mult, ALU.add,
                                 [t_pb[bank], t_mod], [t_hT[ti]])
                        else:
                            S.act(o_ap, i_ap, AF.Identity, [t_pb[bank], t_mod], [t_hT[ti]],
                                  bias=sh1[:, kc:kc + 1], scale=gs1[:, kc:kc + 1])
                if ti >= 1 and (ti - 1) % 4 == 3:
                    g = (ti - 1) // 4
                    c0 = 128 + g * 512
                    rd = [t_hT[1 + 4 * g + k] for k in range(4)] + [t_wqkv]
                    for cbk in range(8):
                        bank = 4 + (nq % 4)
                        nq += 1
                        S.mm(pb[bank][:, :], [(wqkv[:, kc, cbk * 128:(cbk + 1) * 128], hT[:, kc, c0:c0 + 512])
                                              for kc in range(8)], rd, [t_pb[bank]])
                        sb_i = nq % 3
                        S.copy("act" if cbk % 2 == 0 else "dve", stq[sb_i][:], pb[bank][:, :], [t_pb[bank]],
                               [t_stq[sb_i]])
                        if cbk < 4:
                            for z in range(2):
                                S.dma("sp", Qv[cbk, z, :, 2 * g * 128:(2 * g + 2) * 128].rearrange("p (a t) -> p a t", a=2),
                                      stq[sb_i][:].rearrange("p (a z t) -> p a z t", a=2, z=2)[:, :, z, :],
                                      [t_stq[sb_i]], [t_ag1in], key="stq%d_%d" % (sb_i, z), wg="ag1")
                        else:
                            S.dma("sp", Kv[cbk % 4, :, g * 512:(g + 1) * 512], stq[sb_i][:], [t_stq[sb_i]], [t_ag1in],
                                  key="stq%d" % sb_i, wg="ag1")
                    vb = g % 2
                    for k in range(4):
                        bank = 4 + (nq % 4)
                        nq += 1
                        tcol = c0 + k * 128
                        S.mm(pb[bank][:, :], [(hT[:, kc, tcol:tcol + 128], wqkv[:, kc, 1024:1536])
                                              for kc in range(8)], rd, [t_pb[bank]])
                        S.copy("act" if k % 2 == 0 else "dve", vst[vb][:, :, k, 0:128],
                               pb[bank][:, :].rearrange("p (h c) -> p h c", h=4), [t_pb[bank]], [t_vst[vb]])
                    S.dma("sp", Vv[:, :, 4 * g:4 * g + 4, :], vst[vb][:], [t_vst[vb]], [t_ag1in], key="vst%d" % vb, wg="ag1")
            S.op("pool", lambda E: E.collective_compute("AllGather", ALU.bypass, replica_groups=RG,
                                                       ins=[ag1_in.ap().opt()], outs=[ag1_out.ap().opt()]),
                 [t_ag1in], [t_ag1out], dma="cc1", cc=True)
            if debug:
                S.dma("sp", dbg_out("hT", [128, 8, 17 * 128], BF16), hT[:], t_hT, [], key="dbg", group=True)
                S.dma("sp", dbg_out("ag1in", [128, NB1 // 128], BF16), ag1_in.ap(), [t_ag1in], [], key="dbg", group=True)
            S.flush()
        if stop_after == "A":
            return finish(nc, es, S, out_d, dbg)

        with ExitStack() as pbo:
            hloc = sbuf(pbo, "hloc", [128, 4, TPC])
            pcum = sbuf(pbo, "pcum", [128, 4, TPC])
            t_hloc = [[T() for _ in range(4)] for _ in range(4)]
            with ExitStack() as pbs:
                wab = sbuf(pbs, "wab", [128, 4, 128], BF16)
                wxb = sbuf(pbs, "wxb", [128, 4, 128], BF16)
                t_wab = T()
                wabf = sbuf(pbs, "wabf", [128, 2, 4, 128])
                t_wabf = T()
                S.dma("sp", wabf[:, 0], I["waBD"][:, :, :], (), [t_wabf], key="wab", group=True, wg="w")
                S.dma("sp", wabf[:, 1], I["wxBD"][:, :, :], (), [t_wabf], key="wab", group=True, wg="w")
                S.copy("dve", wab[:], wabf[:, 0], [t_wabf], [t_wab], wg="w")
                S.copy("dve", wxb[:], wabf[:, 1], [t_wabf], [t_wab], wg="w")
                zeros = sbuf(pbs, "zeros", [128, 512])
                t_zero = T()
                S.memset("pool", zeros[:], 0.0, [t_zero])
                NB = 3
                xrs = [sbuf(pbs, "xrs%d" % i, [128, 515]) for i in range(NB)]
                xc = [sbuf(pbs, "xc%d" % i, [128, 512]) for i in range(NB)]
                xcb = [sbuf(pbs, "xcb%d" % i, [128, 512], BF16) for i in range(NB)]
                rr = [sbuf(pbs, "rr%d" % i, [128, 512]) for i in range(NB)]
                ii = [sbuf(pbs, "ii%d" % i, [128, 512]) for i in range(NB)]
                aa = [sbuf(pbs, "aa%d" % i, [128, 512]) for i in range(NB)]
                uu = [sbuf(pbs, "uu%d" % i, [128, 512]) for i in range(NB)]
                gg = [sbuf(pbs, "gg%d" % i, [128, 512]) for i in range(NB)]
                t_xrs, t_xc, t_xcb = [T() for _ in range(NB)], [T() for _ in range(NB)], [T() for _ in range(NB)]
                t_rr, t_ii, t_aa = [T() for _ in range(NB)], [T() for _ in range(NB)], [T() for _ in range(NB)]
                t_uu, t_gg = [T() for _ in range(NB)], [T() for _ in range(NB)]
                xtail = sbuf(pbs, "xtail", [128, 4, 3])
                st_h = sbuf(pbs, "st_h", [128, 4])
                st_p = sbuf(pbs, "st_p", [128, 4])
                t_xtail, t_st = [T() for _ in range(4)], [T() for _ in range(4)]
                pq = [psum(pbs, "pq%d" % i, [128, 512]) for i in range(8)]
                t_pq = [T() for _ in range(8)]
                S.memset("dve", st_h[:], 0.0, t_st)
                S.memset("dve", st_p[:], 1.0, t_st)

                def bset(it):
                    k = (it % 2) * 4
                    return pq[k:k + 4], t_pq[k:k + 4]

                def b1(it):
                    g, cg = divmod(it, 4)
                    b = it % NB
                    (PX, PY, PR, PI), (tPX, tPY, tPR, tPI) = bset(it)
                    c0 = 128 + g * 512
                    rd = [t_hT[1 + 4 * g + k] for k in range(4)] + [t_wlru]
                    S.mm(PX[:, :], [(wlru[:, kc, cg * 128:(cg + 1) * 128], hT[:, kc, c0:c0 + 512])
                                    for kc in range(8)], rd, [tPX])
                    S.mm(PY[:, :], [(wlru[:, kc, 512 + cg * 128:512 + (cg + 1) * 128], hT[:, kc, c0:c0 + 512])
                                    for kc in range(8)], rd, [tPY])
                    if g == 0:
                        S.mm(PR[:, 0:3], [(wlru[:, kc, cg * 128:(cg + 1) * 128], hT[:, kc, 125:128])
                                          for kc in range(8)], [t_hT[0], t_wlru], [tPR])
                        S.ts("dve", xrs[b][:, 0:3], PR[:, 0:3], flags[:, 0:1], None, ALU.mult,
                             reads=[tPR, t_const], writes=[t_xrs[b]])
                    else:
                        S.copy("pool", xrs[b][:, 0:3], xtail[:, cg, :], [t_xtail[cg]], [t_xrs[b]])
                    S.copy("act", xrs[b][:, 3:515], PX[:, :], [tPX], [t_xrs[b]])
                    S.copy("pool", xtail[:, cg, :], xrs[b][:, 512:515], [t_xrs[b]], [t_xtail[cg]])
                    w = lambda j: cpar[:, CLW + cg * 4 + j:CLW + cg * 4 + j + 1]
                    S.act(xc[b][:], xrs[b][:, 3:515], AF.Identity, [t_xrs[b], t_const], [t_xc[b]],
                          bias=cpar[:, CLB + cg:CLB + cg + 1], scale=w(3))
                    for j in range(3):
                        S.stt("dve", xc[b][:], xrs[b][:, j:j + 512], w(j), xc[b][:],
                              ALU.mult, ALU.add, [t_xrs[b], t_const], [t_xc[b]])
                    S.copy("pool", xcb[b][:], xc[b][:], [t_xc[b]], [t_xcb[b]])

                def b2(it):
                    g, cg = divmod(it, 4)
                    b = it % NB
                    (PX, PY, PR, PI), (tPX, tPY, tPR, tPI) = bset(it)
                    S.mm(PR[:, :], [(wab[:, cg, :], xcb[b][:])], [t_wab, t_xcb[b]], [tPR])
                    S.mm(PI[:, :], [(wxb[:, cg, :], xcb[b][:])], [t_wab, t_xcb[b]], [tPI])
                    S.act(rr[b][:], PR[:, :], AF.Sigmoid, [tPR, t_const], [t_rr[b]],
                          bias=cpar[:, LBA + cg:LBA + cg + 1])
                    S.act(ii[b][:], PI[:, :], AF.Sigmoid, [tPI, t_const], [t_ii[b]],
                          bias=cpar[:, LBX + cg:LBX + cg + 1])
                    S.act(uu[b][:], PY[:, :], AF.Square, [tPY], [t_uu[b]])
                    S.ts("dve", uu[b][:], uu[b][:], GELU_C, 1.0, ALU.mult, ALU.add, [t_uu[b]], [t_uu[b]])
                    S.tt("dve", uu[b][:], uu[b][:], PY[:, :], ALU.mult, [t_uu[b], tPY], [t_uu[b]])
                    S.act(uu[b][:], uu[b][:], AF.Sigmoid, [t_uu[b]], [t_uu[b]], scale=GELU_K)
                    S.tt("dve", gg[b][:], uu[b][:], PY[:, :], ALU.mult, [t_uu[b], tPY], [t_gg[b]])
                    S.act(aa[b][:], rr[b][:], AF.Exp, [t_rr[b], t_const], [t_aa[b]],
                          scale=cpar[:, NSP + cg:NSP + cg + 1])

                def b3(it):
                    g, cg = divmod(it, 4)
                    b = it % NB
                    om_, t_om_ = rr[b], t_rr[b]
                    S.tt("pool", om_[:], aa[b][:], aa[b][:], ALU.mult, [t_aa[b]], [t_om_])
                    S.ts("dve", om_[:], om_[:], -1.0, 1.0, ALU.mult, ALU.add, [t_om_], [t_om_])
                    S.ts("dve", om_[:], om_[:], 0.0, None, ALU.max, reads=[t_om_], writes=[t_om_])
                    S.act(om_[:], om_[:], AF.Ln, [t_om_], [t_om_])
                    S.act(om_[:], om_[:], AF.Exp, [t_om_], [t_om_], scale=0.5)
                    S.tt("pool", ii[b][:], ii[b][:], xc[b][:], ALU.mult, [t_ii[b], t_xc[b]], [t_ii[b]])
                    S.tt("dve", ii[b][:], ii[b][:], om_[:], ALU.mult, [t_ii[b], t_om_], [t_ii[b]])
                    hl = hloc[:, cg, g * 512:(g + 1) * 512]
                    pc = pcum[:, cg, g * 512:(g + 1) * 512]
                    S.scan("dve", hl, aa[b][:], ii[b][:], st_h[:, cg:cg + 1], [t_aa[b], t_ii[b], t_st[cg]],
                           [t_hloc[cg][g]])
                    S.scan("dve", pc, aa[b][:], zeros[:], st_p[:, cg:cg + 1], [t_aa[b], t_zero, t_st[cg]],
                           [t_hloc[cg][g]])
                    S.copy("pool", st_h[:, cg:cg + 1], hloc[:, cg, g * 512 + 511:g * 512 + 512],
                           [t_hloc[cg][g]], [t_st[cg]])
                    S.copy("pool", st_p[:, cg:cg + 1], pcum[:, cg, g * 512 + 511:g * 512 + 512],
                           [t_hloc[cg][g]], [t_st[cg]])
                    S.tt("pool", hl, hl, gg[b][:], ALU.mult, [t_hloc[cg][g], t_gg[b], t_st[cg]],
                         [t_hloc[cg][g]])
                    S.tt("dve", pc, pc, gg[b][:], ALU.mult, [t_hloc[cg][g], t_gg[b], t_st[cg]],
                         [t_hloc[cg][g]])

                for t in range(16 + 2):
                    if t < 16:
                        b1(t)
                    if 0 <= t - 1 < 16:
                        b2(t - 1)
                    if 0 <= t - 2 < 16:
                        b3(t - 2)
                stg = sbuf(pbs, "stg", [128, 8])
                t_stg = T()
                S.copy("dve", stg[:, 0:4], st_p[:], t_st, [t_stg])
                S.copy("dve", stg[:, 4:8], st_h[:], t_st, [t_stg])
                S.dma("sp", ag2_in.ap(), stg[:], [t_stg], [t_ag2in], key="ag2st")
                S.op("pool", lambda E: E.collective_compute("AllGather", ALU.bypass, replica_groups=RG,
                                                           ins=[ag2_in.ap().opt()], outs=[ag2_out.ap().opt()]),
                     [t_ag2in], [t_ag2out], dma="cc2", cc=True)
                if debug:
                    S.dma("sp", dbg_out("hlocG", [128, 4, TPC]), hloc[:], [x for y in t_hloc for x in y], [], key="dbg", group=True)
                S.flush()
            with ExitStack() as pf:
                car = sbuf(pf, "car", [128, 8, 8])
                pre = sbuf(pf, "pre", [128, 4, 8])
                cin = sbuf(pf, "cin", [128, 4])
                jk = sbuf(pf, "jk", [128, 8])
                t_car, t_pre, t_cin, t_jk = T(), T(), T(), T()
                S.dma("sp", car[:], ag2_out.ap().rearrange("(r p) c -> p r c", r=8), [t_ag2out], [t_car], key="car")
                for cg in range(4):
                    S.scan("dve", pre[:, cg, :], car[:, :, cg], car[:, :, 4 + cg], 0.0, [t_car], [t_pre])
                    S.stt("dve", jk[:], pre[:, cg, :], 1.0, selp[:], ALU.mult, ALU.mult, [t_pre, t_const],
                          [t_jk, t_cin], accum=cin[:, cg:cg + 1])
                for cg in range(4):
                    for hf in range(2):
                        sl = slice(hf * 1024, (hf + 1) * 1024)
                        S.stt("dve", lru_out[:, cg, sl], pcum[:, cg, sl], cin[:, cg:cg + 1],
                              hloc[:, cg, sl], ALU.mult, ALU.add, [t_cin] + t_hloc[cg], [t_lru[cg]])
                if debug:
                    S.dma("sp", dbg_out("lru_out", [128, 4, TPC], BF16), lru_out[:], t_lru, [], key="dbg", group=True)
                S.flush()
    if stop_after == "B":
        return finish(nc, es, S, out_d, dbg)

    pw = ExitStack()
    wupA = sbuf(pw, "wupA", [128, 4, 2 * D_FF], BF16)
    t_wup = [T() for _ in range(8)]
    WPC = 2048

    def wup_piece(j, wus, t_wus, wdst):
        kc, pc_ = j // 3, (j % 3) * WPC
        gk = kc if wdst is wupA else kc + 4
        S.dma("act", wus[j % 2][:], I["w_up"][gk * 128:(gk + 1) * 128, pc_:pc_ + WPC], (), [t_wus[j % 2]],
              key="wus%d" % (j % 2))
        S.copy("dve", wdst[:, kc, pc_:pc_ + WPC], wus[j % 2][:], [t_wus[j % 2]], [t_wup[gk]], wg=("wup", gk))

    with ExitStack() as p2:
        wus2 = [sbuf(p2, "wus2_%d" % i, [128, WPC]) for i in range(2)]
        t_wus2 = [T(), T()]
        KT = sbuf(p2, "KT", [128, S_TOT], BF16)
        Vt = sbuf(p2, "Vt", [128, 128, 130], BF16)
        QT = sbuf(p2, "QT", [128, NQB, 128], BF16)
        t_kv = [T() for _ in range(8)]
        t_q = [T() for _ in range(8)]
        bN = sbuf(p2, "bN", [128, 3, 128])
        bM = sbuf(p2, "bM", [128, 3, 128])
        bfar = sbuf(p2, "bfar", [128, 1])
        gsub = sbuf(p2, "gsub", [128, 128])
        t_b = T()
        S.dma("sp", bN[:], I["bias_g"][:, :, :], (), [t_b], key="p2c", group=True, wg="c")
        S.dma("sp", bM[:], I["bias_m"][:, :, :], (), [t_b], key="p2c", group=True, wg="c")
        S.dma("sp", bfar[:], I["bfar"][:, :], (), [t_b], key="p2c", group=True, wg="c")
        S.dma("sp", gsub[:], I["g_subln"][0:1, :].partition_broadcast(128), (), [t_b], key="p2c", group=True, wg="c")
        S.tt("dve", bN[:], bN[:], bM[:], ALU.add, [t_b], [t_b])
        S.ts("dve", gsub[:], gsub[:], 1.0 - LAMBDA_INIT, None, ALU.mult, reads=[t_b], writes=[t_b])
        for r in range(8):
            def ldk(E, r=r):
                p = pid_of(E, "p2")
                base = (p // 2) * (128 * 2048) + (KOFF + r * NB1)
                src = bass.AP(g1f.tensor, base, [[2048, 128], [1, 2048]])
                return E.dma_start(out=KT[:, r * 2048:(r + 1) * 2048], in_=src)

            def ldv(E, r=r):
                p = pid_of(E, "p2")
                base = (p // 2) * (128 * 16 * 130) + (VOFF + r * NB1)
                src = bass.AP(g1f.tensor, base, [[16 * 130, 128], [1, 16 * 130]])
                return E.dma_start(out=Vt[:, r * 16:(r + 1) * 16, :].rearrange("p b c -> p (b c)"), in_=src)

            def ldq(E, r=r):
                p = pid_of(E, "p2")
                base = (p // 2) * (2 * 128 * 1024) + (p % 2) * (128 * 1024) + (QOFF + r * NB1)
                src = bass.AP(g1f.tensor, base, [[1024, 128], [1, 1024]])
                return E.dma_start(out=QT[:, r * 8:(r + 1) * 8, :].rearrange("p b t -> p (b t)"), in_=src)
            S.dmaf("sp", ldq, [t_ag1out], [t_q[r]], key="qg", group=True)
            S.dmaf("act", ldk, [t_ag1out], [t_kv[r]], key="kg", group=True, wg="kv")
            S.dmaf("pool", ldv, [t_ag1out], [t_kv[r]], key="vg", group=True, wg="kv")

        NSB = 3
        psS = [psum(p2, "psS%d" % i, [128, 2, 512]) for i in range(NSB)]
        acc = [psum(p2, "acc%d" % i, [128, 512]) for i in range(2)]
        t_psS, t_acc = [T() for _ in range(NSB)], [T(), T()]
        NPB = 3
        PT = [sbuf(p2, "PT%d" % i, [128, 2, 512], BF16) for i in range(NPB)]
        t_PT = [T() for _ in range(NPB)]
        tmpn = sbuf(p2, "tmpn", [128, 2, 384])
        t_tmpn = T()
        attn = sbuf(p2, "attn", [128, NQB, 128], BF16)
        t_attn = [T() for _ in range(4)]
        o1 = [sbuf(p2, "o1_%d" % i, [128, 128]) for i in range(2)]
        osm = [sbuf(p2, "osm%d" % i, [128, 8]) for i in range(2)]
        ojk = sbuf(p2, "ojk", [128, 128])
        t_o1, t_osm, t_ojk = [T(), T()], [T(), T()], T()
        items = []
        for m in range(NQB):
            far = list(range(0, max(2 * m - 1, 0)))
            near = [kb for kb in (2 * m - 1, 2 * m, 2 * m + 1) if kb >= 0]
            groups = [(far[i:i + 4], False) for i in range(0, len(far), 4)] + [(near, True)]
            for gidx, (kbs, is_near) in enumerate(groups):
                items.append((m, kbs, is_near, gidx == 0, gidx == len(groups) - 1))
        NIT = len(items)
        LA = 2

        def st1(t):
            m, kbs, is_near, first, last = items[t]
            sb_i = t % NSB
            rd = [t_kv[r] for r in sorted({kb // 16 for kb in kbs})] + [t_q[m // 8]]

            def qk(E):
                res = []
                for jj, kb in enumerate(kbs):
                    for mp in range(2):
                        res.append(E.matmul(psS[sb_i][:, mp, jj * 128:(jj + 1) * 128],
                                            lhsT=KT[64 * mp:64 * mp + 64, kb * 128:(kb + 1) * 128],
                                            rhs=QT[64 * mp:64 * mp + 64, m, :], start=True, stop=True))
                return res
            S.op("pe", qk, rd, [t_psS[sb_i]])

        def st2(t):
            m, kbs, is_near, first, last = items[t]
            n = len(kbs)
            sb_i, pb_i = t % NSB, t % NPB
            if not is_near:
                S.act(PT[pb_i][:, :, 0:n * 128], psS[sb_i][:, :, 0:n * 128], AF.Exp, [t_psS[sb_i], t_b],
                      [t_PT[pb_i]], bias=bfar[:, 0:1], scale=0.125)
            else:
                t0 = 3 - n
                for mp in range(2):
                    S.stt("dve", tmpn[:, mp, 0:n * 128], psS[sb_i][:, mp, 0:n * 128], 0.125,
                          bN[:, t0:3, :].rearrange("p a b -> p (a b)"), ALU.mult, ALU.add,
                          [t_psS[sb_i], t_b], [t_tmpn])
                S.act(PT[pb_i][:, :, 0:n * 128], tmpn[:, :, 0:n * 128], AF.Exp, [t_tmpn], [t_PT[pb_i]])

        def st3(t):
            m, kbs, is_near, first, last = items[t]
            pb_i, ab = t % NPB, m % 2
            rdk = sorted({kb // 16 for kb in kbs})

            def pv(E):
                res = []
                for jj, kb in enumerate(kbs):
                    for mp in range(2):
                        stt_ = first and jj == 0 and mp == 0
                        res.append(E.matmul(acc[ab][:, mp * 130:mp * 130 + 129],
                                            lhsT=PT[pb_i][:, mp, jj * 128:(jj + 1) * 128],
                                            rhs=Vt[:, kb, 0:129], start=stt_, stop=False,
                                            skip_group_check=True))
                return res
            S.op("pe", pv, [t_PT[pb_i]] + [t_kv[r] for r in rdk], [t_acc[ab]])
            return last

        def fin_a(m):
            ab = m % 2
            A = acc[ab]
            den = A[:, 128:128 + 131:130]
            S.op("dve", lambda E: E.reciprocal(out=osm[ab][:, 0:2], in_=den), [t_acc[ab]], [t_osm[ab]])
            S.ts("dve", osm[ab][:, 2:3], osm[ab][:, 1:2], lam_t[:, 0:1], -1.0, ALU.mult, ALU.mult,
                 [t_osm[ab], t_const], [t_osm[ab]])
            S.ts("dve", o1[ab][:], A[:, 0:128], osm[ab][:, 0:1], None, ALU.mult, reads=[t_acc[ab], t_osm[ab]],
                 writes=[t_o1[ab]])
            S.stt("dve", o1[ab][:], A[:, 130:258], osm[ab][:, 2:3], o1[ab][:], ALU.mult, ALU.add,
                  [t_acc[ab], t_osm[ab], t_o1[ab]], [t_o1[ab]])
            S.stt("dve", ojk[:], o1[ab][:], 1.0, o1[ab][:], ALU.mult, ALU.mult, [t_o1[ab]], [t_ojk, t_osm[ab]],
                  accum=osm[ab][:, 3:4])
            S.ts("dve", osm[ab][:, 3:4], osm[ab][:, 3:4], 1.0 / 128, EPS, ALU.mult, ALU.add, [t_osm[ab]],
                 [t_osm[ab]])

        def ag3_part(pp):
            S.dma("sp", ag3_in[pp].ap().rearrange("(m q) d -> q m d", q=128), attn[:, 16 * pp:16 * pp + 16, :],
                  [t_attn[pp]], [t_ag3in[pp]], key="attst%d" % pp)
            S.op("pool", lambda E: E.collective_compute(
                "AllGather", ALU.bypass, replica_groups=RG, ins=[ag3_in[pp].ap().opt()],
                outs=[ag3_out.ap()[pp * 8 * 2048:(pp + 1) * 8 * 2048, :].opt()]),
                [t_ag3in[pp]], [t_ag3out], dma="cc3_%d" % pp, cc=True, wg="ag3")

        def fin_b(m):
            ab = m % 2
            S.act(osm[ab][:, 3:4], osm[ab][:, 3:4], AF.Ln, [t_osm[ab]], [t_osm[ab]])
            S.act(osm[ab][:, 3:4], osm[ab][:, 3:4], AF.Exp, [t_osm[ab]], [t_osm[ab]], scale=-0.5)
            S.stt("dve", attn[:, m, :], o1[ab][:], osm[ab][:, 3:4], gsub[:], ALU.mult, ALU.mult,
                  [t_o1[ab], t_osm[ab], t_b], [t_attn[m // 16]], wg="attn")
            if m % 16 == 15:
                ag3_part(m // 16)

        deferred = {}
        for t in range(min(LA, NIT)):
            st1(t)
        for t in range(NIT):
            if t + LA < NIT:
                st1(t + LA)
            st2(t)
            for mm in deferred.pop(t, []):
                fin_b(mm)
            if st3(t):
                mdone = items[t][0]
                fin_a(mdone)
                deferred.setdefault(t + 3, []).append(mdone)
                if mdone >= 4 and mdone % 4 == 0 and (mdone - 4) // 4 < 12:
                    wup_piece((mdone - 4) // 4, wus2, t_wus2, wupA)
        for t in sorted(deferred):
            for mm in deferred[t]:
                fin_b(mm)
        if debug:
            S.dma("sp", dbg_out("attn", [128, NQB, 128], BF16), attn[:], t_attn, [], key="dbg", group=True)
        S.flush()
    if stop_after == "2":
        return finish(nc, es, S, out_d, dbg)

    with ExitStack() as p3:
        wupB = sbuf(p3, "wupB", [128, 4, 2 * D_FF], BF16)

        def wupk(kc, c0, c1):
            return wupA[:, kc, c0:c1] if kc < 4 else wupB[:, kc - 4, c0:c1]
        ahalo = sbuf(p3, "ahalo", [128, 24, 2])
        t_ahalo = [T() for _ in range(24)]
        with ExitStack() as p3a:
            wo = sbuf(p3a, "wo", [128, 8, D], BF16)
            g1b = sbuf(p3a, "g1b", [128, D])
            wus3 = [sbuf(p3a, "wus3_%d" % i, [128, WPC]) for i in range(2)]
            t_wus3 = [T(), T()]
            h2last = sbuf(p3a, "h2last", [128, 8, 2], BF16)
            t_h2last = T()
            t_wo, t_g1b = T(), T()
            S.dma("sp", g1b[:], mod_scr.ap()[0:1, 2048:3072].partition_broadcast(128), [t_modscr], [t_g1b], key="g1b")
            for kc in range(8):
                b = kc % 2
                S.dma("act", wus3[b][:, 0:D], I["w_out"][kc * 128:(kc + 1) * 128, :], (), [t_wus3[b]], key="wus%d" % b)
                S.tt("dve", wo[:, kc, :], wus3[b][:, 0:D], g1b[:], ALU.mult, [t_wus3[b], t_g1b], [t_wo], wg="wo")
            att_all = sbuf(p3a, "att_all", [128, NT, 512], BF16)
            t_attall = T()
            for par in range(2):
                for h in range(4):
                    def lda(E, par=par, h=h):
                        p = pid_of(E, "p3")
                        base = (p // 2) * (8 * 16 * 16384) + (p % 2) * (8 * 16384) + (par + 2 * h) * (16 * 16384)
                        src = bass.AP(ag3_out.ap().tensor, base, [[128, 128], [128 * 128, 8], [1, 128]])
                        dst = att_all[:, :, :].rearrange("q (mm two) (h d) -> q mm two h d", two=2, h=4)[:, :, par, h, :]
                        return E.dma_start(out=dst, in_=src)
                    S.dmaf("act", lda, [t_ag3out], [t_attall], key="attg", group=True, wg="att")
            attT = [sbuf(p3a, "attT%d" % i, [128, 4, 128], BF16) for i in range(2)]
            xt3 = [sbuf(p3a, "xt3_%d" % i, [128, D]) for i in range(2)]
            xm = [sbuf(p3a, "xm%d" % i, [128, D]) for i in range(2)]
            xn3 = [sbuf(p3a, "xn3_%d" % i, [128, D]) for i in range(2)]
            h2t = [sbuf(p3a, "h2t%d" % i, [128, 8, 128], BF16) for i in range(2)]
            ss3 = sbuf(p3a, "ss3", [128, 16])
            t_att, t_attT, t_xt3, t_xm = [T(), T()], [T(), T()], [T(), T()], [T(), T()]
            t_xn3, t_h2t, t_sq3, t_ss3 = [T(), T()], [T(), T()], T(), T()
            pT = psum(p3a, "pT", [128, 1024], BF16)
            pm = [psum(p3a, "pm%d" % i, [128, 512]) for i in range(4)]
            ptr = [psum(p3a, "ptr%d" % i, [128, 512]) for i in range(2)]
            pha = psum(p3a, "pha", [128, 512])
            t_pT, t_pm, t_ptr, t_pha = T(), [T() for _ in range(4)], [T(), T()], T()
            order = [15] + list(range(15))
            def st_a(n_i):
                tt = order[n_i]
                b = n_i % 2
                S.dma("sp", xt3[b][:], I["x_own"][tt * 128:(tt + 1) * 128, :], (), [t_xt3[b]], key="xt3_%d" % b)
                if n_i < 12:
                    wup_piece(n_i, wus3, t_wus3, wupB)
                for h in range(4):
                    S.tr(pT[:, h * 128:(h + 1) * 128], att_all[:, tt, h * 128:(h + 1) * 128], identb[:],
                         [t_attall, t_const], [t_pT], wg=("pT", n_i))
                S.copy("act", attT[b][:].rearrange("p a b -> p (a b)"), pT[:, 0:512], [t_pT], [t_attT[b]])
                for half in range(2):
                    items = [(lru_out[:, cg, tt * 128:(tt + 1) * 128], wo[:, cg, half * 512:(half + 1) * 512])
                             for cg in range(4)]
                    items += [(attT[b][:, h, :], wo[:, 4 + h, half * 512:(half + 1) * 512]) for h in range(4)]
                    pk = (n_i % 2) * 2 + half
                    S.mm(pm[pk][:, :], items, t_lru + [t_attT[b], t_wo], [t_pm[pk]])
                    S.tt("dve", xm[b][:, half * 512:(half + 1) * 512], pm[pk][:, :],
                         xt3[b][:, half * 512:(half + 1) * 512], ALU.add, [t_pm[pk], t_xt3[b]], [t_xm[b]])
                S.dma("sp", xmid_scr.ap()[tt * 128:(tt + 1) * 128, :], xm[b][:], [t_xm[b]], [t_xmid[tt]], key="xmst%d" % b)

            def st_b(n_i):
                tt = order[n_i]
                b = n_i % 2
                S.act(xn3[b][:], xm[b][:], AF.Square, [t_xm[b]], [t_xn3[b], t_ss3], accum=ss3[:, tt:tt + 1])
                S.ts("dve", ss3[:, tt:tt + 1], ss3[:, tt:tt + 1], 1.0 / D, EPS, ALU.mult, ALU.add, [t_ss3], [t_ss3])
                S.act(ss3[:, tt:tt + 1], ss3[:, tt:tt + 1], AF.Ln, [t_ss3], [t_ss3])
                S.act(ss3[:, tt:tt + 1], ss3[:, tt:tt + 1], AF.Exp, [t_ss3], [t_ss3], scale=-0.5)
                S.ts("dve", xn3[b][:], xm[b][:], ss3[:, tt:tt + 1], None, ALU.mult, reads=[t_xm[b], t_ss3],
                     writes=[t_xn3[b]])
                for half in range(2):
                    for j in range(4):
                        kc = half * 4 + j
                        S.tr(ptr[half][:, j * 128:(j + 1) * 128], xn3[b][:, kc * 128:(kc + 1) * 128], ident[:],
                             [t_xn3[b], t_const], [t_ptr[half]])
                    for j in range(4):
                        kc = half * 4 + j
                        if j % 2 == 0:
                            S.ts("dve", h2t[b][:, kc, :], ptr[half][:, j * 128:(j + 1) * 128], gs2[:, kc:kc + 1],
                                 sh2[:, kc:kc + 1], ALU.mult, ALU.add, [t_ptr[half], t_mod], [t_h2t[b]])
                        else:
                            S.act(h2t[b][:, kc, :], ptr[half][:, j * 128:(j + 1) * 128], AF.Identity,
                                  [t_ptr[half], t_mod], [t_h2t[b]], bias=sh2[:, kc:kc + 1], scale=gs2[:, kc:kc + 1])
                S.dma("sp", h2_scr.ap()[:, :, tt * 128:(tt + 1) * 128].rearrange("k p t -> p k t"), h2t[b][:],
                      [t_h2t[b]], [t_h2scr[tt]], key="h2st%d" % b)
                if tt == 15:
                    S.copy("dve", h2last[:], h2t[b][:, :, 126:128], [t_h2t[b]], [t_h2last])
                if n_i == 13:
                    for c in range(24):
                        S.mm(pha[:, 2 * c:2 * c + 2], [(wupk(kc, c * 128, (c + 1) * 128), h2last[:, kc, :])
                                                      for kc in range(8)], t_wup + [t_h2last], [t_pha])
                    hst = sbuf(p3a, "hst", [128, 48])
                    t_hst = T()
                    S.copy("dve", hst[:], pha[:, 0:48], [t_pha], [t_hst])
                    S.dma("sp", ag4_in.ap(), hst[:], [t_hst], [t_ag4in], key="ag4st")
                    S.op("pool", lambda E: E.collective_compute("AllGather", ALU.bypass, replica_groups=RG,
                                                               ins=[ag4_in.ap().opt()], outs=[ag4_out.ap().opt()]),
                         [t_ag4in], [t_ag4out], dma="cc4", cc=True)

            for t in range(17):
                if t < 16:
                    st_a(t)
                if t >= 1:
                    st_b(t - 1)

            def ldh(E):
                p = pid_of(E, "p3")
                return E.dma_start(out=ahalo[:].rearrange("p a b -> p (a b)"),
                                   in_=ag4_out.ap()[ds(((p + 7) % 8) * 128, 128), :])
            S.dmaf("pool", ldh, [t_ag4out], t_ahalo, key="ahalo")
            S.ts("dve", ahalo[:].rearrange("p a b -> p (a b)"), ahalo[:].rearrange("p a b -> p (a b)"),
                 flags[:, 0:1], None, ALU.mult, reads=t_ahalo + [t_const], writes=t_ahalo)
            if debug:
                S.dma("sp", dbg_out("xmid", [TPC, D]), xmid_scr.ap(), t_xmid, [], key="dbg", group=True)
            S.flush()
        if stop_after == "3a":
            return finish(nc, es, S, out_d, dbg)

        with ExitStack() as p3b:
            wd = sbuf(p3b, "wd", [128, 24, D], BF16)
            t_wd = [T() for _ in range(24)]
            g2b = sbuf(p3b, "g2b", [128, D])
            gfb = sbuf(p3b, "gfb", [128, D])
            wsd = [sbuf(p3b, "wsd%d" % i, [128, D]) for i in range(2)]
            t_g2b, t_gfb, t_wsd = T(), T(), [T(), T()]
            S.dma("sp", g2b[:], mod_scr.ap()[0:1, 5120:6144].partition_broadcast(128), [t_modscr], [t_g2b], key="g2b")
            S.dma("sp", gfb[:], I["g_final"][0:1, :].partition_broadcast(128), (), [t_gfb], key="gfb")
            def load_wd(c):
                b = c % 2
                S.dma("act", wsd[b][:], I["w_down"][c * 128:(c + 1) * 128, :], (), [t_wsd[b]], key="wsd%d" % b)
                S.tt("pool", wd[:, c, :], wsd[b][:], g2b[:], ALU.mult, [t_wsd[b], t_g2b], [t_wd[c]])
            h2g = [sbuf(p3b, "h2g%d" % i, [128, 8, 256], BF16) for i in range(2)]
            xg = sbuf(p3b, "xg", [128, 2, D])
            t_h2g, t_xg = [T(), T()], [T(), T()]
            NE = 4
            yy = [sbuf(p3b, "yy%d" % i, [128, 256]) for i in range(NE)]
            u3 = [sbuf(p3b, "u3_%d" % i, [128, 256]) for i in range(NE)]
            actT = [sbuf(p3b, "actT%d" % i, [128, 256], BF16) for i in range(NE)]
            t_yy, t_u3, t_actT = [T() for _ in range(NE)], [T() for _ in range(NE)], [T() for _ in range(NE)]
            ssf = sbuf(p3b, "ssf", [128, 16])
            t_ssf = T()
            pff = [psum(p3b, "pff%d" % i, [128, 512]) for i in range(4)]
            pag = [psum(p3b, "pag%d" % i, [128, 512]) for i in range(4)]
            t_pff, t_pag = [T() for _ in range(4)], [T() for _ in range(4)]
            NITF = 8 * 24

            def f1(it):
                gix, c = divmod(it, 24)
                hb, e, k = gix % 2, it % NE, it % 4
                if c == 0:
                    t0 = gix * 256
                    S.dma("sp", h2g[hb][:], h2_scr.ap()[:, :, t0:t0 + 256].rearrange("k p t -> p k t"),
                          [t_h2scr[2 * gix], t_h2scr[2 * gix + 1]], [t_h2g[hb]], key="h2g%d" % hb)
                if it < 24:
                    load_wd(it)
                pa_, pg_, tp = pag[k][:, 0:256], pag[k][:, 256:512], t_pag[k]
                S.mm(pa_, [(wupk(kc, c * 128, (c + 1) * 128), h2g[hb][:, kc, :]) for kc in range(8)],
                     t_wup + [t_h2g[hb]], [tp])
                S.mm(pg_, [(wupk(kc, D_FF + c * 128, D_FF + (c + 1) * 128), h2g[hb][:, kc, :])
                           for kc in range(8)], t_wup + [t_h2g[hb]], [tp], wg=("ag", it))
                w = lambda j: cfw[:, c, j:j + 1]
                S.act(yy[e][:], pa_, AF.Identity, [tp, t_const], [t_yy[e]], bias=cfb[:, c:c + 1], scale=w(2))
                S.stt("dve", yy[e][:, 1:256], pag[k][:, 0:255], w(1), yy[e][:, 1:256], ALU.mult, ALU.add,
                      [tp, t_const], [t_yy[e]])
                S.stt("dve", yy[e][:, 2:256], pag[k][:, 0:254], w(0), yy[e][:, 2:256], ALU.mult, ALU.add,
                      [tp, t_const], [t_yy[e]])
                S.stt("dve", yy[e][:, 0:1], ahalo[:, c, 1:2], w(1), yy[e][:, 0:1], ALU.mult, ALU.add,
                      [t_ahalo[c], t_const], [t_yy[e]])
                S.stt("dve", yy[e][:, 0:2], ahalo[:, c, 0:2], w(0), yy[e][:, 0:2], ALU.mult, ALU.add,
                      [t_ahalo[c], t_const], [t_yy[e]])
                S.copy("act", ahalo[:, c, :], pag[k][:, 254:256], [tp, t_yy[e]], [t_ahalo[c]])

            def f2(it):
                gix, c = divmod(it, 24)
                e, k = it % NE, it % 4
                S.act(u3[e][:], yy[e][:], AF.Square, [t_yy[e]], [t_u3[e]])
                S.ts("pool", u3[e][:], u3[e][:], GELU_C, 1.0, ALU.mult, ALU.add, [t_u3[e]], [t_u3[e]])
                S.tt("pool", u3[e][:], u3[e][:], yy[e][:], ALU.mult, [t_u3[e], t_yy[e]], [t_u3[e]])
                S.act(u3[e][:], u3[e][:], AF.Sigmoid, [t_u3[e]], [t_u3[e]], scale=GELU_K)
                S.tt("dve", u3[e][:], u3[e][:], yy[e][:], ALU.mult, [t_u3[e], t_yy[e]], [t_u3[e]])
                S.tt("dve", actT[e][:], u3[e][:], pag[k][:, 256:512], ALU.mult, [t_u3[e], t_pag[k]], [t_actT[e]])

            def f3(it):
                gix, c = divmod(it, 24)
                e = it % NE

                def dn(E):
                    res = []
                    for tb in range(2):
                        for half in range(2):
                            res.append(E.matmul(pff[tb * 2 + half][:, :], lhsT=actT[e][:, tb * 128:(tb + 1) * 128],
                                                rhs=wd[:, c, half * 512:(half + 1) * 512],
                                                start=(c == 0), stop=(c == 23)))
                    return res
                S.op("pe", dn, [t_actT[e], t_wd[c]], t_pff)
                if c == 23:
                    ffin(gix)

            def ffin(gix):
                for tb in range(2):
                    tt = 2 * gix + tb
                    S.dma("sp", xg[:, tb, :], xmid_scr.ap()[tt * 128:(tt + 1) * 128, :], [t_xmid[tt]], [t_xg[tb]],
                          key="xg%d" % tb)
                    for half in range(2):
                        S.tt("dve", xg[:, tb, half * 512:(half + 1) * 512], pff[tb * 2 + half][:, :],
                             xg[:, tb, half * 512:(half + 1) * 512], ALU.add, [t_pff[tb * 2 + half], t_xg[tb]],
                             [t_xg[tb]])
                    S.act(wsd[0][:], xg[:, tb, :], AF.Square, [t_xg[tb]], [t_wsd[0], t_ssf], accum=ssf[:, tt:tt + 1])
                    S.ts("dve", ssf[:, tt:tt + 1], ssf[:, tt:tt + 1], 1.0 / D, EPS, ALU.mult, ALU.add, [t_ssf],
                         [t_ssf])
                    S.act(ssf[:, tt:tt + 1], ssf[:, tt:tt + 1], AF.Ln, [t_ssf], [t_ssf])
                    S.act(ssf[:, tt:tt + 1], ssf[:, tt:tt + 1], AF.Exp, [t_ssf], [t_ssf], scale=-0.5)
                    S.stt("dve", xg[:, tb, :], xg[:, tb, :], ssf[:, tt:tt + 1], gfb[:], ALU.mult, ALU.mult,
                          [t_xg[tb], t_ssf, t_gfb], [t_xg[tb]])
                    S.dma("sp", out_d[tt * 128:(tt + 1) * 128, :], xg[:, tb, :], [t_xg[tb]], [], key="out%d" % tb)

            for t in range(NITF + 3):
                if t < NITF:
                    f1(t)
                if 0 <= t - 1 < NITF:
                    f2(t - 1)
                if 0 <= t - 3 < NITF:
                    f3(t - 3)
            S.flush()
    return finish(nc, es, S, out_d, dbg)


def finish(nc, es, S, out_d, dbg):
    def fin(E):
        for k, sm in S.dsem.items():
            if S.dcnt[k] > 0:
                E.wait_ge(sm, S.dcnt[k])
        return None
    o = Op()
    o.eng, o.fn, o.dma, o.needed, o.ev, o.cc, o.deps, o.idx, o.tail = "sp", fin, None, False, None, False, [], S.nidx, None
    S.pending.append(o)
    S.flush()
    return nc, dbg


_CACHE = {}


def kernel(**inputs):
    maps = make_inmaps(inputs)
    if "nc" not in _CACHE:
        _CACHE["nc"] = build()[0]
    nc = _CACHE["nc"]
    res = run_bass_kernel_spmd(nc, maps, core_ids=list(range(NCORES)))
    out = np.concatenate([np.asarray(r["out"], np.float32) for r in res.results], axis=0)
    return out.reshape(1, S_TOT, D)
```

```python
import numpy as np
from contextlib import ExitStack
import concourse.bass as bass
import concourse.mybir as mybir
from concourse.bass import ds
from concourse.bass_utils import run_bass_kernel_spmd

F32 = mybir.dt.float32
BF16 = mybir.dt.bfloat16
ALU = mybir.AluOpType
AF = mybir.ActivationFunctionType

NCORES = 8
S_TOT = 16384
D = 1024
TPC = S_TOT // NCORES
NT = TPC // 128
D_IN = 2560
D_FF = 3072
EPS = 1e-6
NQB = 64
LAMBDA_INIT = 0.8 - 0.6
QOFF, KOFF, VOFF = 0, 4 * 128 * 2048, 2 * 4 * 128 * 2048
NB1 = VOFF + 4 * 128 * 16 * 130
GELU_K = 1.5957691216057308
GELU_C = 0.044715


class Tok:
    __slots__ = ("name", "ws", "wg", "gdeps", "r")

    def __init__(self, name=""):
        self.name = name
        self.ws = []
        self.wg = None
        self.gdeps = []
        self.r = {}


class Op:
    __slots__ = ("eng", "fn", "deps", "dma", "needed", "ev", "idx", "tail", "cc")


class Sched:
    ENGS = ("pe", "act", "dve", "pool", "sp")

    def __init__(self, nc, es):
        self.nc = nc
        self.es = es
        self.sem = {k: es.enter_context(nc.semaphore("s_" + k)) for k in self.ENGS}
        self.dsem = {}
        self.dcnt = {}
        self.dgroup = set()
        self.cnt = {k: 0 for k in self.ENGS}
        self.seen = {k: {} for k in self.ENGS}
        self.pending = []
        self.nidx = 0

    def op(self, eng, fn, reads=(), writes=(), dma=None, deps=(), cc=False, wg=None, group=False):
        o = Op()
        o.eng, o.fn, o.dma, o.needed, o.ev, o.cc = eng, fn, dma, False, None, cc
        o.idx = self.nidx
        o.tail = None
        self.nidx += 1
        if dma is not None and dma not in self.dsem:
            self.dsem[dma] = self.es.enter_context(self.nc.semaphore("d_" + str(dma)))
            self.dcnt[dma] = 0
        if group:
            self.dgroup.add(dma)
        d = {}

        def add(x):
            if x is None or x is o:
                return
            k = ("d", x.dma) if x.dma is not None else ("e", x.eng)
            if k not in d or d[k].idx < x.idx:
                d[k] = x

        for x in deps:
            add(x)
        for t in reads:
            for x in t.ws:
                add(x)
        for t in writes:
            if wg is not None and t.wg == wg:
                for x in t.gdeps:
                    add(x)
                for x in t.r.values():
                    add(x)
            else:
                gd = list(t.ws) + list(t.r.values())
                for x in gd:
                    add(x)
                t.gdeps = gd
        o.deps = list(d.values())
        for x in o.deps:
            x.needed = True
        mk = ("d", dma) if dma is not None else ("e", eng)
        for t in reads:
            t.r[mk] = o
        for t in writes:
            if wg is not None and t.wg == wg:
                t.ws.append(o)
            else:
                t.ws = [o]
                t.wg = wg
                t.r = {}
        self.pending.append(o)
        return o

    def flush(self):
        nc = self.nc
        last = {}
        for o in self.pending:
            if o.dma is None:
                last[o.eng] = o
        for o in last.values():
            o.needed = True
        for o in self.pending:
            o.tail = last
            if o.dma is not None:
                self.dcnt[o.dma] += (1 if o.cc else 16)
                o.ev = ("d", o.dma, self.dcnt[o.dma])
            elif o.needed:
                self.cnt[o.eng] += 1
                o.ev = ("e", o.eng, self.cnt[o.eng])
        for o in self.pending:
            if o.dma is not None and o.dma in self.dgroup:
                o.ev = ("d", o.dma, self.dcnt[o.dma])
        byeng = {k: [o for o in self.pending if o.eng == k] for k in self.ENGS}
        drain = sorted({o.dma for o in self.pending if o.dma is not None and not o.cc}, key=str)

        def emit2(E, ename):
            seen = self.seen[ename]
            for o in byeng[ename]:
                waits = []
                for dpo in o.deps:
                    if dpo.ev is None:
                        dpo = dpo.tail[dpo.eng]
                    kind, key, val = dpo.ev
                    if kind == "e" and key == "pe" and ename == "pe" and o.dma is None:
                        continue
                    sk = (kind, key)
                    if seen.get(sk, 0) >= val:
                        continue
                    seen[sk] = val
                    waits.append((self.sem[key] if kind == "e" else self.dsem[key], val))
                for sm, v in waits[1:]:
                    E.wait_ge(sm, v)
                ins = o.fn(E)
                if ins is None:
                    for sm, v in waits[:1]:
                        E.wait_ge(sm, v)
                    if o.ev is not None and o.dma is None:
                        E.nop().then_inc(self.sem[ename], 1)
                    continue
                if not isinstance(ins, (list, tuple)):
                    ins = [ins]
                if waits:
                    ins[0]._wait_ge(*waits[0])
                if o.dma is not None:
                    ins[-1].then_inc(self.dsem[o.dma], 1 if o.cc else 16)
                elif o.ev is not None:
                    ins[-1].then_inc(self.sem[ename], 1)

        with nc.Block() as block:
            @block.tensor
            def _(E):
                emit2(E, "pe")

            @block.scalar
            def _(E):
                emit2(E, "act")

            @block.vector
            def _(E):
                emit2(E, "dve")

            @block.gpsimd
            def _(E):
                emit2(E, "pool")

            @block.sync
            def _(E):
                emit2(E, "sp")
                for k in drain:
                    E.wait_ge(self.dsem[k], self.dcnt[k])
        self.pending = []

    def dma(self, eng, out, in_, reads=(), writes=(), key="misc", group=False, wg=None):
        return self.op(eng, lambda E: E.dma_start(out=out, in_=in_), reads, writes, dma=key, group=group, wg=wg)

    def dmaf(self, eng, fn, reads=(), writes=(), key="misc", group=False, wg=None):
        return self.op(eng, fn, reads, writes, dma=key, group=group, wg=wg)

    def act(self, out, in_, func, reads=(), writes=(), bias=None, scale=1.0, accum=None, wg=None):
        kw = {}
        if bias is not None:
            kw["bias"] = bias
        if accum is not None:
            kw["accum_out"] = accum
        return self.op("act", lambda E: E.activation(out=out, in_=in_, func=func, scale=scale, **kw),
                       reads, writes, wg=wg)

    def ts(self, eng, out, in0, s1, s2, op0, op1=None, reads=(), writes=(), accum=None, wg=None):
        kw = {}
        if op1 is not None:
            kw["op1"] = op1
        if accum is not None:
            kw["accum_out"] = accum
        return self.op(eng, lambda E: E.tensor_scalar(out=out, in0=in0, scalar1=s1, scalar2=s2, op0=op0, **kw),
                       reads, writes, wg=wg)

    def stt(self, eng, out, in0, scalar, in1, op0, op1, reads=(), writes=(), accum=None, wg=None):
        kw = {}
        if accum is not None:
            kw["accum_out"] = accum
        return self.op(eng, lambda E: E.scalar_tensor_tensor(out=out, in0=in0, scalar=scalar, in1=in1,
                                                             op0=op0, op1=op1, **kw), reads, writes, wg=wg)

    def tt(self, eng, out, in0, in1, op, reads=(), writes=(), wg=None):
        return self.op(eng, lambda E: E.tensor_tensor(out=out, in0=in0, in1=in1, op=op), reads, writes, wg=wg)

    def copy(self, eng, out, in_, reads=(), writes=(), wg=None):
        if eng == "act":
            return self.op(eng, lambda E: E.copy(out=out, in_=in_), reads, writes, wg=wg)
        return self.op(eng, lambda E: E.tensor_copy(out=out, in_=in_), reads, writes, wg=wg)

    def memset(self, eng, ap, val, writes=(), wg=None):
        return self.op(eng, lambda E: E.memset(ap, val), (), writes, wg=wg)

    def scan(self, eng, out, d0, d1, init, reads=(), writes=()):
        return self.op(eng, lambda E: E.tensor_tensor_scan(out=out, data0=d0, data1=d1, initial=init,
                                                           op0=ALU.mult, op1=ALU.add), reads, writes)

    def mm(self, out, items, reads=(), writes=(), start=True, stop=True, skip=False, wg=None):
        def fn(E):
            res = []
            n = len(items)
            for i, (l, r) in enumerate(items):
                kw = {}
                if skip:
                    kw["skip_group_check"] = True
                res.append(E.matmul(out, lhsT=l, rhs=r, start=(start and i == 0), stop=(stop and i == n - 1), **kw))
            return res
        return self.op("pe", fn, reads, writes, wg=wg)

    def tr(self, out, in_, ident, reads=(), writes=(), wg=None):
        return self.op("pe", lambda E: E.transpose(out=out, in_=in_, identity=ident), reads, writes, wg=wg)


def _colT(v, n):
    return np.ascontiguousarray(np.asarray(v, np.float32).reshape(n, 128).T)


def _t5_bucket(rel):
    n = np.maximum(rel, 0)
    nf = np.maximum(n, 1).astype(np.float32)
    large = 16 + (np.log(nf / 16) / np.log(128 / 16) * 16).astype(np.int32)
    large = np.minimum(large, 31)
    return np.where(n < 16, n, large)


INPUT_SPECS = [
    ("x_own", [TPC, D]), ("x_prev", [128, D]), ("flags", [128, 4]), ("sel_prev", [128, 8]),
    ("c_in", [1, D]), ("w_adaT", [128, 48, D]), ("b_adaT", [128, 48]), ("g1T", [128, 8]), ("g2T", [128, 8]),
    ("g_final", [1, D]), ("w_in", [D, D_IN]), ("clw", [128, 4, 4]), ("clb", [128, 4]),
    ("waBD", [128, 4, 128]), ("wxBD", [128, 4, 128]), ("lbaT", [128, 4]), ("lbxT", [128, 4]), ("lamT", [128, 4]),
    ("lamq", [1, 256]), ("g_subln", [1, 128]), ("w_out", [D, D]), ("w_up", [D, 2 * D_FF]),
    ("cfw", [128, 24, 3]), ("cfb", [128, 24]), ("w_down", [D_FF, D]),
    ("bias_g", [128, 3, 128]), ("bias_m", [128, 3, 128]), ("bfar", [128, 1]), ("ident", [128, 128]),
]


def make_inmaps(inp):
    f = lambda a: np.ascontiguousarray(np.asarray(a, np.float32))
    x = f(inp["x"])[0]
    w_ada = f(inp["w_ada"])[0]
    w_adaT = np.ascontiguousarray(w_ada.T.reshape(48, 128, D).transpose(1, 0, 2))
    wa, wx = f(inp["lru_wa"])[0], f(inp["lru_wx"])[0]

    def bd(w):
        o = np.zeros((128, 4, 128), np.float32)
        for g in range(4):
            o[0:64, g, 0:64] = w[2 * g]
            o[64:128, g, 64:128] = w[2 * g + 1]
        return o
    clw = np.ascontiguousarray(f(inp["conv_lru_w"])[0].reshape(4, 4, 128).transpose(2, 1, 0))
    cfw = np.ascontiguousarray(f(inp["conv_ffn_w"])[0].reshape(3, 24, 128).transpose(2, 1, 0))
    rel_bias = f(inp["rel_bias"])
    common = {
        "c_in": f(inp["c"]), "w_adaT": w_adaT, "b_adaT": _colT(f(inp["b_ada"])[0], 48),
        "g1T": _colT(f(inp["g_norm1"])[0], 8), "g2T": _colT(f(inp["g_norm2"])[0], 8),
        "g_final": f(inp["g_final"]).reshape(1, D), "w_in": f(inp["w_in"])[0],
        "clw": clw, "clb": _colT(f(inp["conv_lru_b"])[0], 4), "waBD": bd(wa), "wxBD": bd(wx),
        "lbaT": _colT(f(inp["lru_ba"])[0], 4), "lbxT": _colT(f(inp["lru_bx"])[0], 4),
        "lamT": _colT(f(inp["lru_lambda"])[0], 4),
        "lamq": np.concatenate([f(inp["lam_q1"]), f(inp["lam_k1"]), f(inp["lam_q2"]), f(inp["lam_k2"])], 0).reshape(1, 256),
        "g_subln": f(inp["g_subln"]).reshape(1, 128), "w_out": f(inp["w_out"])[0], "w_up": f(inp["w_up"])[0],
        "cfw": cfw, "cfb": _colT(f(inp["conv_ffn_b"])[0], 24), "w_down": f(inp["w_down"])[0],
        "ident": np.eye(128, dtype=np.float32),
    }
    kk = np.arange(128)[:, None]
    qq = np.arange(128)[None, :]
    maps = []
    for r in range(NCORES):
        h, z = r // 2, r % 2
        m = dict(common)
        m["x_own"] = np.ascontiguousarray(x[r * TPC:(r + 1) * TPC])
        m["x_prev"] = np.ascontiguousarray(x[r * TPC - 128:r * TPC]) if r > 0 else np.zeros((128, D), np.float32)
        fl = np.zeros((128, 4), np.float32)
        fl[:, 0] = 1.0 if r > 0 else 0.0
        m["flags"] = fl
        sp = np.zeros((128, 8), np.float32)
        if r > 0:
            sp[:, r - 1] = 1.0
        m["sel_prev"] = sp
        bg = np.zeros((128, 3, 128), np.float32)
        bm = np.zeros((128, 3, 128), np.float32)
        for t in range(3):
            dd = z + 1 - t
            rel = dd * 128 + qq - kk
            bg[:, t, :] = rel_bias[_t5_bucket(rel), h]
            bm[:, t, :] = np.where(rel >= 0, 0.0, -1e30)
        m["bias_g"], m["bias_m"] = bg, bm
        m["bfar"] = np.ascontiguousarray(np.broadcast_to(rel_bias[31, h], (128, 1))).astype(np.float32)
        maps.append(m)
    return maps


def build(debug=False, stop_after=None):
    nc = bass.Bass("TRN2", target_bir_lowering=False)
    es = ExitStack()
    I = {n: nc.dram_tensor(n, list(s), F32, kind="ExternalInput").ap() for n, s in INPUT_SPECS}
    out_d = nc.dram_tensor("out", [TPC, D], F32, kind="ExternalOutput").ap()
    dbg = {}

    def dbg_out(name, shape, dt=F32):
        dbg[name] = nc.dram_tensor("dbg_" + name, list(shape), dt, kind="ExternalOutput").ap()
        return dbg[name]

    ag1_in = nc.dram_tensor("ag1_in", [128, NB1 // 128], BF16)
    ag1_out = nc.dram_tensor("ag1_out", [1024, NB1 // 128], BF16)
    ag2_in = nc.dram_tensor("ag2_in", [128, 8], F32)
    ag2_out = nc.dram_tensor("ag2_out", [1024, 8], F32)
    ag3_in = [nc.dram_tensor("ag3_in%d" % i, [16 * 128, 128], BF16) for i in range(4)]
    ag3_out = nc.dram_tensor("ag3_out", [8 * NQB * 128, 128], BF16)
    ag4_in = nc.dram_tensor("ag4_in", [128, 48], F32)
    ag4_out = nc.dram_tensor("ag4_out", [1024, 48], F32)
    mod_scr = nc.dram_tensor("mod_scr", [1, 48 * 128], F32)
    xmid_scr = nc.dram_tensor("xmid_scr", [TPC, D], F32)
    h2_scr = nc.dram_tensor("h2_scr", [8, 128, TPC], BF16)
    a1f = ag1_in.ap().rearrange("a b -> (a b)")
    g1f = ag1_out.ap().rearrange("a b -> (a b)")
    RG = [list(range(NCORES))]

    S = Sched(nc, es)
    T = Tok
    pidc = {}

    def pid_of(E, name):
        k = (name, S.nidx // 10 ** 9, id(E))
        if k not in pidc:
            pidc[k] = E.partition_id()
        return pidc[k]

    def sbuf(st, name, shape, dt=F32):
        return st.enter_context(nc.sbuf_tensor("sb_" + name, list(shape), dt))

    def psum(st, name, shape, dt=F32):
        return st.enter_context(nc.psum_tensor("ps_" + name, list(shape), dt))

    t_ag1in, t_ag1out, t_ag2in, t_ag2out = T(), T(), T(), T()
    t_ag3in, t_ag3out, t_ag4in, t_ag4out = [T() for _ in range(4)], T(), T(), T()
    t_modscr, t_xmid, t_h2scr = T(), [T() for _ in range(NT)], [T() for _ in range(NT)]

    ident = sbuf(es, "ident", [128, 128])
    identb = sbuf(es, "identb", [128, 128], BF16)
    modT = sbuf(es, "modT", [128, 48])
    gs1 = sbuf(es, "gs1", [128, 8])
    gs2 = sbuf(es, "gs2", [128, 8])
    cpar = sbuf(es, "cpar", [128, 64])
    flags = sbuf(es, "flags", [128, 4])
    selp = sbuf(es, "selp", [128, 8])
    lam_t = sbuf(es, "lam_t", [128, 4])
    lru_out = sbuf(es, "lru_out", [128, 4, TPC], BF16)
    t_const = T()
    t_mod = T()
    t_lru = [T() for _ in range(4)]
    CLW, CLB, LBA, LBX, LAM, NSP = 0, 16, 20, 24, 28, 32
    cfw = sbuf(es, "cfw", [128, 24, 3])
    cfb = sbuf(es, "cfb", [128, 24])

    with ExitStack() as p0:
        cb = sbuf(p0, "cb", [128, D])
        wst = [sbuf(p0, "wst%d" % i, [128, 4, D]) for i in range(2)]
        junk = {"dve": sbuf(p0, "junk_d", [128, D]), "pool": sbuf(p0, "junk_p", [128, D])}
        lq = sbuf(p0, "lq", [128, 4, 64])
        ltmp = sbuf(p0, "ltmp", [128, 8])
        modsb = sbuf(p0, "modsb", [48, 128])
        pt0 = psum(p0, "pt0", [128, 512])
        t_cb, t_wst, t_junk = T(), [T(), T()], {"dve": T(), "pool": T()}
        t_lq, t_ltmp, t_modsb, t_pt0 = T(), T(), T(), T()
        tmodc = [T() for _ in range(48)]

        S.dma("sp", ident[:], I["ident"][:, :], (), [t_const], key="c0", group=True, wg="c0")
        S.dma("sp", flags[:], I["flags"][:, :], (), [t_const], key="c0", group=True, wg="c0")
        S.dma("sp", selp[:], I["sel_prev"][:, :], (), [t_const], key="c0", group=True, wg="c0")
        S.dma("sp", cpar[:, CLW:CLW + 16], I["clw"].rearrange("p g j -> p (g j)"), (), [t_const], key="c0", group=True, wg="c0")
        S.dma("sp", cpar[:, CLB:CLB + 4], I["clb"][:, :], (), [t_const], key="c0", group=True, wg="c0")
        S.dma("sp", cpar[:, LBA:LBA + 4], I["lbaT"][:, :], (), [t_const], key="c0", group=True, wg="c0")
        S.dma("sp", cpar[:, LBX:LBX + 4], I["lbxT"][:, :], (), [t_const], key="c0", group=True, wg="c0")
        S.dma("sp", cpar[:, LAM:LAM + 4], I["lamT"][:, :], (), [t_const], key="c0", group=True, wg="c0")
        S.dma("sp", cfw[:], I["cfw"][:, :, :], (), [t_const], key="c0", group=True, wg="c0")
        S.dma("sp", cfb[:], I["cfb"][:, :], (), [t_const], key="c0", group=True, wg="c0")
        S.dma("sp", gs1[:], I["g1T"][:, :], (), [t_mod], key="c0", group=True, wg="c0")
        S.dma("sp", gs2[:], I["g2T"][:, :], (), [t_mod], key="c0", group=True, wg="c0")
        badd = sbuf(p0, "badd", [128, 48])
        t_badd = T()
        S.dma("sp", badd[:], I["b_adaT"][:, :], (), [t_badd], key="c0", group=True)
        S.copy("dve", identb[:], ident[:], [t_const], [t_const])
        S.dma("sp", cb[:], I["c_in"][0:1, :].partition_broadcast(128), (), [t_cb], key="c0", group=True)
        S.act(cb[:], cb[:], AF.Silu, [t_cb], [t_cb])
        S.dma("sp", lq[:].rearrange("p a b -> p (a b)"),
              I["lamq"][0:1, :].partition_broadcast(128), (), [t_lq], key="c0", group=True)
        for j in range(2):
            S.stt("dve", junk["dve"][:, 0:64], lq[:, 2 * j, :], 1.0, lq[:, 2 * j + 1, :], ALU.mult, ALU.mult,
                  [t_lq], [t_junk["dve"], t_ltmp], accum=ltmp[:, j:j + 1])
        S.act(ltmp[:, 2:4], ltmp[:, 0:2], AF.Exp, [t_ltmp], [t_ltmp])
        S.stt("dve", lam_t[:, 0:1], ltmp[:, 2:3], LAMBDA_INIT, ltmp[:, 3:4], ALU.add, ALU.subtract,
              [t_ltmp], [t_const])
        S.act(cpar[:, NSP:NSP + 4], cpar[:, LAM:LAM + 4], AF.Exp, [t_const], [t_const], scale=-1.0)
        S.act(cpar[:, NSP:NSP + 4], cpar[:, NSP:NSP + 4], AF.Ln, [t_const], [t_const], bias=1.0)
        S.ts("dve", cpar[:, NSP:NSP + 4], cpar[:, NSP:NSP + 4], -8.0, None, ALU.mult, reads=[t_const],
             writes=[t_const])
        for j in range(12):
            b = j % 2
            S.dma("sp", wst[b][:], I["w_adaT"][:, 4 * j:4 * j + 4, :], (), [t_wst[b]], key="wst%d" % b)
            for jj in range(4):
                e = "dve"
                ch = 4 * j + jj
                S.stt(e, junk[e][:], wst[b][:, jj, :], 1.0, cb[:], ALU.mult, ALU.mult,
                      [t_wst[b], t_cb], [t_junk[e], tmodc[ch]], accum=modT[:, ch:ch + 1])
        S.tt("dve", modT[:], modT[:], badd[:], ALU.add, [t_badd] + tmodc, [t_mod])
        S.stt("dve", gs1[:], modT[:, 8:16], 1.0, gs1[:], ALU.add, ALU.mult, [t_mod], [t_mod])
        S.stt("dve", gs2[:], modT[:, 32:40], 1.0, gs2[:], ALU.add, ALU.mult, [t_mod], [t_mod])
        S.tr(pt0[0:48, 0:128], modT[:, :], ident[:], [t_mod, t_const], [t_pt0])
        S.copy("dve", modsb[:], pt0[0:48, 0:128], [t_pt0], [t_modsb])
        S.dma("sp", mod_scr.ap().rearrange("o (a b) -> (o a) b", a=48), modsb[:], [t_modsb], [t_modscr], key="modst")
        if debug:
            S.dma("sp", dbg_out("modT", [128, 48]), modT[:], [t_mod], [], key="dbg", group=True)
        S.flush()
    sh1 = modT[:, 0:8]
    sh2 = modT[:, 24:32]
    if stop_after == "0":
        return finish(nc, es, S, out_d, dbg)

    with ExitStack() as p1:
        hT = sbuf(p1, "hT", [128, 8, 17 * 128], BF16)
        t_hT = [T() for _ in range(17)]
        wlru = sbuf(p1, "wlru", [128, 8, 1024], BF16)
        t_wlru = T()

        with ExitStack() as pa:
            wqkv = sbuf(pa, "wqkv", [128, 8, 1536], BF16)
            t_wqkv = T()
            wis = [sbuf(pa, "wis%d" % i, [128, D_IN]) for i in range(4)]
            t_wis = [T() for _ in range(4)]

            def load_w(kc):
                S.dma("act", wis[kc % 4][:], I["w_in"][kc * 128:(kc + 1) * 128, :], (), [t_wis[kc % 4]],
                      key="wis%d" % (kc % 4))
            for kc in range(4):
                load_w(kc)
            xt = [sbuf(pa, "xt%d" % i, [128, D]) for i in range(2)]
            xn = [sbuf(pa, "xn%d" % i, [128, D]) for i in range(2)]
            sq = sbuf(pa, "sq", [128, D])
            ss = sbuf(pa, "ss", [128, 40])
            stq = [sbuf(pa, "stq%d" % i, [128, 512], BF16) for i in range(3)]
            vst = [sbuf(pa, "vst%d" % i, [128, 4, 4, 130], BF16) for i in range(2)]
            t_xt, t_xn, t_sq, t_ss = [T(), T()], [T(), T()], T(), T()
            t_stq, t_vst = [T() for _ in range(3)], [T(), T()]
            pb = [psum(pa, "pa%d" % i, [128, 512]) for i in range(8)]
            t_pb = [T() for _ in range(8)]
            for i in range(2):
                S.memset("pool", vst[i][:, :, :, 128:129], 1.0, [t_vst[i]])
                S.memset("pool", vst[i][:, :, :, 129:130], 0.0, [t_vst[i]])
            Qv = a1f[QOFF:QOFF + 4 * 128 * 2048].rearrange("(h z p t) -> h z p t", h=4, z=2, p=128)
            Kv = a1f[KOFF:KOFF + 4 * 128 * 2048].rearrange("(h p t) -> h p t", h=4, p=128)
            Vv = a1f[VOFF:VOFF + 4 * 128 * 16 * 130].rearrange("(h p b c) -> p h b c", h=4, p=128, b=16)
            nq = 0

            def cast_w(kc):
                S.copy("dve", wqkv[:, kc, :], wis[kc % 4][:, 0:1536], [t_wis[kc % 4]], [t_wqkv], wg="w")
                S.copy("act", wlru[:, kc, :], wis[kc % 4][:, 1536:2560], [t_wis[kc % 4]], [t_wlru], wg="w")
                if kc + 4 < 8:
                    load_w(kc + 4)
            for ti in range(17):
                if 1 <= ti <= 4:
                    cast_w(2 * (ti - 1))
                    cast_w(2 * (ti - 1) + 1)
                b = ti % 2
                src = I["x_prev"][:, :] if ti == 0 else I["x_own"][(ti - 1) * 128:ti * 128, :]
                S.dma("sp", xt[b][:], src, (), [t_xt[b]], key="xt%d" % b)
                S.act(sq[:], xt[b][:], AF.Square, [t_xt[b]], [t_sq, t_ss], accum=ss[:, ti:ti + 1])
                S.ts("dve", ss[:, ti:ti + 1], ss[:, ti:ti + 1], 1.0 / D, EPS, ALU.mult, ALU.add, [t_ss], [t_ss])
                S.act(ss[:, ti:ti + 1], ss[:, ti:ti + 1], AF.Ln, [t_ss], [t_ss])
                S.act(ss[:, ti:ti + 1], ss[:, ti:ti + 1], AF.Exp, [t_ss], [t_ss], scale=-0.5)
                S.ts("dve", xn[b][:], xt[b][:], ss[:, ti:ti + 1], None, ALU.mult, reads=[t_xt[b], t_ss],
                     writes=[t_xn[b]])
                for half in range(2):
                    bank = (ti % 2) * 2 + half
                    for j in range(4):
                        kc = half * 4 + j
                        S.tr(pb[bank][:, j * 128:(j + 1) * 128], xn[b][:, kc * 128:(kc + 1) * 128], ident[:],
                             [t_xn[b], t_const], [t_pb[bank]])
                    for j in range(4):
                        kc = half * 4 + j
                        o_ap = hT[:, kc, ti * 128:(ti + 1) * 128]
                        i_ap = pb[bank][:, j * 128:(j + 1) * 128]
                        if j % 2 == 0:
                            S.ts("dve", o_ap, i_ap, gs1[:, kc:kc + 1], sh1[:, kc:kc + 1], ALU.mult, ALU.add,
                                 [t_pb[bank], t_mod], [t_hT[ti]])
                        else:
                            S.act(o_ap, i_ap, AF.Identity, [t_pb[bank], t_mod], [t_hT[ti]],
                                  bias=sh1[:, kc:kc + 1], scale=gs1[:, kc:kc + 1])
                if ti >= 1 and (ti - 1) % 4 == 3:
                    g = (ti - 1) // 4
                    c0 = 128 + g * 512
                    rd = [t_hT[1 + 4 * g + k] for k in range(4)] + [t_wqkv]
                    for cbk in range(8):
                        bank = 4 + (nq % 4)
                        nq += 1
                        S.mm(pb[bank][:, :], [(wqkv[:, kc, cbk * 128:(cbk + 1) * 128], hT[:, kc, c0:c0 + 512])
                                              for kc in range(8)], rd, [t_pb[bank]])
                        sb_i = nq % 3
                        S.copy("act" if cbk % 2 == 0 else "dve", stq[sb_i][:], pb[bank][:, :], [t_pb[bank]],
                               [t_stq[sb_i]])
                        if cbk < 4:
                            for z in range(2):
                                S.dma("sp", Qv[cbk, z, :, 2 * g * 128:(2 * g + 2) * 128].rearrange("p (a t) -> p a t", a=2),
                                      stq[sb_i][:].rearrange("p (a z t) -> p a z t", a=2, z=2)[:, :, z, :],
                                      [t_stq[sb_i]], [t_ag1in], key="stq%d_%d" % (sb_i, z), wg="ag1")
                        else:
                            S.dma("sp", Kv[cbk % 4, :, g * 512:(g + 1) * 512], stq[sb_i][:], [t_stq[sb_i]], [t_ag1in],
                                  key="stq%d" % sb_i, wg="ag1")
                    vb = g % 2
                    for k in range(4):
                        bank = 4 + (nq % 4)
                        nq += 1
                        tcol = c0 + k * 128
                        S.mm(pb[bank][:, :], [(hT[:, kc, tcol:tcol + 128], wqkv[:, kc, 1024:1536])
                                              for kc in range(8)], rd, [t_pb[bank]])
                        S.copy("act" if k % 2 == 0 else "dve", vst[vb][:, :, k, 0:128],
                               pb[bank][:, :].rearrange("p (h c) -> p h c", h=4), [t_pb[bank]], [t_vst[vb]])
                    S.dma("sp", Vv[:, :, 4 * g:4 * g + 4, :], vst[vb][:], [t_vst[vb]], [t_ag1in], key="vst%d" % vb, wg="ag1")
            S.op("pool", lambda E: E.collective_compute("AllGather", ALU.bypass, replica_groups=RG,
                                                       ins=[ag1_in.ap().opt()], outs=[ag1_out.ap().opt()]),
                 [t_ag1in], [t_ag1out], dma="cc1", cc=True)
            if debug:
                S.dma("sp", dbg_out("hT", [128, 8, 17 * 128], BF16), hT[:], t_hT, [], key="dbg", group=True)
                S.dma("sp", dbg_out("ag1in", [128, NB1 // 128], BF16), ag1_in.ap(), [t_ag1in], [], key="dbg", group=True)
            S.flush()
        if stop_after == "A":
            return finish(nc, es, S, out_d, dbg)

        with ExitStack() as pbo:
            hloc = sbuf(pbo, "hloc", [128, 4, TPC])
            pcum = sbuf(pbo, "pcum", [128, 4, TPC])
            t_hloc = [[T() for _ in range(4)] for _ in range(4)]
            with ExitStack() as pbs:
                wab = sbuf(pbs, "wab", [128, 4, 128], BF16)
                wxb = sbuf(pbs, "wxb", [128, 4, 128], BF16)
                t_wab = T()
                wabf = sbuf(pbs, "wabf", [128, 2, 4, 128])
                t_wabf = T()
                S.dma("sp", wabf[:, 0], I["waBD"][:, :, :], (), [t_wabf], key="wab", group=True, wg="w")
                S.dma("sp", wabf[:, 1], I["wxBD"][:, :, :], (), [t_wabf], key="wab", group=True, wg="w")
                S.copy("dve", wab[:], wabf[:, 0], [t_wabf], [t_wab], wg="w")
                S.copy("dve", wxb[:], wabf[:, 1], [t_wabf], [t_wab], wg="w")
                zeros = sbuf(pbs, "zeros", [128, 512])
                t_zero = T()
                S.memset("pool", zeros[:], 0.0, [t_zero])
                NB = 3
                xrs = [sbuf(pbs, "xrs%d" % i, [128, 515]) for i in range(NB)]
                xc = [sbuf(pbs, "xc%d" % i, [128, 512]) for i in range(NB)]
                xcb = [sbuf(pbs, "xcb%d" % i, [128, 512], BF16) for i in range(NB)]
                rr = [sbuf(pbs, "rr%d" % i, [128, 512]) for i in range(NB)]
                ii = [sbuf(pbs, "ii%d" % i, [128, 512]) for i in range(NB)]
                aa = [sbuf(pbs, "aa%d" % i, [128, 512]) for i in range(NB)]
                uu = [sbuf(pbs, "uu%d" % i, [128, 512]) for i in range(NB)]
                gg = [sbuf(pbs, "gg%d" % i, [128, 512]) for i in range(NB)]
                t_xrs, t_xc, t_xcb = [T() for _ in range(NB)], [T() for _ in range(NB)], [T() for _ in range(NB)]
                t_rr, t_ii, t_aa = [T() for _ in range(NB)], [T() for _ in range(NB)], [T() for _ in range(NB)]
                t_uu, t_gg = [T() for _ in range(NB)], [T() for _ in range(NB)]
                xtail = sbuf(pbs, "xtail", [128, 4, 3])
                st_h = sbuf(pbs, "st_h", [128, 4])
                st_p = sbuf(pbs, "st_p", [128, 4])
                t_xtail, t_st = [T() for _ in range(4)], [T() for _ in range(4)]
                pq = [psum(pbs, "pq%d" % i, [128, 512]) for i in range(8)]
                t_pq = [T() for _ in range(8)]
                S.memset("dve", st_h[:], 0.0, t_st)
                S.memset("dve", st_p[:], 1.0, t_st)

                def bset(it):
                    k = (it % 2) * 4
                    return pq[k:k + 4], t_pq[k:k + 4]

                def b1(it):
                    g, cg = divmod(it, 4)
                    b = it % NB
                    (PX, PY, PR, PI), (tPX, tPY, tPR, tPI) = bset(it)
                    c0 = 128 + g * 512
                    rd = [t_hT[1 + 4 * g + k] for k in range(4)] + [t_wlru]
                    S.mm(PX[:, :], [(wlru[:, kc, cg * 128:(cg + 1) * 128], hT[:, kc, c0:c0 + 512])
                                    for kc in range(8)], rd, [tPX])
                    S.mm(PY[:, :], [(wlru[:, kc, 512 + cg * 128:512 + (cg + 1) * 128], hT[:, kc, c0:c0 + 512])
                                    for kc in range(8)], rd, [tPY])
                    if g == 0:
                        S.mm(PR[:, 0:3], [(wlru[:, kc, cg * 128:(cg + 1) * 128], hT[:, kc, 125:128])
                                          for kc in range(8)], [t_hT[0], t_wlru], [tPR])
                        S.ts("dve", xrs[b][:, 0:3], PR[:, 0:3], flags[:, 0:1], None, ALU.mult,
                             reads=[tPR, t_const], writes=[t_xrs[b]])
                    else:
                        S.copy("pool", xrs[b][:, 0:3], xtail[:, cg, :], [t_xtail[cg]], [t_xrs[b]])
                    S.copy("act", xrs[b][:, 3:515], PX[:, :], [tPX], [t_xrs[b]])
                    S.copy("pool", xtail[:, cg, :], xrs[b][:, 512:515], [t_xrs[b]], [t_xtail[cg]])
                    w = lambda j: cpar[:, CLW + cg * 4 + j:CLW + cg * 4 + j + 1]
                    S.act(xc[b][:], xrs[b][:, 3:515], AF.Identity, [t_xrs[b], t_const], [t_xc[b]],
                          bias=cpar[:, CLB + cg:CLB + cg + 1], scale=w(3))
                    for j in range(3):
                        S.stt("dve", xc[b][:], xrs[b][:, j:j + 512], w(j), xc[b][:],
                              ALU.mult, ALU.add, [t_xrs[b], t_const], [t_xc[b]])
                    S.copy("pool", xcb[b][:], xc[b][:], [t_xc[b]], [t_xcb[b]])

                def b2(it):
                    g, cg = divmod(it, 4)
                    b = it % NB
                    (PX, PY, PR, PI), (tPX, tPY, tPR, tPI) = bset(it)
                    S.mm(PR[:, :], [(wab[:, cg, :], xcb[b][:])], [t_wab, t_xcb[b]], [tPR])
                    S.mm(PI[:, :], [(wxb[:, cg, :], xcb[b][:])], [t_wab, t_xcb[b]], [tPI])
                    S.act(rr[b][:], PR[:, :], AF.Sigmoid, [tPR, t_const], [t_rr[b]],
                          bias=cpar[:, LBA + cg:LBA + cg + 1])
                    S.act(ii[b][:], PI[:, :], AF.Sigmoid, [tPI, t_const], [t_ii[b]],
                          bias=cpar[:, LBX + cg:LBX + cg + 1])
                    S.act(uu[b][:], PY[:, :], AF.Square, [tPY], [t_uu[b]])
                    S.ts("dve", uu[b][:], uu[b][:], GELU_C, 1.0, ALU.mult, ALU.add, [t_uu[b]], [t_uu[b]])
                    S.tt("dve", uu[b][:], uu[b][:], PY[:, :], ALU.mult, [t_uu[b], tPY], [t_uu[b]])
                    S.act(uu[b][:], uu[b][:], AF.Sigmoid, [t_uu[b]], [t_uu[b]], scale=GELU_K)
                    S.tt("dve", gg[b][:], uu[b][:], PY[:, :], ALU.mult, [t_uu[b], tPY], [t_gg[b]])
                    S.act(aa[b][:], rr[b][:], AF.Exp, [t_rr[b], t_const], [t_aa[b]],
                          scale=cpar[:, NSP + cg:NSP + cg + 1])

                def b3(it):
                    g, cg = divmod(it, 4)
                    b = it % NB
                    om_, t_om_ = rr[b], t_rr[b]
                    S.tt("pool", om_[:], aa[b][:], aa[b][:], ALU.mult, [t_aa[b]], [t_om_])
                    S.ts("dve", om_[:], om_[:], -1.0, 1.0, ALU.mult, ALU.add, [t_om_], [t_om_])
                    S.ts("dve", om_[:], om_[:], 0.0, None, ALU.max, reads=[t_om_], writes=[t_om_])
                    S.act(om_[:], om_[:], AF.Ln, [t_om_], [t_om_])
                    S.act(om_[:], om_[:], AF.Exp, [t_om_], [t_om_], scale=0.5)
                    S.tt("pool", ii[b][:], ii[b][:], xc[b][:], ALU.mult, [t_ii[b], t_xc[b]], [t_ii[b]])
                    S.tt("dve", ii[b][:], ii[b][:], om_[:], ALU.mult, [t_ii[b], t_om_], [t_ii[b]])
                    hl = hloc[:, cg, g * 512:(g + 1) * 512]
                    pc = pcum[:, cg, g * 512:(g + 1) * 512]
                    S.scan("dve", hl, aa[b][:], ii[b][:], st_h[:, cg:cg + 1], [t_aa[b], t_ii[b], t_st[cg]],
                           [t_hloc[cg][g]])
                    S.scan("dve", pc, aa[b][:], zeros[:], st_p[:, cg:cg + 1], [t_aa[b], t_zero, t_st[cg]],
                           [t_hloc[cg][g]])
                    S.copy("pool", st_h[:, cg:cg + 1], hloc[:, cg, g * 512 + 511:g * 512 + 512],
                           [t_hloc[cg][g]], [t_st[cg]])
                    S.copy("pool", st_p[:, cg:cg + 1], pcum[:, cg, g * 512 + 511:g * 512 + 512],
                           [t_hloc[cg][g]], [t_st[cg]])
                    S.tt("pool", hl, hl, gg[b][:], ALU.mult, [t_hloc[cg][g], t_gg[b], t_st[cg]],
                         [t_hloc[cg][g]])
                    S.tt("dve", pc, pc, gg[b][:], ALU.mult, [t_hloc[cg][g], t_gg[b], t_st[cg]],
                         [t_hloc[cg][g]])

                for t in range(16 + 2):
                    if t < 16:
                        b1(t)
                    if 0 <= t - 1 < 16:
                        b2(t - 1)
                    if 0 <= t - 2 < 16:
                        b3(t - 2)
                stg = sbuf(pbs, "stg", [128, 8])
                t_stg = T()
                S.copy("dve", stg[:, 0:4], st_p[:], t_st, [t_stg])
                S.copy("dve", stg[:, 4:8], st_h[:], t_st, [t_stg])
                S.dma("sp", ag2_in.ap(), stg[:], [t_stg], [t_ag2in], key="ag2st")
                S.op("pool", lambda E: E.collective_compute("AllGather", ALU.bypass, replica_groups=RG,
                                                           ins=[ag2_in.ap().opt()], outs=[ag2_out.ap().opt()]),
                     [t_ag2in], [t_ag2out], dma="cc2", cc=True)
                if debug:
                    S.dma("sp", dbg_out("hlocG", [128, 4, TPC]), hloc[:], [x for y in t_hloc for x in y], [], key="dbg", group=True)
                S.flush()
            with ExitStack() as pf:
                car = sbuf(pf, "car", [128, 8, 8])
                pre = sbuf(pf, "pre", [128, 4, 8])
                cin = sbuf(pf, "cin", [128, 4])
                jk = sbuf(pf, "jk", [128, 8])
                t_car, t_pre, t_cin, t_jk = T(), T(), T(), T()
                S.dma("sp", car[:], ag2_out.ap().rearrange("(r p) c -> p r c", r=8), [t_ag2out], [t_car], key="car")
                for cg in range(4):
                    S.scan("dve", pre[:, cg, :], car[:, :, cg], car[:, :, 4 + cg], 0.0, [t_car], [t_pre])
                    S.stt("dve", jk[:], pre[:, cg, :], 1.0, selp[:], ALU.mult, ALU.mult, [t_pre, t_const],
                          [t_jk, t_cin], accum=cin[:, cg:cg + 1])
                for cg in range(4):
                    for hf in range(2):
                        sl = slice(hf * 1024, (hf + 1) * 1024)
                        S.stt("dve", lru_out[:, cg, sl], pcum[:, cg, sl], cin[:, cg:cg + 1],
                              hloc[:, cg, sl], ALU.mult, ALU.add, [t_cin] + t_hloc[cg], [t_lru[cg]])
                if debug:
                    S.dma("sp", dbg_out("lru_out", [128, 4, TPC], BF16), lru_out[:], t_lru, [], key="dbg", group=True)
                S.flush()
    if stop_after == "B":
        return finish(nc, es, S, out_d, dbg)

    pw = ExitStack()
    wupA = sbuf(pw, "wupA", [128, 4, 2 * D_FF], BF16)
    t_wup = [T() for _ in range(8)]
    WPC = 2048

    def wup_piece(j, wus, t_wus, wdst):
        kc, pc_ = j // 3, (j % 3) * WPC
        gk = kc if wdst is wupA else kc + 4
        S.dma("act", wus[j % 2][:], I["w_up"][gk * 128:(gk + 1) * 128, pc_:pc_ + WPC], (), [t_wus[j % 2]],
              key="wus%d" % (j % 2))
        S.copy("dve", wdst[:, kc, pc_:pc_ + WPC], wus[j % 2][:], [t_wus[j % 2]], [t_wup[gk]], wg=("wup", gk))

    with ExitStack() as p2:
        wus2 = [sbuf(p2, "wus2_%d" % i, [128, WPC]) for i in range(2)]
        t_wus2 = [T(), T()]
        KT = sbuf(p2, "KT", [128, S_TOT], BF16)
        Vt = sbuf(p2, "Vt", [128, 128, 130], BF16)
        QT = sbuf(p2, "QT", [128, NQB, 128], BF16)
        t_kv = [T() for _ in range(8)]
        t_q = [T() for _ in range(8)]
        bN = sbuf(p2, "bN", [128, 3, 128])
        bM = sbuf(p2, "bM", [128, 3, 128])
        bfar = sbuf(p2, "bfar", [128, 1])
        gsub = sbuf(p2, "gsub", [128, 128])
        t_b = T()
        S.dma("sp", bN[:], I["bias_g"][:, :, :], (), [t_b], key="p2c", group=True, wg="c")
        S.dma("sp", bM[:], I["bias_m"][:, :, :], (), [t_b], key="p2c", group=True, wg="c")
        S.dma("sp", bfar[:], I["bfar"][:, :], (), [t_b], key="p2c", group=True, wg="c")
        S.dma("sp", gsub[:], I["g_subln"][0:1, :].partition_broadcast(128), (), [t_b], key="p2c", group=True, wg="c")
        S.tt("dve", bN[:], bN[:], bM[:], ALU.add, [t_b], [t_b])
        S.ts("dve", gsub[:], gsub[:], 1.0 - LAMBDA_INIT, None, ALU.mult, reads=[t_b], writes=[t_b])
        for r in range(8):
            def ldk(E, r=r):
                p = pid_of(E, "p2")
                base = (p // 2) * (128 * 2048) + (KOFF + r * NB1)
                src = bass.AP(g1f.tensor, base, [[2048, 128], [1, 2048]])
                return E.dma_start(out=KT[:, r * 2048:(r + 1) * 2048], in_=src)

            def ldv(E, r=r):
                p = pid_of(E, "p2")
                base = (p // 2) * (128 * 16 * 130) + (VOFF + r * NB1)
                src = bass.AP(g1f.tensor, base, [[16 * 130, 128], [1, 16 * 130]])
                return E.dma_start(out=Vt[:, r * 16:(r + 1) * 16, :].rearrange("p b c -> p (b c)"), in_=src)

            def ldq(E, r=r):
                p = pid_of(E, "p2")
                base = (p // 2) * (2 * 128 * 1024) + (p % 2) * (128 * 1024) + (QOFF + r * NB1)
                src = bass.AP(g1f.tensor, base, [[1024, 128], [1, 1024]])
                return E.dma_start(out=QT[:, r * 8:(r + 1) * 8, :].rearrange("p b t -> p (b t)"), in_=src)
            S.dmaf("sp", ldq, [t_ag1out], [t_q[r]], key="qg", group=True)
            S.dmaf("act", ldk, [t_ag1out], [t_kv[r]], key="kg", group=True, wg="kv")
            S.dmaf("pool", ldv, [t_ag1out], [t_kv[r]], key="vg", group=True, wg="kv")

        NSB = 3
        psS = [psum(p2, "psS%d" % i, [128, 2, 512]) for i in range(NSB)]
        acc = [psum(p2, "acc%d" % i, [128, 512]) for i in range(2)]
        t_psS, t_acc = [T() for _ in range(NSB)], [T(), T()]
        NPB = 3
        PT = [sbuf(p2, "PT%d" % i, [128, 2, 512], BF16) for i in range(NPB)]
        t_PT = [T() for _ in range(NPB)]
        tmpn = sbuf(p2, "tmpn", [128, 2, 384])
        t_tmpn = T()
        attn = sbuf(p2, "attn", [128, NQB, 128], BF16)
        t_attn = [T() for _ in range(4)]
        o1 = [sbuf(p2, "o1_%d" % i, [128, 128]) for i in range(2)]
        osm = [sbuf(p2, "osm%d" % i, [128, 8]) for i in range(2)]
        ojk = sbuf(p2, "ojk", [128, 128])
        t_o1, t_osm, t_ojk = [T(), T()], [T(), T()], T()
        items = []
        for m in range(NQB):
            far = list(range(0, max(2 * m - 1, 0)))
            near = [kb for kb in (2 * m - 1, 2 * m, 2 * m + 1) if kb >= 0]
            groups = [(far[i:i + 4], False) for i in range(0, len(far), 4)] + [(near, True)]
            for gidx, (kbs, is_near) in enumerate(groups):
                items.append((m, kbs, is_near, gidx == 0, gidx == len(groups) - 1))
        NIT = len(items)
        LA = 2

        def st1(t):
            m, kbs, is_near, first, last = items[t]
            sb_i = t % NSB
            rd = [t_kv[r] for r in sorted({kb // 16 for kb in kbs})] + [t_q[m // 8]]

            def qk(E):
                res = []
                for jj, kb in enumerate(kbs):
                    for mp in range(2):
                        res.append(E.matmul(psS[sb_i][:, mp, jj * 128:(jj + 1) * 128],
                                            lhsT=KT[64 * mp:64 * mp + 64, kb * 128:(kb + 1) * 128],
                                            rhs=QT[64 * mp:64 * mp + 64, m, :], start=True, stop=True))
                return res
            S.op("pe", qk, rd, [t_psS[sb_i]])

        def st2(t):
            m, kbs, is_near, first, last = items[t]
            n = len(kbs)
            sb_i, pb_i = t % NSB, t % NPB
            if not is_near:
                S.act(PT[pb_i][:, :, 0:n * 128], psS[sb_i][:, :, 0:n * 128], AF.Exp, [t_psS[sb_i], t_b],
                      [t_PT[pb_i]], bias=bfar[:, 0:1], scale=0.125)
            else:
                t0 = 3 - n
                for mp in range(2):
                    S.stt("dve", tmpn[:, mp, 0:n * 128], psS[sb_i][:, mp, 0:n * 128], 0.125,
                          bN[:, t0:3, :].rearrange("p a b -> p (a b)"), ALU.mult, ALU.add,
                          [t_psS[sb_i], t_b], [t_tmpn])
                S.act(PT[pb_i][:, :, 0:n * 128], tmpn[:, :, 0:n * 128], AF.Exp, [t_tmpn], [t_PT[pb_i]])

        def st3(t):
            m, kbs, is_near, first, last = items[t]
            pb_i, ab = t % NPB, m % 2
            rdk = sorted({kb // 16 for kb in kbs})

            def pv(E):
                res = []
                for jj, kb in enumerate(kbs):
                    for mp in range(2):
                        stt_ = first and jj == 0 and mp == 0
                        res.append(E.matmul(acc[ab][:, mp * 130:mp * 130 + 129],
                                            lhsT=PT[pb_i][:, mp, jj * 128:(jj + 1) * 128],
                                            rhs=Vt[:, kb, 0:129], start=stt_, stop=False,
                                            skip_group_check=True))
                return res
            S.op("pe", pv, [t_PT[pb_i]] + [t_kv[r] for r in rdk], [t_acc[ab]])
            return last

        def fin_a(m):
            ab = m % 2
            A = acc[ab]
            den = A[:, 128:128 + 131:130]
            S.op("dve", lambda E: E.reciprocal(out=osm[ab][:, 0:2], in_=den), [t_acc[ab]], [t_osm[ab]])
            S.ts("dve", osm[ab][:, 2:3], osm[ab][:, 1:2], lam_t[:, 0:1], -1.0, ALU.mult, ALU.mult,
                 [t_osm[ab], t_const], [t_osm[ab]])
            S.ts("dve", o1[ab][:], A[:, 0:128], osm[ab][:, 0:1], None, ALU.mult, reads=[t_acc[ab], t_osm[ab]],
                 writes=[t_o1[ab]])
            S.stt("dve", o1[ab][:], A[:, 130:258], osm[ab][:, 2:3], o1[ab][:], ALU.mult, ALU.add,
                  [t_acc[ab], t_osm[ab], t_o1[ab]], [t_o1[ab]])
            S.stt("dve", ojk[:], o1[ab][:], 1.0, o1[ab][:], ALU.mult, ALU.mult, [t_o1[ab]], [t_ojk, t_osm[ab]],
                  accum=osm[ab][:, 3:4])
            S.ts("dve", osm[ab][:, 3:4], osm[ab][:, 3:4], 1.0 / 128, EPS, ALU.mult, ALU.add, [t_osm[ab]],
                 [t_osm[ab]])

        def ag3_part(pp):
            S.dma("sp", ag3_in[pp].ap().rearrange("(m q) d -> q m d", q=128), attn[:, 16 * pp:16 * pp + 16, :],
                  [t_attn[pp]], [t_ag3in[pp]], key="attst%d" % pp)
            S.op("pool", lambda E: E.collective_compute(
                "AllGather", ALU.bypass, replica_groups=RG, ins=[ag3_in[pp].ap().opt()],
                outs=[ag3_out.ap()[pp * 8 * 2048:(pp + 1) * 8 * 2048, :].opt()]),
                [t_ag3in[pp]], [t_ag3out], dma="cc3_%d" % pp, cc=True, wg="ag3")

        def fin_b(m):
            ab = m % 2
            S.act(osm[ab][:, 3:4], osm[ab][:, 3:4], AF.Ln, [t_osm[ab]], [t_osm[ab]])
            S.act(osm[ab][:, 3:4], osm[ab][:, 3:4], AF.Exp, [t_osm[ab]], [t_osm[ab]], scale=-0.5)
            S.stt("dve", attn[:, m, :], o1[ab][:], osm[ab][:, 3:4], gsub[:], ALU.mult, ALU.mult,
                  [t_o1[ab], t_osm[ab], t_b], [t_attn[m // 16]], wg="attn")
            if m % 16 == 15:
                ag3_part(m // 16)

        deferred = {}
        for t in range(min(LA, NIT)):
            st1(t)
        for t in range(NIT):
            if t + LA < NIT:
                st1(t + LA)
            st2(t)
            for mm in deferred.pop(t, []):
                fin_b(mm)
            if st3(t):
                mdone = items[t][0]
                fin_a(mdone)
                deferred.setdefault(t + 3, []).append(mdone)
                if mdone >= 4 and mdone % 4 == 0 and (mdone - 4) // 4 < 12:
                    wup_piece((mdone - 4) // 4, wus2, t_wus2, wupA)
        for t in sorted(deferred):
            for mm in deferred[t]:
                fin_b(mm)
        if debug:
            S.dma("sp", dbg_out("attn", [128, NQB, 128], BF16), attn[:], t_attn, [], key="dbg", group=True)
        S.flush()
    if stop_after == "2":
        return finish(nc, es, S, out_d, dbg)

    with ExitStack() as p3:
        wupB = sbuf(p3, "wupB", [128, 4, 2 * D_FF], BF16)

        def wupk(kc, c0, c1):
            return wupA[:, kc, c0:c1] if kc < 4 else wupB[:, kc - 4, c0:c1]
        ahalo = sbuf(p3, "ahalo", [128, 24, 2])
        t_ahalo = [T() for _ in range(24)]
        with ExitStack() as p3a:
            wo = sbuf(p3a, "wo", [128, 8, D], BF16)
            g1b = sbuf(p3a, "g1b", [128, D])
            wus3 = [sbuf(p3a, "wus3_%d" % i, [128, WPC]) for i in range(2)]
            t_wus3 = [T(), T()]
            h2last = sbuf(p3a, "h2last", [128, 8, 2], BF16)
            t_h2last = T()
            t_wo, t_g1b = T(), T()
            S.dma("sp", g1b[:], mod_scr.ap()[0:1, 2048:3072].partition_broadcast(128), [t_modscr], [t_g1b], key="g1b")
            for kc in range(8):
                b = kc % 2
                S.dma("act", wus3[b][:, 0:D], I["w_out"][kc * 128:(kc + 1) * 128, :], (), [t_wus3[b]], key="wus%d" % b)
                S.tt("dve", wo[:, kc, :], wus3[b][:, 0:D], g1b[:], ALU.mult, [t_wus3[b], t_g1b], [t_wo], wg="wo")
            att_all = sbuf(p3a, "att_all", [128, NT, 512], BF16)
            t_attall = T()
            for par in range(2):
                for h in range(4):
                    def lda(E, par=par, h=h):
                        p = pid_of(E, "p3")
                        base = (p // 2) * (8 * 16 * 16384) + (p % 2) * (8 * 16384) + (par + 2 * h) * (16 * 16384)
                        src = bass.AP(ag3_out.ap().tensor, base, [[128, 128], [128 * 128, 8], [1, 128]])
                        dst = att_all[:, :, :].rearrange("q (mm two) (h d) -> q mm two h d", two=2, h=4)[:, :, par, h, :]
                        return E.dma_start(out=dst, in_=src)
                    S.dmaf("act", lda, [t_ag3out], [t_attall], key="attg", group=True, wg="att")
            attT = [sbuf(p3a, "attT%d" % i, [128, 4, 128], BF16) for i in range(2)]
            xt3 = [sbuf(p3a, "xt3_%d" % i, [128, D]) for i in range(2)]
            xm = [sbuf(p3a, "xm%d" % i, [128, D]) for i in range(2)]
            xn3 = [sbuf(p3a, "xn3_%d" % i, [128, D]) for i in range(2)]
            h2t = [sbuf(p3a, "h2t%d" % i, [128, 8, 128], BF16) for i in range(2)]
            ss3 = sbuf(p3a, "ss3", [128, 16])
            t_att, t_attT, t_xt3, t_xm = [T(), T()], [T(), T()], [T(), T()], [T(), T()]
            t_xn3, t_h2t, t_sq3, t_ss3 = [T(), T()], [T(), T()], T(), T()
            pT = psum(p3a, "pT", [128, 1024], BF16)
            pm = [psum(p3a, "pm%d" % i, [128, 512]) for i in range(4)]
            ptr = [psum(p3a, "ptr%d" % i, [128, 512]) for i in range(2)]
            pha = psum(p3a, "pha", [128, 512])
            t_pT, t_pm, t_ptr, t_pha = T(), [T() for _ in range(4)], [T(), T()], T()
            order = [15] + list(range(15))
            def st_a(n_i):
                tt = order[n_i]
                b = n_i % 2
                S.dma("sp", xt3[b][:], I["x_own"][tt * 128:(tt + 1) * 128, :], (), [t_xt3[b]], key="xt3_%d" % b)
                if n_i < 12:
                    wup_piece(n_i, wus3, t_wus3, wupB)
                for h in range(4):
                    S.tr(pT[:, h * 128:(h + 1) * 128], att_all[:, tt, h * 128:(h + 1) * 128], identb[:],
                         [t_attall, t_const], [t_pT], wg=("pT", n_i))
                S.copy("act", attT[b][:].rearrange("p a b -> p (a b)"), pT[:, 0:512], [t_pT], [t_attT[b]])
                for half in range(2):
                    items = [(lru_out[:, cg, tt * 128:(tt + 1) * 128], wo[:, cg, half * 512:(half + 1) * 512])
                             for cg in range(4)]
                    items += [(attT[b][:, h, :], wo[:, 4 + h, half * 512:(half + 1) * 512]) for h in range(4)]
                    pk = (n_i % 2) * 2 + half
                    S.mm(pm[pk][:, :], items, t_lru + [t_attT[b], t_wo], [t_pm[pk]])
                    S.tt("dve", xm[b][:, half * 512:(half + 1) * 512], pm[pk][:, :],
                         xt3[b][:, half * 512:(half + 1) * 512], ALU.add, [t_pm[pk], t_xt3[b]], [t_xm[b]])
                S.dma("sp", xmid_scr.ap()[tt * 128:(tt + 1) * 128, :], xm[b][:], [t_xm[b]], [t_xmid[tt]], key="xmst%d" % b)

            def st_b(n_i):
                tt = order[n_i]
                b = n_i % 2
                S.act(xn3[b][:], xm[b][:], AF.Square, [t_xm[b]], [t_xn3[b], t_ss3], accum=ss3[:, tt:tt + 1])
                S.ts("dve", ss3[:, tt:tt + 1], ss3[:, tt:tt + 1], 1.0 / D, EPS, ALU.mult, ALU.add, [t_ss3], [t_ss3])
                S.act(ss3[:, tt:tt + 1], ss3[:, tt:tt + 1], AF.Ln, [t_ss3], [t_ss3])
                S.act(ss3[:, tt:tt + 1], ss3[:, tt:tt + 1], AF.Exp, [t_ss3], [t_ss3], scale=-0.5)
                S.ts("dve", xn3[b][:], xm[b][:], ss3[:, tt:tt + 1], None, ALU.mult, reads=[t_xm[b], t_ss3],
                     writes=[t_xn3[b]])
                for half in range(2):
                    for j in range(4):
                        kc = half * 4 + j
                        S.tr(ptr[half][:, j * 128:(j + 1) * 128], xn3[b][:, kc * 128:(kc + 1) * 128], ident[:],
                             [t_xn3[b], t_const], [t_ptr[half]])
                    for j in range(4):
                        kc = half * 4 + j
                        if j % 2 == 0:
                            S.ts("dve", h2t[b][:, kc, :], ptr[half][:, j * 128:(j + 1) * 128], gs2[:, kc:kc + 1],
                                 sh2[:, kc:kc + 1], ALU.mult, ALU.add, [t_ptr[half], t_mod], [t_h2t[b]])
                        else:
                            S.act(h2t[b][:, kc, :], ptr[half][:, j * 128:(j + 1) * 128], AF.Identity,
                                  [t_ptr[half], t_mod], [t_h2t[b]], bias=sh2[:, kc:kc + 1], scale=gs2[:, kc:kc + 1])
                S.dma("sp", h2_scr.ap()[:, :, tt * 128:(tt + 1) * 128].rearrange("k p t -> p k t"), h2t[b][:],
                      [t_h2t[b]], [t_h2scr[tt]], key="h2st%d" % b)
                if tt == 15:
                    S.copy("dve", h2last[:], h2t[b][:, :, 126:128], [t_h2t[b]], [t_h2last])
                if n_i == 13:
                    for c in range(24):
                        S.mm(pha[:, 2 * c:2 * c + 2], [(wupk(kc, c * 128, (c + 1) * 128), h2last[:, kc, :])
                                                      for kc in range(8)], t_wup + [t_h2last], [t_pha])
                    hst = sbuf(p3a, "hst", [128, 48])
                    t_hst = T()
                    S.copy("dve", hst[:], pha[:, 0:48], [t_pha], [t_hst])
                    S.dma("sp", ag4_in.ap(), hst[:], [t_hst], [t_ag4in], key="ag4st")
                    S.op("pool", lambda E: E.collective_compute("AllGather", ALU.bypass, replica_groups=RG,
                                                               ins=[ag4_in.ap().opt()], outs=[ag4_out.ap().opt()]),
                         [t_ag4in], [t_ag4out], dma="cc4", cc=True)

            for t in range(17):
                if t < 16:
                    st_a(t)
                if t >= 1:
                    st_b(t - 1)

            def ldh(E):
                p = pid_of(E, "p3")
                return E.dma_start(out=ahalo[:].rearrange("p a b -> p (a b)"),
                                   in_=ag4_out.ap()[ds(((p + 7) % 8) * 128, 128), :])
            S.dmaf("pool", ldh, [t_ag4out], t_ahalo, key="ahalo")
            S.ts("dve", ahalo[:].rearrange("p a b -> p (a b)"), ahalo[:].rearrange("p a b -> p (a b)"),
                 flags[:, 0:1], None, ALU.mult, reads=t_ahalo + [t_const], writes=t_ahalo)
            if debug:
                S.dma("sp", dbg_out("xmid", [TPC, D]), xmid_scr.ap(), t_xmid, [], key="dbg", group=True)
            S.flush()
        if stop_after == "3a":
            return finish(nc, es, S, out_d, dbg)

        with ExitStack() as p3b:
            wd = sbuf(p3b, "wd", [128, 24, D], BF16)
            t_wd = [T() for _ in range(24)]
            g2b = sbuf(p3b, "g2b", [128, D])
            gfb = sbuf(p3b, "gfb", [128, D])
            wsd = [sbuf(p3b, "wsd%d" % i, [128, D]) for i in range(2)]
            t_g2b, t_gfb, t_wsd = T(), T(), [T(), T()]
            S.dma("sp", g2b[:], mod_scr.ap()[0:1, 5120:6144].partition_broadcast(128), [t_modscr], [t_g2b], key="g2b")
            S.dma("sp", gfb[:], I["g_final"][0:1, :].partition_broadcast(128), (), [t_gfb], key="gfb")
            def load_wd(c):
                b = c % 2
                S.dma("act", wsd[b][:], I["w_down"][c * 128:(c + 1) * 128, :], (), [t_wsd[b]], key="wsd%d" % b)
                S.tt("dve", wd[:, c, :], wsd[b][:], g2b[:], ALU.mult, [t_wsd[b], t_g2b], [t_wd[c]])
            h2g = [sbuf(p3b, "h2g%d" % i, [128, 8, 256], BF16) for i in range(2)]
            xg = sbuf(p3b, "xg", [128, 2, D])
            t_h2g, t_xg = [T(), T()], [T(), T()]
            NE = 5
            yy = [sbuf(p3b, "yy%d" % i, [128, 256]) for i in range(NE)]
            u3 = [sbuf(p3b, "u3_%d" % i, [128, 256]) for i in range(NE)]
            actT = [sbuf(p3b, "actT%d" % i, [128, 256], BF16) for i in range(NE)]
            t_yy, t_u3, t_actT = [T() for _ in range(NE)], [T() for _ in range(NE)], [T() for _ in range(NE)]
            ssf = sbuf(p3b, "ssf", [128, 16])
            t_ssf = T()
            pff = [psum(p3b, "pff%d" % i, [128, 512]) for i in range(4)]
            pag = [psum(p3b, "pag%d" % i, [128, 512]) for i in range(4)]
            t_pff, t_pag = [T() for _ in range(4)], [T() for _ in range(4)]
            NITF = 8 * 24

            def f1(it):
                gix, c = divmod(it, 24)
                hb, e, k = gix % 2, it % NE, it % 4
                if c == 0:
                    t0 = gix * 256
                    S.dma("sp", h2g[hb][:], h2_scr.ap()[:, :, t0:t0 + 256].rearrange("k p t -> p k t"),
                          [t_h2scr[2 * gix], t_h2scr[2 * gix + 1]], [t_h2g[hb]], key="h2g%d" % hb)
                if it < 24:
                    load_wd(it)
                pa_, pg_, tp = pag[k][:, 0:256], pag[k][:, 256:512], t_pag[k]
                S.mm(pa_, [(wupk(kc, c * 128, (c + 1) * 128), h2g[hb][:, kc, :]) for kc in range(8)],
                     t_wup + [t_h2g[hb]], [tp])
                S.mm(pg_, [(wupk(kc, D_FF + c * 128, D_FF + (c + 1) * 128), h2g[hb][:, kc, :])
                           for kc in range(8)], t_wup + [t_h2g[hb]], [tp], wg=("ag", it))
                w = lambda j: cfw[:, c, j:j + 1]
                S.act(yy[e][:], pa_, AF.Identity, [tp, t_const], [t_yy[e]], bias=cfb[:, c:c + 1], scale=w(2))
                S.stt("dve", yy[e][:, 1:256], pag[k][:, 0:255], w(1), yy[e][:, 1:256], ALU.mult, ALU.add,
                      [tp, t_const], [t_yy[e]])
                S.stt("dve", yy[e][:, 2:256], pag[k][:, 0:254], w(0), yy[e][:, 2:256], ALU.mult, ALU.add,
                      [tp, t_const], [t_yy[e]])
                S.stt("dve", yy[e][:, 0:1], ahalo[:, c, 1:2], w(1), yy[e][:, 0:1], ALU.mult, ALU.add,
                      [t_ahalo[c], t_const], [t_yy[e]])
                S.stt("dve", yy[e][:, 0:2], ahalo[:, c, 0:2], w(0), yy[e][:, 0:2], ALU.mult, ALU.add,
                      [t_ahalo[c], t_const], [t_yy[e]])
                S.copy("act", ahalo[:, c, :], pag[k][:, 254:256], [tp, t_yy[e]], [t_ahalo[c]])

            def f2(it):
                gix, c = divmod(it, 24)
                e, k = it % NE, it % 4
                S.tt("pool", u3[e][:], yy[e][:], yy[e][:], ALU.mult, [t_yy[e]], [t_u3[e]])
                S.ts("pool", u3[e][:], u3[e][:], GELU_C, 1.0, ALU.mult, ALU.add, [t_u3[e]], [t_u3[e]])
                S.tt("pool", u3[e][:], u3[e][:], yy[e][:], ALU.mult, [t_u3[e], t_yy[e]], [t_u3[e]])
                S.act(u3[e][:], u3[e][:], AF.Sigmoid, [t_u3[e]], [t_u3[e]], scale=GELU_K)
                S.tt("dve", u3[e][:], u3[e][:], yy[e][:], ALU.mult, [t_u3[e], t_yy[e]], [t_u3[e]])
                S.tt("dve", actT[e][:], u3[e][:], pag[k][:, 256:512], ALU.mult, [t_u3[e], t_pag[k]], [t_actT[e]])

            def f3(it):
                gix, c = divmod(it, 24)
                e = it % NE

                def dn(E):
                    res = []
                    for tb in range(2):
                        for half in range(2):
                            res.append(E.matmul(pff[tb * 2 + half][:, :], lhsT=actT[e][:, tb * 128:(tb + 1) * 128],
                                                rhs=wd[:, c, half * 512:(half + 1) * 512],
                                                start=(c == 0), stop=(c == 23)))
                    return res
                S.op("pe", dn, [t_actT[e], t_wd[c]], t_pff)
                if c == 23:
                    ffin(gix)

            def ffin(gix):
                for tb in range(2):
                    tt = 2 * gix + tb
                    S.dma("sp", xg[:, tb, :], xmid_scr.ap()[tt * 128:(tt + 1) * 128, :], [t_xmid[tt]], [t_xg[tb]],
                          key="xg%d" % tb)
                    for half in range(2):
                        S.tt("dve", xg[:, tb, half * 512:(half + 1) * 512], pff[tb * 2 + half][:, :],
                             xg[:, tb, half * 512:(half + 1) * 512], ALU.add, [t_pff[tb * 2 + half], t_xg[tb]],
                             [t_xg[tb]])
                    S.act(wsd[0][:], xg[:, tb, :], AF.Square, [t_xg[tb]], [t_wsd[0], t_ssf], accum=ssf[:, tt:tt + 1])
                    S.ts("dve", ssf[:, tt:tt + 1], ssf[:, tt:tt + 1], 1.0 / D, EPS, ALU.mult, ALU.add, [t_ssf],
                         [t_ssf])
                    S.act(ssf[:, tt:tt + 1], ssf[:, tt:tt + 1], AF.Ln, [t_ssf], [t_ssf])
                    S.act(ssf[:, tt:tt + 1], ssf[:, tt:tt + 1], AF.Exp, [t_ssf], [t_ssf], scale=-0.5)
                    S.stt("dve", xg[:, tb, :], xg[:, tb, :], ssf[:, tt:tt + 1], gfb[:], ALU.mult, ALU.mult,
                          [t_xg[tb], t_ssf, t_gfb], [t_xg[tb]])
                    S.dma("sp", out_d[tt * 128:(tt + 1) * 128, :], xg[:, tb, :], [t_xg[tb]], [], key="out%d" % tb)

            for t in range(NITF + 4):
                if t < NITF:
                    f1(t)
                if 0 <= t - 1 < NITF:
                    f2(t - 1)
                if 0 <= t - 4 < NITF:
                    f3(t - 4)
            S.flush()
    return finish(nc, es, S, out_d, dbg)


def finish(nc, es, S, out_d, dbg):
    def fin(E):
        for k, sm in S.dsem.items():
            if S.dcnt[k] > 0:
                E.wait_ge(sm, S.dcnt[k])
        return None
    o = Op()
    o.eng, o.fn, o.dma, o.needed, o.ev, o.cc, o.deps, o.idx, o.tail = "sp", fin, None, False, None, False, [], S.nidx, None
    S.pending.append(o)
    S.flush()
    return nc, dbg


_CACHE = {}


def kernel(**inputs):
    maps = make_inmaps(inputs)
    if "nc" not in _CACHE:
        _CACHE["nc"] = build()[0]
    nc = _CACHE["nc"]
    res = run_bass_kernel_spmd(nc, maps, core_ids=list(range(NCORES)))
    out = np.concatenate([np.asarray(r["out"], np.float32) for r in res.results], axis=0)
    return out.reshape(1, S_TOT, D)
```

```python
import numpy as np
from contextlib import ExitStack
import concourse.bass as bass
import concourse.mybir as mybir
from concourse.bass import ds
from concourse.bass_utils import run_bass_kernel_spmd

F32 = mybir.dt.float32
BF16 = mybir.dt.bfloat16
ALU = mybir.AluOpType
AF = mybir.ActivationFunctionType

NCORES = 8
S_TOT = 16384
D = 1024
TPC = S_TOT // NCORES
NT = TPC // 128
D_IN = 2560
D_FF = 3072
EPS = 1e-6
NQB = 64
LAMBDA_INIT = 0.8 - 0.6
QOFF, KOFF, VOFF = 0, 4 * 128 * 2048, 2 * 4 * 128 * 2048
NB1 = VOFF + 4 * 128 * 16 * 130
GELU_K = 1.5957691216057308
GELU_C = 0.044715


class Tok:
    __slots__ = ("name", "ws", "wg", "gdeps", "r")

    def __init__(self, name=""):
        self.name = name
        self.ws = []
        self.wg = None
        self.gdeps = []
        self.r = {}


class Op:
    __slots__ = ("eng", "fn", "deps", "dma", "needed", "ev", "idx", "tail", "cc")


class Sched:
    ENGS = ("pe", "act", "dve", "pool", "sp")

    def __init__(self, nc, es):
        self.nc = nc
        self.es = es
        self.sem = {k: es.enter_context(nc.semaphore("s_" + k)) for k in self.ENGS}
        self.dsem = {}
        self.dcnt = {}
        self.dgroup = set()
        self.cnt = {k: 0 for k in self.ENGS}
        self.seen = {k: {} for k in self.ENGS}
        self.pending = []
        self.nidx = 0

    def op(self, eng, fn, reads=(), writes=(), dma=None, deps=(), cc=False, wg=None, group=False):
        o = Op()
        o.eng, o.fn, o.dma, o.needed, o.ev, o.cc = eng, fn, dma, False, None, cc
        o.idx = self.nidx
        o.tail = None
        self.nidx += 1
        if dma is not None and dma not in self.dsem:
            self.dsem[dma] = self.es.enter_context(self.nc.semaphore("d_" + str(dma)))
            self.dcnt[dma] = 0
        if group:
            self.dgroup.add(dma)
        d = {}

        def add(x):
            if x is None or x is o:
                return
            k = ("d", x.dma) if x.dma is not None else ("e", x.eng)
            if k not in d or d[k].idx < x.idx:
                d[k] = x

        for x in deps:
            add(x)
        for t in reads:
            for x in t.ws:
                add(x)
        for t in writes:
            if wg is not None and t.wg == wg:
                for x in t.gdeps:
                    add(x)
                for x in t.r.values():
                    add(x)
            else:
                gd = list(t.ws) + list(t.r.values())
                for x in gd:
                    add(x)
                t.gdeps = gd
        o.deps = list(d.values())
        for x in o.deps:
            x.needed = True
        mk = ("d", dma) if dma is not None else ("e", eng)
        for t in reads:
            t.r[mk] = o
        for t in writes:
            if wg is not None and t.wg == wg:
                t.ws.append(o)
            else:
                t.ws = [o]
                t.wg = wg
                t.r = {}
        self.pending.append(o)
        return o

    def flush(self):
        nc = self.nc
        last = {}
        for o in self.pending:
            if o.dma is None:
                last[o.eng] = o
        for o in last.values():
            o.needed = True
        for o in self.pending:
            o.tail = last
            if o.dma is not None:
                self.dcnt[o.dma] += (1 if o.cc else 16)
                o.ev = ("d", o.dma, self.dcnt[o.dma])
            elif o.needed:
                self.cnt[o.eng] += 1
                o.ev = ("e", o.eng, self.cnt[o.eng])
        for o in self.pending:
            if o.dma is not None and o.dma in self.dgroup:
                o.ev = ("d", o.dma, self.dcnt[o.dma])
        byeng = {k: [o for o in self.pending if o.eng == k] for k in self.ENGS}
        drain = sorted({o.dma for o in self.pending if o.dma is not None and not o.cc}, key=str)

        def emit2(E, ename):
            seen = self.seen[ename]
            for o in byeng[ename]:
                waits = []
                for dpo in o.deps:
                    if dpo.ev is None:
                        dpo = dpo.tail[dpo.eng]
                    kind, key, val = dpo.ev
                    if kind == "e" and key == "pe" and ename == "pe" and o.dma is None:
                        continue
                    sk = (kind, key)
                    if seen.get(sk, 0) >= val:
                        continue
                    seen[sk] = val
                    waits.append((self.sem[key] if kind == "e" else self.dsem[key], val))
                for sm, v in waits[1:]:
                    E.wait_ge(sm, v)
                ins = o.fn(E)
                if ins is None:
                    for sm, v in waits[:1]:
                        E.wait_ge(sm, v)
                    if o.ev is not None and o.dma is None:
                        E.nop().then_inc(self.sem[ename], 1)
                    continue
                if not isinstance(ins, (list, tuple)):
                    ins = [ins]
                if waits:
                    ins[0]._wait_ge(*waits[0])
                if o.dma is not None:
                    ins[-1].then_inc(self.dsem[o.dma], 1 if o.cc else 16)
                elif o.ev is not None:
                    ins[-1].then_inc(self.sem[ename], 1)

        with nc.Block() as block:
            @block.tensor
            def _(E):
                emit2(E, "pe")

            @block.scalar
            def _(E):
                emit2(E, "act")

            @block.vector
            def _(E):
                emit2(E, "dve")

            @block.gpsimd
            def _(E):
                emit2(E, "pool")

            @block.sync
            def _(E):
                emit2(E, "sp")
                for k in drain:
                    E.wait_ge(self.dsem[k], self.dcnt[k])
        self.pending = []

    def dma(self, eng, out, in_, reads=(), writes=(), key="misc", group=False, wg=None):
        return self.op(eng, lambda E: E.dma_start(out=out, in_=in_), reads, writes, dma=key, group=group, wg=wg)

    def dmaf(self, eng, fn, reads=(), writes=(), key="misc", group=False, wg=None):
        return self.op(eng, fn, reads, writes, dma=key, group=group, wg=wg)

    def act(self, out, in_, func, reads=(), writes=(), bias=None, scale=1.0, accum=None, wg=None):
        kw = {}
        if bias is not None:
            kw["bias"] = bias
        if accum is not None:
            kw["accum_out"] = accum
        return self.op("act", lambda E: E.activation(out=out, in_=in_, func=func, scale=scale, **kw),
                       reads, writes, wg=wg)

    def ts(self, eng, out, in0, s1, s2, op0, op1=None, reads=(), writes=(), accum=None, wg=None):
        kw = {}
        if op1 is not None:
            kw["op1"] = op1
        if accum is not None:
            kw["accum_out"] = accum
        return self.op(eng, lambda E: E.tensor_scalar(out=out, in0=in0, scalar1=s1, scalar2=s2, op0=op0, **kw),
                       reads, writes, wg=wg)

    def stt(self, eng, out, in0, scalar, in1, op0, op1, reads=(), writes=(), accum=None, wg=None):
        kw = {}
        if accum is not None:
            kw["accum_out"] = accum
        return self.op(eng, lambda E: E.scalar_tensor_tensor(out=out, in0=in0, scalar=scalar, in1=in1,
                                                             op0=op0, op1=op1, **kw), reads, writes, wg=wg)

    def tt(self, eng, out, in0, in1, op, reads=(), writes=(), wg=None):
        return self.op(eng, lambda E: E.tensor_tensor(out=out, in0=in0, in1=in1, op=op), reads, writes, wg=wg)

    def copy(self, eng, out, in_, reads=(), writes=(), wg=None):
        if eng == "act":
            return self.op(eng, lambda E: E.copy(out=out, in_=in_), reads, writes, wg=wg)
        return self.op(eng, lambda E: E.tensor_copy(out=out, in_=in_), reads, writes, wg=wg)

    def memset(self, eng, ap, val, writes=(), wg=None):
        return self.op(eng, lambda E: E.memset(ap, val), (), writes, wg=wg)

    def scan(self, eng, out, d0, d1, init, reads=(), writes=()):
        return self.op(eng, lambda E: E.tensor_tensor_scan(out=out, data0=d0, data1=d1, initial=init,
                                                           op0=ALU.mult, op1=ALU.add), reads, writes)

    def mm(self, out, items, reads=(), writes=(), start=True, stop=True, skip=False, wg=None):
        def fn(E):
            res = []
            n = len(items)
            for i, (l, r) in enumerate(items):
                kw = {}
                if skip:
                    kw["skip_group_check"] = True
                res.append(E.matmul(out, lhsT=l, rhs=r, start=(start and i == 0), stop=(stop and i == n - 1), **kw))
            return res
        return self.op("pe", fn, reads, writes, wg=wg)

    def tr(self, out, in_, ident, reads=(), writes=(), wg=None):
        return self.op("pe", lambda E: E.transpose(out=out, in_=in_, identity=ident), reads, writes, wg=wg)


def _colT(v, n):
    return np.ascontiguousarray(np.asarray(v, np.float32).reshape(n, 128).T)


def _t5_bucket(rel):
    n = np.maximum(rel, 0)
    nf = np.maximum(n, 1).astype(np.float32)
    large = 16 + (np.log(nf / 16) / np.log(128 / 16) * 16).astype(np.int32)
    large = np.minimum(large, 31)
    return np.where(n < 16, n, large)


INPUT_SPECS = [
    ("x_own", [TPC, D]), ("x_prev", [128, D]), ("flags", [128, 4]), ("sel_prev", [128, 8]),
    ("c_in", [1, D]), ("w_adaT", [128, 48, D]), ("b_adaT", [128, 48]), ("g1T", [128, 8]), ("g2T", [128, 8]),
    ("g_final", [1, D]), ("w_in", [D, D_IN]), ("clw", [128, 4, 4]), ("clb", [128, 4]),
    ("waBD", [128, 4, 128]), ("wxBD", [128, 4, 128]), ("lbaT", [128, 4]), ("lbxT", [128, 4]), ("lamT", [128, 4]),
    ("lamq", [1, 256]), ("g_subln", [1, 128]), ("w_out", [D, D]), ("w_up", [D, 2 * D_FF]),
    ("cfw", [128, 24, 3]), ("cfb", [128, 24]), ("w_down", [D_FF, D]),
    ("bias_g", [128, 3, 128]), ("bias_m", [128, 3, 128]), ("bfar", [128, 1]), ("ident", [128, 128]),
]


def make_inmaps(inp):
    f = lambda a: np.ascontiguousarray(np.asarray(a, np.float32))
    x = f(inp["x"])[0]
    w_ada = f(inp["w_ada"])[0]
    w_adaT = np.ascontiguousarray(w_ada.T.reshape(48, 128, D).transpose(1, 0, 2))
    wa, wx = f(inp["lru_wa"])[0], f(inp["lru_wx"])[0]

    def bd(w):
        o = np.zeros((128, 4, 128), np.float32)
        for g in range(4):
            o[0:64, g, 0:64] = w[2 * g]
            o[64:128, g, 64:128] = w[2 * g + 1]
        return o
    clw = np.ascontiguousarray(f(inp["conv_lru_w"])[0].reshape(4, 4, 128).transpose(2, 1, 0))
    cfw = np.ascontiguousarray(f(inp["conv_ffn_w"])[0].reshape(3, 24, 128).transpose(2, 1, 0))
    rel_bias = f(inp["rel_bias"])
    common = {
        "c_in": f(inp["c"]), "w_adaT": w_adaT, "b_adaT": _colT(f(inp["b_ada"])[0], 48),
        "g1T": _colT(f(inp["g_norm1"])[0], 8), "g2T": _colT(f(inp["g_norm2"])[0], 8),
        "g_final": f(inp["g_final"]).reshape(1, D), "w_in": f(inp["w_in"])[0],
        "clw": clw, "clb": _colT(f(inp["conv_lru_b"])[0], 4), "waBD": bd(wa), "wxBD": bd(wx),
        "lbaT": _colT(f(inp["lru_ba"])[0], 4), "lbxT": _colT(f(inp["lru_bx"])[0], 4),
        "lamT": _colT(f(inp["lru_lambda"])[0], 4),
        "lamq": np.concatenate([f(inp["lam_q1"]), f(inp["lam_k1"]), f(inp["lam_q2"]), f(inp["lam_k2"])], 0).reshape(1, 256),
        "g_subln": f(inp["g_subln"]).reshape(1, 128), "w_out": f(inp["w_out"])[0], "w_up": f(inp["w_up"])[0],
        "cfw": cfw, "cfb": _colT(f(inp["conv_ffn_b"])[0], 24), "w_down": f(inp["w_down"])[0],
        "ident": np.eye(128, dtype=np.float32),
    }
    kk = np.arange(128)[:, None]
    qq = np.arange(128)[None, :]
    maps = []
    for r in range(NCORES):
        h, z = r // 2, r % 2
        m = dict(common)
        m["x_own"] = np.ascontiguousarray(x[r * TPC:(r + 1) * TPC])
        m["x_prev"] = np.ascontiguousarray(x[r * TPC - 128:r * TPC]) if r > 0 else np.zeros((128, D), np.float32)
        fl = np.zeros((128, 4), np.float32)
        fl[:, 0] = 1.0 if r > 0 else 0.0
        m["flags"] = fl
        sp = np.zeros((128, 8), np.float32)
        if r > 0:
            sp[:, r - 1] = 1.0
        m["sel_prev"] = sp
        bg = np.zeros((128, 3, 128), np.float32)
        bm = np.zeros((128, 3, 128), np.float32)
        for t in range(3):
            dd = z + 1 - t
            rel = dd * 128 + qq - kk
            bg[:, t, :] = rel_bias[_t5_bucket(rel), h]
            bm[:, t, :] = np.where(rel >= 0, 0.0, -1e30)
        m["bias_g"], m["bias_m"] = bg, bm
        m["bfar"] = np.ascontiguousarray(np.broadcast_to(rel_bias[31, h], (128, 1))).astype(np.float32)
        maps.append(m)
    return maps


def build(debug=False, stop_after=None):
    nc = bass.Bass("TRN2", target_bir_lowering=False)
    es = ExitStack()
    I = {n: nc.dram_tensor(n, list(s), F32, kind="ExternalInput").ap() for n, s in INPUT_SPECS}
    out_d = nc.dram_tensor("out", [TPC, D], F32, kind="ExternalOutput").ap()
    dbg = {}

    def dbg_out(name, shape, dt=F32):
        dbg[name] = nc.dram_tensor("dbg_" + name, list(shape), dt, kind="ExternalOutput").ap()
        return dbg[name]

    ag1_in = nc.dram_tensor("ag1_in", [128, NB1 // 128], BF16)
    ag1_out = nc.dram_tensor("ag1_out", [1024, NB1 // 128], BF16)
    ag2_in = nc.dram_tensor("ag2_in", [128, 8], F32)
    ag2_out = nc.dram_tensor("ag2_out", [1024, 8], F32)
    ag3_in = [nc.dram_tensor("ag3_in%d" % i, [16 * 128, 128], BF16) for i in range(4)]
    ag3_out = nc.dram_tensor("ag3_out", [8 * NQB * 128, 128], BF16)
    ag4_in = nc.dram_tensor("ag4_in", [128, 48], F32)
    ag4_out = nc.dram_tensor("ag4_out", [1024, 48], F32)
    mod_scr = nc.dram_tensor("mod_scr", [1, 48 * 128], F32)
    xmid_scr = nc.dram_tensor("xmid_scr", [TPC, D], F32)
    h2_scr = nc.dram_tensor("h2_scr", [8, 128, TPC], BF16)
    a1f = ag1_in.ap().rearrange("a b -> (a b)")
    g1f = ag1_out.ap().rearrange("a b -> (a b)")
    RG = [list(range(NCORES))]

    S = Sched(nc, es)
    T = Tok
    pidc = {}

    def pid_of(E, name):
        k = (name, S.nidx // 10 ** 9, id(E))
        if k not in pidc:
            pidc[k] = E.partition_id()
        return pidc[k]

    def sbuf(st, name, shape, dt=F32):
        return st.enter_context(nc.sbuf_tensor("sb_" + name, list(shape), dt))

    def psum(st, name, shape, dt=F32):
        return st.enter_context(nc.psum_tensor("ps_" + name, list(shape), dt))

    t_ag1in, t_ag1out, t_ag2in, t_ag2out = T(), T(), T(), T()
    t_ag3in, t_ag3out, t_ag4in, t_ag4out = [T() for _ in range(4)], T(), T(), T()
    t_modscr, t_xmid, t_h2scr = T(), [T() for _ in range(NT)], [T() for _ in range(NT)]

    ident = sbuf(es, "ident", [128, 128])
    identb = sbuf(es, "identb", [128, 128], BF16)
    modT = sbuf(es, "modT", [128, 48])
    gs1 = sbuf(es, "gs1", [128, 8])
    gs2 = sbuf(es, "gs2", [128, 8])
    cpar = sbuf(es, "cpar", [128, 64])
    flags = sbuf(es, "flags", [128, 4])
    selp = sbuf(es, "selp", [128, 8])
    lam_t = sbuf(es, "lam_t", [128, 4])
    lru_out = sbuf(es, "lru_out", [128, 4, TPC], BF16)
    t_const = T()
    t_mod = T()
    t_lru = [T() for _ in range(4)]
    CLW, CLB, LBA, LBX, LAM, NSP = 0, 16, 20, 24, 28, 32
    cfw = sbuf(es, "cfw", [128, 24, 3])
    cfb = sbuf(es, "cfb", [128, 24])

    with ExitStack() as p0:
        cb = sbuf(p0, "cb", [128, D])
        wst = [sbuf(p0, "wst%d" % i, [128, 4, D]) for i in range(2)]
        junk = {"dve": sbuf(p0, "junk_d", [128, D]), "pool": sbuf(p0, "junk_p", [128, D])}
        lq = sbuf(p0, "lq", [128, 4, 64])
        ltmp = sbuf(p0, "ltmp", [128, 8])
        modsb = sbuf(p0, "modsb", [48, 128])
        pt0 = psum(p0, "pt0", [128, 512])
        t_cb, t_wst, t_junk = T(), [T(), T()], {"dve": T(), "pool": T()}
        t_lq, t_ltmp, t_modsb, t_pt0 = T(), T(), T(), T()
        tmodc = [T() for _ in range(48)]

        S.dma("sp", ident[:], I["ident"][:, :], (), [t_const], key="c0", group=True, wg="c0")
        S.dma("sp", flags[:], I["flags"][:, :], (), [t_const], key="c0", group=True, wg="c0")
        S.dma("sp", selp[:], I["sel_prev"][:, :], (), [t_const], key="c0", group=True, wg="c0")
        S.dma("sp", cpar[:, CLW:CLW + 16], I["clw"].rearrange("p g j -> p (g j)"), (), [t_const], key="c0", group=True, wg="c0")
        S.dma("sp", cpar[:, CLB:CLB + 4], I["clb"][:, :], (), [t_const], key="c0", group=True, wg="c0")
        S.dma("sp", cpar[:, LBA:LBA + 4], I["lbaT"][:, :], (), [t_const], key="c0", group=True, wg="c0")
        S.dma("sp", cpar[:, LBX:LBX + 4], I["lbxT"][:, :], (), [t_const], key="c0", group=True, wg="c0")
        S.dma("sp", cpar[:, LAM:LAM + 4], I["lamT"][:, :], (), [t_const], key="c0", group=True, wg="c0")
        S.dma("sp", cfw[:], I["cfw"][:, :, :], (), [t_const], key="c0", group=True, wg="c0")
        S.dma("sp", cfb[:], I["cfb"][:, :], (), [t_const], key="c0", group=True, wg="c0")
        S.dma("sp", gs1[:], I["g1T"][:, :], (), [t_mod], key="c0", group=True, wg="c0")
        S.dma("sp", gs2[:], I["g2T"][:, :], (), [t_mod], key="c0", group=True, wg="c0")
        badd = sbuf(p0, "badd", [128, 48])
        t_badd = T()
        S.dma("sp", badd[:], I["b_adaT"][:, :], (), [t_badd], key="c0", group=True)
        S.copy("dve", identb[:], ident[:], [t_const], [t_const])
        S.dma("sp", cb[:], I["c_in"][0:1, :].partition_broadcast(128), (), [t_cb], key="c0", group=True)
        S.act(cb[:], cb[:], AF.Silu, [t_cb], [t_cb])
        S.dma("sp", lq[:].rearrange("p a b -> p (a b)"),
              I["lamq"][0:1, :].partition_broadcast(128), (), [t_lq], key="c0", group=True)
        for j in range(2):
            S.stt("dve", junk["dve"][:, 0:64], lq[:, 2 * j, :], 1.0, lq[:, 2 * j + 1, :], ALU.mult, ALU.mult,
                  [t_lq], [t_junk["dve"], t_ltmp], accum=ltmp[:, j:j + 1])
        S.act(ltmp[:, 2:4], ltmp[:, 0:2], AF.Exp, [t_ltmp], [t_ltmp])
        S.stt("dve", lam_t[:, 0:1], ltmp[:, 2:3], LAMBDA_INIT, ltmp[:, 3:4], ALU.add, ALU.subtract,
              [t_ltmp], [t_const])
        S.act(cpar[:, NSP:NSP + 4], cpar[:, LAM:LAM + 4], AF.Exp, [t_const], [t_const], scale=-1.0)
        S.act(cpar[:, NSP:NSP + 4], cpar[:, NSP:NSP + 4], AF.Ln, [t_const], [t_const], bias=1.0)
        S.ts("dve", cpar[:, NSP:NSP + 4], cpar[:, NSP:NSP + 4], -8.0, None, ALU.mult, reads=[t_const],
             writes=[t_const])
        for j in range(12):
            b = j % 2
            S.dma("sp", wst[b][:], I["w_adaT"][:, 4 * j:4 * j + 4, :], (), [t_wst[b]], key="wst%d" % b)
            for jj in range(4):
                e = "dve"
                ch = 4 * j + jj
                S.stt(e, junk[e][:], wst[b][:, jj, :], 1.0, cb[:], ALU.mult, ALU.mult,
                      [t_wst[b], t_cb], [t_junk[e], tmodc[ch]], accum=modT[:, ch:ch + 1])
        S.tt("dve", modT[:], modT[:], badd[:], ALU.add, [t_badd] + tmodc, [t_mod])
        S.stt("dve", gs1[:], modT[:, 8:16], 1.0, gs1[:], ALU.add, ALU.mult, [t_mod], [t_mod])
        S.stt("dve", gs2[:], modT[:, 32:40], 1.0, gs2[:], ALU.add, ALU.mult, [t_mod], [t_mod])
        S.tr(pt0[0:48, 0:128], modT[:, :], ident[:], [t_mod, t_const], [t_pt0])
        S.copy("dve", modsb[:], pt0[0:48, 0:128], [t_pt0], [t_modsb])
        S.dma("sp", mod_scr.ap().rearrange("o (a b) -> (o a) b", a=48), modsb[:], [t_modsb], [t_modscr], key="modst")
        if debug:
            S.dma("sp", dbg_out("modT", [128, 48]), modT[:], [t_mod], [], key="dbg", group=True)
        S.flush()
    sh1 = modT[:, 0:8]
    sh2 = modT[:, 24:32]
    if stop_after == "0":
        return finish(nc, es, S, out_d, dbg)

    with ExitStack() as p1:
        hT = sbuf(p1, "hT", [128, 8, 17 * 128], BF16)
        t_hT = [T() for _ in range(17)]
        wlru = sbuf(p1, "wlru", [128, 8, 1024], BF16)
        t_wlru = T()

        with ExitStack() as pa:
            wqkv = sbuf(pa, "wqkv", [128, 8, 1536], BF16)
            t_wqkv = T()
            wis = [sbuf(pa, "wis%d" % i, [128, D_IN]) for i in range(4)]
            t_wis = [T() for _ in range(4)]

            def load_w(kc):
                S.dma("act", wis[kc % 4][:], I["w_in"][kc * 128:(kc + 1) * 128, :], (), [t_wis[kc % 4]],
                      key="wis%d" % (kc % 4))
            for kc in range(4):
                load_w(kc)
            xt = [sbuf(pa, "xt%d" % i, [128, D]) for i in range(2)]
            xn = [sbuf(pa, "xn%d" % i, [128, D]) for i in range(2)]
            sq = sbuf(pa, "sq", [128, D])
            ss = sbuf(pa, "ss", [128, 40])
            stq = [sbuf(pa, "stq%d" % i, [128, 512], BF16) for i in range(3)]
            vst = [sbuf(pa, "vst%d" % i, [128, 4, 4, 130], BF16) for i in range(2)]
            t_xt, t_xn, t_sq, t_ss = [T(), T()], [T(), T()], T(), T()
            t_stq, t_vst = [T() for _ in range(3)], [T(), T()]
            pb = [psum(pa, "pa%d" % i, [128, 512]) for i in range(8)]
            t_pb = [T() for _ in range(8)]
            for i in range(2):
                S.memset("pool", vst[i][:, :, :, 128:129], 1.0, [t_vst[i]])
                S.memset("pool", vst[i][:, :, :, 129:130], 0.0, [t_vst[i]])
            Qv = a1f[QOFF:QOFF + 4 * 128 * 2048].rearrange("(h z p t) -> h z p t", h=4, z=2, p=128)
            Kv = a1f[KOFF:KOFF + 4 * 128 * 2048].rearrange("(h p t) -> h p t", h=4, p=128)
            Vv = a1f[VOFF:VOFF + 4 * 128 * 16 * 130].rearrange("(h p b c) -> p h b c", h=4, p=128, b=16)
            nq = 0

            def cast_w(kc):
                S.copy("dve", wqkv[:, kc, :], wis[kc % 4][:, 0:1536], [t_wis[kc % 4]], [t_wqkv], wg="w")
                S.copy("act", wlru[:, kc, :], wis[kc % 4][:, 1536:2560], [t_wis[kc % 4]], [t_wlru], wg="w")
                if kc + 4 < 8:
                    load_w(kc + 4)
            for ti in range(17):
                if 1 <= ti <= 4:
                    cast_w(2 * (ti - 1))
                    cast_w(2 * (ti - 1) + 1)
                b = ti % 2
                src = I["x_prev"][:, :] if ti == 0 else I["x_own"][(ti - 1) * 128:ti * 128, :]
                S.dma("sp", xt[b][:], src, (), [t_xt[b]], key="xt%d" % b)
                S.act(sq[:], xt[b][:], AF.Square, [t_xt[b]], [t_sq, t_ss], accum=ss[:, ti:ti + 1])
                S.ts("dve", ss[:, ti:ti + 1], ss[:, ti:ti + 1], 1.0 / D, EPS, ALU.mult, ALU.add, [t_ss], [t_ss])
                S.act(ss[:, ti:ti + 1], ss[:, ti:ti + 1], AF.Ln, [t_ss], [t_ss])
                S.act(ss[:, ti:ti + 1], ss[:, ti:ti + 1], AF.Exp, [t_ss], [t_ss], scale=-0.5)
                S.ts("dve", xn[b][:], xt[b][:], ss[:, ti:ti + 1], None, ALU.mult, reads=[t_xt[b], t_ss],
                     writes=[t_xn[b]])
                for half in range(2):
                    bank = (ti % 2) * 2 + half
                    for j in range(4):
                        kc = half * 4 + j
                        S.tr(pb[bank][:, j * 128:(j + 1) * 128], xn[b][:, kc * 128:(kc + 1) * 128], ident[:],
                             [t_xn[b], t_const], [t_pb[bank]])
                    for j in range(4):
                        kc = half * 4 + j
                        o_ap = hT[:, kc, ti * 128:(ti + 1) * 128]
                        i_ap = pb[bank][:, j * 128:(j + 1) * 128]
                        if j % 2 == 0:
                            S.ts("dve", o_ap, i_ap, gs1[:, kc:kc + 1], sh1[:, kc:kc + 1], ALU.mult, ALU.add,
                                 [t_pb[bank], t_mod], [t_hT[ti]])
                        else:
                            S.act(o_ap, i_ap, AF.Identity, [t_pb[bank], t_mod], [t_hT[ti]],
                                  bias=sh1[:, kc:kc + 1], scale=gs1[:, kc:kc + 1])
                if ti >= 1 and (ti - 1) % 4 == 3:
                    g = (ti - 1) // 4
                    c0 = 128 + g * 512
                    rd = [t_hT[1 + 4 * g + k] for k in range(4)] + [t_wqkv]
                    for cbk in range(8):
                        bank = 4 + (nq % 4)
                        nq += 1
                        S.mm(pb[bank][:, :], [(wqkv[:, kc, cbk * 128:(cbk + 1) * 128], hT[:, kc, c0:c0 + 512])
                                              for kc in range(8)], rd, [t_pb[bank]])
                        sb_i = nq % 3
                        S.copy("act" if cbk % 2 == 0 else "dve", stq[sb_i][:], pb[bank][:, :], [t_pb[bank]],
                               [t_stq[sb_i]])
                        if cbk < 4:
                            for z in range(2):
                                S.dma("sp", Qv[cbk, z, :, 2 * g * 128:(2 * g + 2) * 128].rearrange("p (a t) -> p a t", a=2),
                                      stq[sb_i][:].rearrange("p (a z t) -> p a z t", a=2, z=2)[:, :, z, :],
                                      [t_stq[sb_i]], [t_ag1in], key="stq%d_%d" % (sb_i, z), wg="ag1")
                        else:
                            S.dma("sp", Kv[cbk % 4, :, g * 512:(g + 1) * 512], stq[sb_i][:], [t_stq[sb_i]], [t_ag1in],
                                  key="stq%d" % sb_i, wg="ag1")
                    vb = g % 2
                    for k in range(4):
                        bank = 4 + (nq % 4)
                        nq += 1
                        tcol = c0 + k * 128
                        S.mm(pb[bank][:, :], [(hT[:, kc, tcol:tcol + 128], wqkv[:, kc, 1024:1536])
                                              for kc in range(8)], rd, [t_pb[bank]])
                        S.copy("act" if k % 2 == 0 else "dve", vst[vb][:, :, k, 0:128],
                               pb[bank][:, :].rearrange("p (h c) -> p h c", h=4), [t_pb[bank]], [t_vst[vb]])
                    S.dma("sp", Vv[:, :, 4 * g:4 * g + 4, :], vst[vb][:], [t_vst[vb]], [t_ag1in], key="vst%d" % vb, wg="ag1")
            S.op("pool", lambda E: E.collective_compute("AllGather", ALU.bypass, replica_groups=RG,
                                                       ins=[ag1_in.ap().opt()], outs=[ag1_out.ap().opt()]),
                 [t_ag1in], [t_ag1out], dma="cc1", cc=True)
            if debug:
                S.dma("sp", dbg_out("hT", [128, 8, 17 * 128], BF16), hT[:], t_hT, [], key="dbg", group=True)
                S.dma("sp", dbg_out("ag1in", [128, NB1 // 128], BF16), ag1_in.ap(), [t_ag1in], [], key="dbg", group=True)
            S.flush()
        if stop_after == "A":
            return finish(nc, es, S, out_d, dbg)

        with ExitStack() as pbo:
            hloc = sbuf(pbo, "hloc", [128, 4, TPC])
            pcum = sbuf(pbo, "pcum", [128, 4, TPC])
            t_hloc = [[T() for _ in range(4)] for _ in range(4)]
            with ExitStack() as pbs:
                wab = sbuf(pbs, "wab", [128, 4, 128], BF16)
                wxb = sbuf(pbs, "wxb", [128, 4, 128], BF16)
                t_wab = T()
                wabf = sbuf(pbs, "wabf", [128, 2, 4, 128])
                t_wabf = T()
                S.dma("sp", wabf[:, 0], I["waBD"][:, :, :], (), [t_wabf], key="wab", group=True, wg="w")
                S.dma("sp", wabf[:, 1], I["wxBD"][:, :, :], (), [t_wabf], key="wab", group=True, wg="w")
                S.copy("dve", wab[:], wabf[:, 0], [t_wabf], [t_wab], wg="w")
                S.copy("dve", wxb[:], wabf[:, 1], [t_wabf], [t_wab], wg="w")
                zeros = sbuf(pbs, "zeros", [128, 512])
                t_zero = T()
                S.memset("pool", zeros[:], 0.0, [t_zero])
                NB = 4
                xrs = [sbuf(pbs, "xrs%d" % i, [128, 515]) for i in range(NB)]
                xc = [sbuf(pbs, "xc%d" % i, [128, 512]) for i in range(NB)]
                xcb = [sbuf(pbs, "xcb%d" % i, [128, 512], BF16) for i in range(NB)]
                rr = [sbuf(pbs, "rr%d" % i, [128, 512]) for i in range(NB)]
                ii = [sbuf(pbs, "ii%d" % i, [128, 512]) for i in range(NB)]
                aa = [sbuf(pbs, "aa%d" % i, [128, 512]) for i in range(NB)]
                uu = [sbuf(pbs, "uu%d" % i, [128, 512]) for i in range(NB)]
                gg = [sbuf(pbs, "gg%d" % i, [128, 512]) for i in range(NB)]
                t_xrs, t_xc, t_xcb = [T() for _ in range(NB)], [T() for _ in range(NB)], [T() for _ in range(NB)]
                t_rr, t_ii, t_aa = [T() for _ in range(NB)], [T() for _ in range(NB)], [T() for _ in range(NB)]
                t_uu, t_gg = [T() for _ in range(NB)], [T() for _ in range(NB)]
                xtail = sbuf(pbs, "xtail", [128, 4, 3])
                st_h = sbuf(pbs, "st_h", [128, 4])
                st_p = sbuf(pbs, "st_p", [128, 4])
                t_xtail, t_st = [T() for _ in range(4)], [T() for _ in range(4)]
                pq = [psum(pbs, "pq%d" % i, [128, 512]) for i in range(8)]
                t_pq = [T() for _ in range(8)]
                S.memset("dve", st_h[:], 0.0, t_st)
                S.memset("dve", st_p[:], 1.0, t_st)

                def bset(it):
                    k = (it % 2) * 4
                    return pq[k:k + 4], t_pq[k:k + 4]

                def b1(it):
                    g, cg = divmod(it, 4)
                    b = it % NB
                    (PX, PY, PR, PI), (tPX, tPY, tPR, tPI) = bset(it)
                    c0 = 128 + g * 512
                    rd = [t_hT[1 + 4 * g + k] for k in range(4)] + [t_wlru]
                    S.mm(PX[:, :], [(wlru[:, kc, cg * 128:(cg + 1) * 128], hT[:, kc, c0:c0 + 512])
                                    for kc in range(8)], rd, [tPX])
                    S.mm(PY[:, :], [(wlru[:, kc, 512 + cg * 128:512 + (cg + 1) * 128], hT[:, kc, c0:c0 + 512])
                                    for kc in range(8)], rd, [tPY])
                    if g == 0:
                        S.mm(PR[:, 0:3], [(wlru[:, kc, cg * 128:(cg + 1) * 128], hT[:, kc, 125:128])
                                          for kc in range(8)], [t_hT[0], t_wlru], [tPR])
                        S.ts("dve", xrs[b][:, 0:3], PR[:, 0:3], flags[:, 0:1], None, ALU.mult,
                             reads=[tPR, t_const], writes=[t_xrs[b]])
                    else:
                        S.copy("pool", xrs[b][:, 0:3], xtail[:, cg, :], [t_xtail[cg]], [t_xrs[b]])
                    S.copy("act", xrs[b][:, 3:515], PX[:, :], [tPX], [t_xrs[b]])
                    S.copy("pool", xtail[:, cg, :], xrs[b][:, 512:515], [t_xrs[b]], [t_xtail[cg]])
                    w = lambda j: cpar[:, CLW + cg * 4 + j:CLW + cg * 4 + j + 1]
                    S.act(xc[b][:], xrs[b][:, 3:515], AF.Identity, [t_xrs[b], t_const], [t_xc[b]],
                          bias=cpar[:, CLB + cg:CLB + cg + 1], scale=w(3))
                    for j in range(3):
                        S.stt("dve", xc[b][:], xrs[b][:, j:j + 512], w(j), xc[b][:],
                              ALU.mult, ALU.add, [t_xrs[b], t_const], [t_xc[b]])
                    S.copy("pool", xcb[b][:], xc[b][:], [t_xc[b]], [t_xcb[b]])

                def b2(it):
                    g, cg = divmod(it, 4)
                    b = it % NB
                    (PX, PY, PR, PI), (tPX, tPY, tPR, tPI) = bset(it)
                    S.mm(PR[:, :], [(wab[:, cg, :], xcb[b][:])], [t_wab, t_xcb[b]], [tPR])
                    S.mm(PI[:, :], [(wxb[:, cg, :], xcb[b][:])], [t_wab, t_xcb[b]], [tPI])
                    S.act(rr[b][:], PR[:, :], AF.Sigmoid, [tPR, t_const], [t_rr[b]],
                          bias=cpar[:, LBA + cg:LBA + cg + 1])
                    S.act(ii[b][:], PI[:, :], AF.Sigmoid, [tPI, t_const], [t_ii[b]],
                          bias=cpar[:, LBX + cg:LBX + cg + 1])
                    S.act(uu[b][:], PY[:, :], AF.Square, [tPY], [t_uu[b]])
                    S.ts("dve", uu[b][:], uu[b][:], GELU_C, 1.0, ALU.mult, ALU.add, [t_uu[b]], [t_uu[b]])
                    S.tt("dve", uu[b][:], uu[b][:], PY[:, :], ALU.mult, [t_uu[b], tPY], [t_uu[b]])
                    S.act(uu[b][:], uu[b][:], AF.Sigmoid, [t_uu[b]], [t_uu[b]], scale=GELU_K)
                    S.tt("dve", gg[b][:], uu[b][:], PY[:, :], ALU.mult, [t_uu[b], tPY], [t_gg[b]])
                    S.act(aa[b][:], rr[b][:], AF.Exp, [t_rr[b], t_const], [t_aa[b]],
                          scale=cpar[:, NSP + cg:NSP + cg + 1])

                def b3(it):
                    g, cg = divmod(it, 4)
                    b = it % NB
                    om_, t_om_ = rr[b], t_rr[b]
                    S.tt("pool", om_[:], aa[b][:], aa[b][:], ALU.mult, [t_aa[b]], [t_om_])
                    S.ts("dve", om_[:], om_[:], -1.0, 1.0, ALU.mult, ALU.add, [t_om_], [t_om_])
                    S.ts("dve", om_[:], om_[:], 0.0, None, ALU.max, reads=[t_om_], writes=[t_om_])
                    S.act(om_[:], om_[:], AF.Ln, [t_om_], [t_om_])
                    S.act(om_[:], om_[:], AF.Exp, [t_om_], [t_om_], scale=0.5)
                    S.tt("pool", ii[b][:], ii[b][:], xc[b][:], ALU.mult, [t_ii[b], t_xc[b]], [t_ii[b]])
                    S.tt("dve", ii[b][:], ii[b][:], om_[:], ALU.mult, [t_ii[b], t_om_], [t_ii[b]])
                    hl = hloc[:, cg, g * 512:(g + 1) * 512]
                    pc = pcum[:, cg, g * 512:(g + 1) * 512]
                    S.scan("dve", hl, aa[b][:], ii[b][:], st_h[:, cg:cg + 1], [t_aa[b], t_ii[b], t_st[cg]],
                           [t_hloc[cg][g]])
                    S.scan("dve", pc, aa[b][:], zeros[:], st_p[:, cg:cg + 1], [t_aa[b], t_zero, t_st[cg]],
                           [t_hloc[cg][g]])
                    S.copy("pool", st_h[:, cg:cg + 1], hloc[:, cg, g * 512 + 511:g * 512 + 512],
                           [t_hloc[cg][g]], [t_st[cg]])
                    S.copy("pool", st_p[:, cg:cg + 1], pcum[:, cg, g * 512 + 511:g * 512 + 512],
                           [t_hloc[cg][g]], [t_st[cg]])
                    S.tt("pool", hl, hl, gg[b][:], ALU.mult, [t_hloc[cg][g], t_gg[b], t_st[cg]],
                         [t_hloc[cg][g]])
                    S.tt("dve", pc, pc, gg[b][:], ALU.mult, [t_hloc[cg][g], t_gg[b], t_st[cg]],
                         [t_hloc[cg][g]])

                for t in range(16 + 3):
                    if t < 16:
                        b1(t)
                    if 0 <= t - 1 < 16:
                        b2(t - 1)
                    if 0 <= t - 3 < 16:
                        b3(t - 3)
                stg = sbuf(pbs, "stg", [128, 8])
                t_stg = T()
                S.copy("dve", stg[:, 0:4], st_p[:], t_st, [t_stg])
                S.copy("dve", stg[:, 4:8], st_h[:], t_st, [t_stg])
                S.dma("sp", ag2_in.ap(), stg[:], [t_stg], [t_ag2in], key="ag2st")
                S.op("pool", lambda E: E.collective_compute("AllGather", ALU.bypass, replica_groups=RG,
                                                           ins=[ag2_in.ap().opt()], outs=[ag2_out.ap().opt()]),
                     [t_ag2in], [t_ag2out], dma="cc2", cc=True)
                if debug:
                    S.dma("sp", dbg_out("hlocG", [128, 4, TPC]), hloc[:], [x for y in t_hloc for x in y], [], key="dbg", group=True)
                S.flush()
            with ExitStack() as pf:
                car = sbuf(pf, "car", [128, 8, 8])
                pre = sbuf(pf, "pre", [128, 4, 8])
                cin = sbuf(pf, "cin", [128, 4])
                jk = sbuf(pf, "jk", [128, 8])
                t_car, t_pre, t_cin, t_jk = T(), T(), T(), T()
                S.dma("sp", car[:], ag2_out.ap().rearrange("(r p) c -> p r c", r=8), [t_ag2out], [t_car], key="car")
                for cg in range(4):
                    S.scan("dve", pre[:, cg, :], car[:, :, cg], car[:, :, 4 + cg], 0.0, [t_car], [t_pre])
                    S.stt("dve", jk[:], pre[:, cg, :], 1.0, selp[:], ALU.mult, ALU.mult, [t_pre, t_const],
                          [t_jk, t_cin], accum=cin[:, cg:cg + 1])
                for cg in range(4):
                    for hf in range(2):
                        sl = slice(hf * 1024, (hf + 1) * 1024)
                        S.stt("dve", lru_out[:, cg, sl], pcum[:, cg, sl], cin[:, cg:cg + 1],
                              hloc[:, cg, sl], ALU.mult, ALU.add, [t_cin] + t_hloc[cg], [t_lru[cg]])
                if debug:
                    S.dma("sp", dbg_out("lru_out", [128, 4, TPC], BF16), lru_out[:], t_lru, [], key="dbg", group=True)
                S.flush()
    if stop_after == "B":
        return finish(nc, es, S, out_d, dbg)

    pw = ExitStack()
    wupA = sbuf(pw, "wupA", [128, 4, 2 * D_FF], BF16)
    t_wup = [T() for _ in range(8)]
    WPC = 2048

    def wup_piece(j, wus, t_wus, wdst):
        kc, pc_ = j // 3, (j % 3) * WPC
        gk = kc if wdst is wupA else kc + 4
        S.dma("act", wus[j % 2][:], I["w_up"][gk * 128:(gk + 1) * 128, pc_:pc_ + WPC], (), [t_wus[j % 2]],
              key="wus%d" % (j % 2))
        S.copy("dve", wdst[:, kc, pc_:pc_ + WPC], wus[j % 2][:], [t_wus[j % 2]], [t_wup[gk]], wg=("wup", gk))

    with ExitStack() as p2:
        wus2 = [sbuf(p2, "wus2_%d" % i, [128, WPC]) for i in range(2)]
        t_wus2 = [T(), T()]
        KT = sbuf(p2, "KT", [128, S_TOT], BF16)
        Vt = sbuf(p2, "Vt", [128, 128, 130], BF16)
        QT = sbuf(p2, "QT", [128, NQB, 128], BF16)
        t_kv = [T() for _ in range(8)]
        t_q = [T() for _ in range(8)]
        bN = sbuf(p2, "bN", [128, 3, 128])
        bM = sbuf(p2, "bM", [128, 3, 128])
        bfar = sbuf(p2, "bfar", [128, 1])
        gsub = sbuf(p2, "gsub", [128, 128])
        t_b = T()
        S.dma("sp", bN[:], I["bias_g"][:, :, :], (), [t_b], key="p2c", group=True, wg="c")
        S.dma("sp", bM[:], I["bias_m"][:, :, :], (), [t_b], key="p2c", group=True, wg="c")
        S.dma("sp", bfar[:], I["bfar"][:, :], (), [t_b], key="p2c", group=True, wg="c")
        S.dma("sp", gsub[:], I["g_subln"][0:1, :].partition_broadcast(128), (), [t_b], key="p2c", group=True, wg="c")
        S.tt("dve", bN[:], bN[:], bM[:], ALU.add, [t_b], [t_b])
        S.ts("dve", gsub[:], gsub[:], 1.0 - LAMBDA_INIT, None, ALU.mult, reads=[t_b], writes=[t_b])
        for r in range(8):
            def ldk(E, r=r):
                p = pid_of(E, "p2")
                base = (p // 2) * (128 * 2048) + (KOFF + r * NB1)
                src = bass.AP(g1f.tensor, base, [[2048, 128], [1, 2048]])
                return E.dma_start(out=KT[:, r * 2048:(r + 1) * 2048], in_=src)

            def ldv(E, r=r):
                p = pid_of(E, "p2")
                base = (p // 2) * (128 * 16 * 130) + (VOFF + r * NB1)
                src = bass.AP(g1f.tensor, base, [[16 * 130, 128], [1, 16 * 130]])
                return E.dma_start(out=Vt[:, r * 16:(r + 1) * 16, :].rearrange("p b c -> p (b c)"), in_=src)

            def ldq(E, r=r):
                p = pid_of(E, "p2")
                base = (p // 2) * (2 * 128 * 1024) + (p % 2) * (128 * 1024) + (QOFF + r * NB1)
                src = bass.AP(g1f.tensor, base, [[1024, 128], [1, 1024]])
                return E.dma_start(out=QT[:, r * 8:(r + 1) * 8, :].rearrange("p b t -> p (b t)"), in_=src)
            S.dmaf("sp", ldq, [t_ag1out], [t_q[r]], key="qg", group=True)
            S.dmaf("act", ldk, [t_ag1out], [t_kv[r]], key="kg", group=True, wg="kv")
            S.dmaf("pool", ldv, [t_ag1out], [t_kv[r]], key="vg", group=True, wg="kv")

        NSB = 3
        psS = [psum(p2, "psS%d" % i, [128, 2, 512]) for i in range(NSB)]
        acc = [psum(p2, "acc%d" % i, [128, 512]) for i in range(2)]
        t_psS, t_acc = [T() for _ in range(NSB)], [T(), T()]
        NPB = 3
        PT = [sbuf(p2, "PT%d" % i, [128, 2, 512], BF16) for i in range(NPB)]
        t_PT = [T() for _ in range(NPB)]
        tmpn = sbuf(p2, "tmpn", [128, 2, 384])
        t_tmpn = T()
        attn = sbuf(p2, "attn", [128, NQB, 128], BF16)
        t_attn = [T() for _ in range(4)]
        o1 = [sbuf(p2, "o1_%d" % i, [128, 128]) for i in range(2)]
        osm = [sbuf(p2, "osm%d" % i, [128, 8]) for i in range(2)]
        ojk = sbuf(p2, "ojk", [128, 128])
        t_o1, t_osm, t_ojk = [T(), T()], [T(), T()], T()
        items = []
        for m in range(NQB):
            far = list(range(0, max(2 * m - 1, 0)))
            near = [kb for kb in (2 * m - 1, 2 * m, 2 * m + 1) if kb >= 0]
            groups = [(far[i:i + 4], False) for i in range(0, len(far), 4)] + [(near, True)]
            for gidx, (kbs, is_near) in enumerate(groups):
                items.append((m, kbs, is_near, gidx == 0, gidx == len(groups) - 1))
        NIT = len(items)
        LA = 2

        def st1(t):
            m, kbs, is_near, first, last = items[t]
            sb_i = t % NSB
            rd = [t_kv[r] for r in sorted({kb // 16 for kb in kbs})] + [t_q[m // 8]]

            def qk(E):
                res = []
                for jj, kb in enumerate(kbs):
                    for mp in range(2):
                        res.append(E.matmul(psS[sb_i][:, mp, jj * 128:(jj + 1) * 128],
                                            lhsT=KT[64 * mp:64 * mp + 64, kb * 128:(kb + 1) * 128],
                                            rhs=QT[64 * mp:64 * mp + 64, m, :], start=True, stop=True))
                return res
            S.op("pe", qk, rd, [t_psS[sb_i]])

        def st2(t):
            m, kbs, is_near, first, last = items[t]
            n = len(kbs)
            sb_i, pb_i = t % NSB, t % NPB
            if not is_near:
                S.act(PT[pb_i][:, :, 0:n * 128], psS[sb_i][:, :, 0:n * 128], AF.Exp, [t_psS[sb_i], t_b],
                      [t_PT[pb_i]], bias=bfar[:, 0:1], scale=0.125)
            else:
                t0 = 3 - n
                for mp in range(2):
                    S.stt("dve", tmpn[:, mp, 0:n * 128], psS[sb_i][:, mp, 0:n * 128], 0.125,
                          bN[:, t0:3, :].rearrange("p a b -> p (a b)"), ALU.mult, ALU.add,
                          [t_psS[sb_i], t_b], [t_tmpn])
                S.act(PT[pb_i][:, :, 0:n * 128], tmpn[:, :, 0:n * 128], AF.Exp, [t_tmpn], [t_PT[pb_i]])

        def st3(t):
            m, kbs, is_near, first, last = items[t]
            pb_i, ab = t % NPB, m % 2
            rdk = sorted({kb // 16 for kb in kbs})

            def pv(E):
                res = []
                for jj, kb in enumerate(kbs):
                    for mp in range(2):
                        stt_ = first and jj == 0 and mp == 0
                        res.append(E.matmul(acc[ab][:, mp * 130:mp * 130 + 129],
                                            lhsT=PT[pb_i][:, mp, jj * 128:(jj + 1) * 128],
                                            rhs=Vt[:, kb, 0:129], start=stt_, stop=False,
                                            skip_group_check=True))
                return res
            S.op("pe", pv, [t_PT[pb_i]] + [t_kv[r] for r in rdk], [t_acc[ab]])
            return last

        def fin_a(m):
            ab = m % 2
            A = acc[ab]
            den = A[:, 128:128 + 131:130]
            S.op("dve", lambda E: E.reciprocal(out=osm[ab][:, 0:2], in_=den), [t_acc[ab]], [t_osm[ab]])
            S.ts("dve", osm[ab][:, 2:3], osm[ab][:, 1:2], lam_t[:, 0:1], -1.0, ALU.mult, ALU.mult,
                 [t_osm[ab], t_const], [t_osm[ab]])
            S.ts("dve", o1[ab][:], A[:, 0:128], osm[ab][:, 0:1], None, ALU.mult, reads=[t_acc[ab], t_osm[ab]],
                 writes=[t_o1[ab]])
            S.stt("dve", o1[ab][:], A[:, 130:258], osm[ab][:, 2:3], o1[ab][:], ALU.mult, ALU.add,
                  [t_acc[ab], t_osm[ab], t_o1[ab]], [t_o1[ab]])
            S.stt("dve", ojk[:], o1[ab][:], 1.0, o1[ab][:], ALU.mult, ALU.mult, [t_o1[ab]], [t_ojk, t_osm[ab]],
                  accum=osm[ab][:, 3:4])
            S.ts("dve", osm[ab][:, 3:4], osm[ab][:, 3:4], 1.0 / 128, EPS, ALU.mult, ALU.add, [t_osm[ab]],
                 [t_osm[ab]])

        def ag3_part(pp):
            S.dma("sp", ag3_in[pp].ap().rearrange("(m q) d -> q m d", q=128), attn[:, 16 * pp:16 * pp + 16, :],
                  [t_attn[pp]], [t_ag3in[pp]], key="attst%d" % pp)
            S.op("pool", lambda E: E.collective_compute(
                "AllGather", ALU.bypass, replica_groups=RG, ins=[ag3_in[pp].ap().opt()],
                outs=[ag3_out.ap()[pp * 8 * 2048:(pp + 1) * 8 * 2048, :].opt()]),
                [t_ag3in[pp]], [t_ag3out], dma="cc3_%d" % pp, cc=True, wg="ag3")

        def fin_b(m):
            ab = m % 2
            S.act(osm[ab][:, 3:4], osm[ab][:, 3:4], AF.Ln, [t_osm[ab]], [t_osm[ab]])
            S.act(osm[ab][:, 3:4], osm[ab][:, 3:4], AF.Exp, [t_osm[ab]], [t_osm[ab]], scale=-0.5)
            S.stt("dve", attn[:, m, :], o1[ab][:], osm[ab][:, 3:4], gsub[:], ALU.mult, ALU.mult,
                  [t_o1[ab], t_osm[ab], t_b], [t_attn[m // 16]], wg="attn")
            if m % 16 == 15:
                ag3_part(m // 16)

        deferred = {}
        for t in range(min(LA, NIT)):
            st1(t)
        for t in range(NIT):
            if t + LA < NIT:
                st1(t + LA)
            st2(t)
            for mm in deferred.pop(t, []):
                fin_b(mm)
            if st3(t):
                mdone = items[t][0]
                fin_a(mdone)
                deferred.setdefault(t + 3, []).append(mdone)
                if mdone >= 4 and mdone % 4 == 0 and (mdone - 4) // 4 < 12:
                    wup_piece((mdone - 4) // 4, wus2, t_wus2, wupA)
        for t in sorted(deferred):
            for mm in deferred[t]:
                fin_b(mm)
        if debug:
            S.dma("sp", dbg_out("attn", [128, NQB, 128], BF16), attn[:], t_attn, [], key="dbg", group=True)
        S.flush()
    if stop_after == "2":
        return finish(nc, es, S, out_d, dbg)

    with ExitStack() as p3:
        wupB = sbuf(p3, "wupB", [128, 4, 2 * D_FF], BF16)

        def wupk(kc, c0, c1):
            return wupA[:, kc, c0:c1] if kc < 4 else wupB[:, kc - 4, c0:c1]
        ahalo = sbuf(p3, "ahalo", [128, 24, 2])
        t_ahalo = [T() for _ in range(24)]
        with ExitStack() as p3a:
            wo = sbuf(p3a, "wo", [128, 8, D], BF16)
            g1b = sbuf(p3a, "g1b", [128, D])
            wus3 = [sbuf(p3a, "wus3_%d" % i, [128, WPC]) for i in range(2)]
            t_wus3 = [T(), T()]
            h2last = sbuf(p3a, "h2last", [128, 8, 2], BF16)
            t_h2last = T()
            t_wo, t_g1b = T(), T()
            S.dma("sp", g1b[:], mod_scr.ap()[0:1, 2048:3072].partition_broadcast(128), [t_modscr], [t_g1b], key="g1b")
            for kc in range(8):
                b = kc % 2
                S.dma("act", wus3[b][:, 0:D], I["w_out"][kc * 128:(kc + 1) * 128, :], (), [t_wus3[b]], key="wus%d" % b)
                S.tt("dve", wo[:, kc, :], wus3[b][:, 0:D], g1b[:], ALU.mult, [t_wus3[b], t_g1b], [t_wo], wg="wo")
            att_all = sbuf(p3a, "att_all", [128, NT, 512], BF16)
            t_attall = T()
            for par in range(2):
                for h in range(4):
                    def lda(E, par=par, h=h):
                        p = pid_of(E, "p3")
                        base = (p // 2) * (8 * 16 * 16384) + (p % 2) * (8 * 16384) + (par + 2 * h) * (16 * 16384)
                        src = bass.AP(ag3_out.ap().tensor, base, [[128, 128], [128 * 128, 8], [1, 128]])
                        dst = att_all[:, :, :].rearrange("q (mm two) (h d) -> q mm two h d", two=2, h=4)[:, :, par, h, :]
                        return E.dma_start(out=dst, in_=src)
                    S.dmaf("act", lda, [t_ag3out], [t_attall], key="attg", group=True, wg="att")
            attT = [sbuf(p3a, "attT%d" % i, [128, 4, 128], BF16) for i in range(2)]
            xt3 = [sbuf(p3a, "xt3_%d" % i, [128, D]) for i in range(2)]
            xm = [sbuf(p3a, "xm%d" % i, [128, D]) for i in range(2)]
            xn3 = [sbuf(p3a, "xn3_%d" % i, [128, D]) for i in range(2)]
            h2t = [sbuf(p3a, "h2t%d" % i, [128, 8, 128], BF16) for i in range(2)]
            ss3 = sbuf(p3a, "ss3", [128, 16])
            t_att, t_attT, t_xt3, t_xm = [T(), T()], [T(), T()], [T(), T()], [T(), T()]
            t_xn3, t_h2t, t_sq3, t_ss3 = [T(), T()], [T(), T()], T(), T()
            pT = psum(p3a, "pT", [128, 1024], BF16)
            pm = [psum(p3a, "pm%d" % i, [128, 512]) for i in range(4)]
            ptr = [psum(p3a, "ptr%d" % i, [128, 512]) for i in range(2)]
            pha = psum(p3a, "pha", [128, 512])
            t_pT, t_pm, t_ptr, t_pha = T(), [T() for _ in range(4)], [T(), T()], T()
            order = [15] + list(range(15))
            def st_a(n_i):
                tt = order[n_i]
                b = n_i % 2
                S.dma("sp", xt3[b][:], I["x_own"][tt * 128:(tt + 1) * 128, :], (), [t_xt3[b]], key="xt3_%d" % b)
                if n_i < 12:
                    wup_piece(n_i, wus3, t_wus3, wupB)
                for h in range(4):
                    S.tr(pT[:, h * 128:(h + 1) * 128], att_all[:, tt, h * 128:(h + 1) * 128], identb[:],
                         [t_attall, t_const], [t_pT], wg=("pT", n_i))
                S.copy("act", attT[b][:].rearrange("p a b -> p (a b)"), pT[:, 0:512], [t_pT], [t_attT[b]])
                for half in range(2):
                    items = [(lru_out[:, cg, tt * 128:(tt + 1) * 128], wo[:, cg, half * 512:(half + 1) * 512])
                             for cg in range(4)]
                    items += [(attT[b][:, h, :], wo[:, 4 + h, half * 512:(half + 1) * 512]) for h in range(4)]
                    pk = (n_i % 2) * 2 + half
                    S.mm(pm[pk][:, :], items, t_lru + [t_attT[b], t_wo], [t_pm[pk]])
                    S.tt("dve", xm[b][:, half * 512:(half + 1) * 512], pm[pk][:, :],
                         xt3[b][:, half * 512:(half + 1) * 512], ALU.add, [t_pm[pk], t_xt3[b]], [t_xm[b]])
                S.dma("sp", xmid_scr.ap()[tt * 128:(tt + 1) * 128, :], xm[b][:], [t_xm[b]], [t_xmid[tt]], key="xmst%d" % b)

            def st_b(n_i):
                tt = order[n_i]
                b = n_i % 2
                S.act(xn3[b][:], xm[b][:], AF.Square, [t_xm[b]], [t_xn3[b], t_ss3], accum=ss3[:, tt:tt + 1])
                S.ts("dve", ss3[:, tt:tt + 1], ss3[:, tt:tt + 1], 1.0 / D, EPS, ALU.mult, ALU.add, [t_ss3], [t_ss3])
                S.act(ss3[:, tt:tt + 1], ss3[:, tt:tt + 1], AF.Ln, [t_ss3], [t_ss3])
                S.act(ss3[:, tt:tt + 1], ss3[:, tt:tt + 1], AF.Exp, [t_ss3], [t_ss3], scale=-0.5)
                S.ts("dve", xn3[b][:], xm[b][:], ss3[:, tt:tt + 1], None, ALU.mult, reads=[t_xm[b], t_ss3],
                     writes=[t_xn3[b]])
                for half in range(2):
                    for j in range(4):
                        kc = half * 4 + j
                        S.tr(ptr[half][:, j * 128:(j + 1) * 128], xn3[b][:, kc * 128:(kc + 1) * 128], ident[:],
                             [t_xn3[b], t_const], [t_ptr[half]])
                    for j in range(4):
                        kc = half * 4 + j
                        if j % 2 == 0:
                            S.ts("dve", h2t[b][:, kc, :], ptr[half][:, j * 128:(j + 1) * 128], gs2[:, kc:kc + 1],
                                 sh2[:, kc:kc + 1], ALU.mult, ALU.add, [t_ptr[half], t_mod], [t_h2t[b]])
                        else:
                            S.act(h2t[b][:, kc, :], ptr[half][:, j * 128:(j + 1) * 128], AF.Identity,
                                  [t_ptr[half], t_mod], [t_h2t[b]], bias=sh2[:, kc:kc + 1], scale=gs2[:, kc:kc + 1])
                S.dma("sp", h2_scr.ap()[:, :, tt * 128:(tt + 1) * 128].rearrange("k p t -> p k t"), h2t[b][:],
                      [t_h2t[b]], [t_h2scr[tt]], key="h2st%d" % b)
                if tt == 15:
                    S.copy("dve", h2last[:], h2t[b][:, :, 126:128], [t_h2t[b]], [t_h2last])
                if n_i == 13:
                    for c in range(24):
                        S.mm(pha[:, 2 * c:2 * c + 2], [(wupk(kc, c * 128, (c + 1) * 128), h2last[:, kc, :])
                                                      for kc in range(8)], t_wup + [t_h2last], [t_pha])
                    hst = sbuf(p3a, "hst", [128, 48])
                    t_hst = T()
                    S.copy("dve", hst[:], pha[:, 0:48], [t_pha], [t_hst])
                    S.dma("sp", ag4_in.ap(), hst[:], [t_hst], [t_ag4in], key="ag4st")
                    S.op("pool", lambda E: E.collective_compute("AllGather", ALU.bypass, replica_groups=RG,
                                                               ins=[ag4_in.ap().opt()], outs=[ag4_out.ap().opt()]),
                         [t_ag4in], [t_ag4out], dma="cc4", cc=True)

            for t in range(17):
                if t < 16:
                    st_a(t)
                if t >= 1:
                    st_b(t - 1)

            def ldh(E):
                p = pid_of(E, "p3")
                return E.dma_start(out=ahalo[:].rearrange("p a b -> p (a b)"),
                                   in_=ag4_out.ap()[ds(((p + 7) % 8) * 128, 128), :])
            S.dmaf("pool", ldh, [t_ag4out], t_ahalo, key="ahalo")
            S.ts("dve", ahalo[:].rearrange("p a b -> p (a b)"), ahalo[:].rearrange("p a b -> p (a b)"),
                 flags[:, 0:1], None, ALU.mult, reads=t_ahalo + [t_const], writes=t_ahalo)
            if debug:
                S.dma("sp", dbg_out("xmid", [TPC, D]), xmid_scr.ap(), t_xmid, [], key="dbg", group=True)
            S.flush()
        if stop_after == "3a":
            return finish(nc, es, S, out_d, dbg)

        with ExitStack() as p3b:
            wd = sbuf(p3b, "wd", [128, 24, D], BF16)
            t_wd = [T() for _ in range(24)]
            g2b = sbuf(p3b, "g2b", [128, D])
            gfb = sbuf(p3b, "gfb", [128, D])
            wsd = [sbuf(p3b, "wsd%d" % i, [128, D]) for i in range(2)]
            t_g2b, t_gfb, t_wsd = T(), T(), [T(), T()]
            S.dma("sp", g2b[:], mod_scr.ap()[0:1, 5120:6144].partition_broadcast(128), [t_modscr], [t_g2b], key="g2b")
            S.dma("sp", gfb[:], I["g_final"][0:1, :].partition_broadcast(128), (), [t_gfb], key="gfb")
            def load_wd(c):
                b = c % 2
                S.dma("act", wsd[b][:], I["w_down"][c * 128:(c + 1) * 128, :], (), [t_wsd[b]], key="wsd%d" % b)
                S.tt("dve", wd[:, c, :], wsd[b][:], g2b[:], ALU.mult, [t_wsd[b], t_g2b], [t_wd[c]])
            h2g = [sbuf(p3b, "h2g%d" % i, [128, 8, 256], BF16) for i in range(2)]
            xg = sbuf(p3b, "xg", [128, 2, D])
            t_h2g, t_xg = [T(), T()], [T(), T()]
            NE = 5
            yy = [sbuf(p3b, "yy%d" % i, [128, 256]) for i in range(NE)]
            u3 = [sbuf(p3b, "u3_%d" % i, [128, 256]) for i in range(NE)]
            actT = [sbuf(p3b, "actT%d" % i, [128, 256], BF16) for i in range(NE)]
            t_yy, t_u3, t_actT = [T() for _ in range(NE)], [T() for _ in range(NE)], [T() for _ in range(NE)]
            ssf = sbuf(p3b, "ssf", [128, 16])
            t_ssf = T()
            pff = [psum(p3b, "pff%d" % i, [128, 512]) for i in range(4)]
            pag = [psum(p3b, "pag%d" % i, [128, 512]) for i in range(4)]
            t_pff, t_pag = [T() for _ in range(4)], [T() for _ in range(4)]
            NITF = 8 * 24

            def f1(it):
                gix, c = divmod(it, 24)
                hb, e, k = gix % 2, it % NE, it % 4
                if c == 0:
                    t0 = gix * 256
                    S.dma("sp", h2g[hb][:], h2_scr.ap()[:, :, t0:t0 + 256].rearrange("k p t -> p k t"),
                          [t_h2scr[2 * gix], t_h2scr[2 * gix + 1]], [t_h2g[hb]], key="h2g%d" % hb)
                if it < 24:
                    load_wd(it)
                pa_, pg_, tp = pag[k][:, 0:256], pag[k][:, 256:512], t_pag[k]
                S.mm(pa_, [(wupk(kc, c * 128, (c + 1) * 128), h2g[hb][:, kc, :]) for kc in range(8)],
                     t_wup + [t_h2g[hb]], [tp])
                S.mm(pg_, [(wupk(kc, D_FF + c * 128, D_FF + (c + 1) * 128), h2g[hb][:, kc, :])
                           for kc in range(8)], t_wup + [t_h2g[hb]], [tp], wg=("ag", it))
                w = lambda j: cfw[:, c, j:j + 1]
                S.act(yy[e][:], pa_, AF.Identity, [tp, t_const], [t_yy[e]], bias=cfb[:, c:c + 1], scale=w(2))
                S.stt("dve", yy[e][:, 1:256], pag[k][:, 0:255], w(1), yy[e][:, 1:256], ALU.mult, ALU.add,
                      [tp, t_const], [t_yy[e]])
                S.stt("dve", yy[e][:, 2:256], pag[k][:, 0:254], w(0), yy[e][:, 2:256], ALU.mult, ALU.add,
                      [tp, t_const], [t_yy[e]])
                S.stt("dve", yy[e][:, 0:1], ahalo[:, c, 1:2], w(1), yy[e][:, 0:1], ALU.mult, ALU.add,
                      [t_ahalo[c], t_const], [t_yy[e]])
                S.stt("dve", yy[e][:, 0:2], ahalo[:, c, 0:2], w(0), yy[e][:, 0:2], ALU.mult, ALU.add,
                      [t_ahalo[c], t_const], [t_yy[e]])
                S.copy("act", ahalo[:, c, :], pag[k][:, 254:256], [tp, t_yy[e]], [t_ahalo[c]])

            def f2(it):
                gix, c = divmod(it, 24)
                e, k = it % NE, it % 4
                S.tt("pool", u3[e][:], yy[e][:], yy[e][:], ALU.mult, [t_yy[e]], [t_u3[e]])
                S.ts("pool", u3[e][:], u3[e][:], GELU_C, 1.0, ALU.mult, ALU.add, [t_u3[e]], [t_u3[e]])
                S.tt("pool", u3[e][:], u3[e][:], yy[e][:], ALU.mult, [t_u3[e], t_yy[e]], [t_u3[e]])
                S.act(u3[e][:], u3[e][:], AF.Sigmoid, [t_u3[e]], [t_u3[e]], scale=GELU_K)
                S.tt("dve", u3[e][:], u3[e][:], yy[e][:], ALU.mult, [t_u3[e], t_yy[e]], [t_u3[e]])
                S.tt("dve", actT[e][:], u3[e][:], pag[k][:, 256:512], ALU.mult, [t_u3[e], t_pag[k]], [t_actT[e]])

            def f3(it):
                gix, c = divmod(it, 24)
                e = it % NE

                def dn(E):
                    res = []
                    for tb in range(2):
                        for half in range(2):
                            res.append(E.matmul(pff[tb * 2 + half][:, :], lhsT=actT[e][:, tb * 128:(tb + 1) * 128],
                                                rhs=wd[:, c, half * 512:(half + 1) * 512],
                                                start=(c == 0), stop=(c == 23)))
                    return res
                S.op("pe", dn, [t_actT[e], t_wd[c]], t_pff)
                if c == 23:
                    ffin(gix)

            def ffin(gix):
                for tb in range(2):
                    tt = 2 * gix + tb
                    S.dma("sp", xg[:, tb, :], xmid_scr.ap()[tt * 128:(tt + 1) * 128, :], [t_xmid[tt]], [t_xg[tb]],
                          key="xg%d" % tb)
                    for half in range(2):
                        S.tt("dve", xg[:, tb, half * 512:(half + 1) * 512], pff[tb * 2 + half][:, :],
                             xg[:, tb, half * 512:(half + 1) * 512], ALU.add, [t_pff[tb * 2 + half], t_xg[tb]],
                             [t_xg[tb]])
                    S.act(wsd[0][:], xg[:, tb, :], AF.Square, [t_xg[tb]], [t_wsd[0], t_ssf], accum=ssf[:, tt:tt + 1])
                    S.ts("dve", ssf[:, tt:tt + 1], ssf[:, tt:tt + 1], 1.0 / D, EPS, ALU.mult, ALU.add, [t_ssf],
                         [t_ssf])
                    S.act(ssf[:, tt:tt + 1], ssf[:, tt:tt + 1], AF.Ln, [t_ssf], [t_ssf])
                    S.act(ssf[:, tt:tt + 1], ssf[:, tt:tt + 1], AF.Exp, [t_ssf], [t_ssf], scale=-0.5)
                    S.stt("dve", xg[:, tb, :], xg[:, tb, :], ssf[:, tt:tt + 1], gfb[:], ALU.mult, ALU.mult,
                          [t_xg[tb], t_ssf, t_gfb], [t_xg[tb]])
                    S.dma("sp", out_d[tt * 128:(tt + 1) * 128, :], xg[:, tb, :], [t_xg[tb]], [], key="out%d" % tb)

            for t in range(NITF + 4):
                if t < NITF:
                    f1(t)
                if 0 <= t - 1 < NITF:
                    f2(t - 1)
                if 0 <= t - 4 < NITF:
                    f3(t - 4)
            S.flush()
    return finish(nc, es, S, out_d, dbg)


def finish(nc, es, S, out_d, dbg):
    def fin(E):
        for k, sm in S.dsem.items():
            if S.dcnt[k] > 0:
                E.wait_ge(sm, S.dcnt[k])
        return None
    o = Op()
    o.eng, o.fn, o.dma, o.needed, o.ev, o.cc, o.deps, o.idx, o.tail = "sp", fin, None, False, None, False, [], S.nidx, None
    S.pending.append(o)
    S.flush()
    return nc, dbg


_CACHE = {}


def kernel(**inputs):
    maps = make_inmaps(inputs)
    if "nc" not in _CACHE:
        _CACHE["nc"] = build()[0]
    nc = _CACHE["nc"]
    res = run_bass_kernel_spmd(nc, maps, core_ids=list(range(NCORES)))
    out = np.concatenate([np.asarray(r["out"], np.float32) for r in res.results], axis=0)
    return out.reshape(1, S_TOT, D)
```
